# Optimizing a Trainium2 kernel written in Bass

```python
import jax, jax.numpy as jnp
from jax import lax
import numpy as np

D_MODEL = 1024
BATCH = 16
SEQ = 4096
DEPTH = 1
DEC_BATCH = 32
DEC_SEQ = 64
PAST_LEN = 4096

CHUNK = 64
D_MIX = D_MODEL
D_A = D_MIX // 2
D_B = D_MIX - D_A
D_IN = 2 * D_A + 3 * D_B
CONV_A_WIDTH = 31
CONV_B_WIDTH = 3
D_FF = -(-8 * D_MODEL // (3 * 256)) * 256
EPS = 1e-6

kernel_name = "hybrid_conformer_shortconv_stream_step"


def rms_norm(x, g):
    x32 = x.astype(jnp.float32)
    y = x32 * lax.rsqrt(jnp.mean(x32 * x32, axis=-1, keepdims=True) + EPS)
    return (y * g.astype(jnp.float32)).astype(x.dtype)


def layer_norm(x, g, b):
    x32 = x.astype(jnp.float32)
    mu = jnp.mean(x32, axis=-1, keepdims=True)
    xc = x32 - mu
    y = xc * lax.rsqrt(jnp.mean(xc * xc, axis=-1, keepdims=True) + EPS)
    return (y * g.astype(jnp.float32) + b.astype(jnp.float32)).astype(x.dtype)


def causal_dwconv(u, hist, w):
    k = w.shape[0]
    up = jnp.concatenate([hist.astype(u.dtype), u], axis=1)
    y = lax.conv_general_dilated(
        up, w[:, None, :].astype(u.dtype), window_strides=(1,), padding="VALID",
        dimension_numbers=("NWC", "WIO", "NWC"), feature_group_count=u.shape[-1])
    return y, up[:, up.shape[1] - (k - 1):, :]


def mixer(h, hist_a, hist_b, w_in, conv_a_w, conv_a_b, ln_g, ln_b, conv_b_w, w_out):
    p = jnp.einsum("btd,de->bte", h, w_in)
    a_val, a_gate, g_b, g_c, v = jnp.split(
        p, [D_A, 2 * D_A, 2 * D_A + D_B, 2 * D_A + 2 * D_B], axis=-1)
    a = a_val * jax.nn.sigmoid(a_gate)
    a, new_a = causal_dwconv(a, hist_a, conv_a_w)
    a = jax.nn.silu(layer_norm(a + conv_a_b, ln_g, ln_b))
    u, new_b = causal_dwconv(g_c * v, hist_b, conv_b_w)
    bo = g_b * u
    out = jnp.einsum("bte,ed->btd", jnp.concatenate([a, bo], axis=-1), w_out)
    return out, new_a, new_b


def swiglu(h, w_gate_up, w_down):
    gu = jnp.einsum("btd,df->btf", h, w_gate_up)
    g, u = jnp.split(gu, 2, axis=-1)
    return jnp.einsum("btf,fd->btd", jax.nn.silu(g) * u, w_down)


def setup_inputs(seed: int = 0) -> dict:
    key = jax.random.key(seed)
    ks = jax.random.split(key, 20)
    n = jax.random.normal
    f32 = jnp.float32
    return {
        "x_prompt": n(ks[0], (BATCH, SEQ, D_MODEL), f32),
        "x_sample": n(ks[1], (DEC_BATCH, DEC_SEQ, D_MODEL), f32),
        "cache_conv_a": 0.5 * n(ks[2], (DEPTH, DEC_BATCH, CONV_A_WIDTH - 1, D_A), f32),
        "cache_conv_b": 0.5 * n(ks[3], (DEPTH, DEC_BATCH, CONV_B_WIDTH - 1, D_B), f32),
        "norm_mix_pre": 1.0 + 0.01 * n(ks[4], (DEPTH, D_MODEL), f32),
        "norm_mix_post": 1.0 + 0.01 * n(ks[5], (DEPTH, D_MODEL), f32),
        "w_in": n(ks[6], (DEPTH, D_MODEL, D_IN), f32) * D_MODEL ** -0.5,
        "conv_a_w": n(ks[7], (DEPTH, CONV_A_WIDTH, D_A), f32) * CONV_A_WIDTH ** -0.5,
        "conv_a_b": 0.01 * n(ks[8], (DEPTH, D_A), f32),
        "conv_a_ln_g": 1.0 + 0.01 * n(ks[9], (DEPTH, D_A), f32),
        "conv_a_ln_b": 0.01 * n(ks[10], (DEPTH, D_A), f32),
        "conv_b_w": n(ks[11], (DEPTH, CONV_B_WIDTH, D_B), f32) * CONV_B_WIDTH ** -0.5,
        "w_out": n(ks[12], (DEPTH, D_MIX, D_MODEL), f32) * D_MIX ** -0.5,
        "norm_ffn_pre": 1.0 + 0.01 * n(ks[13], (DEPTH, D_MODEL), f32),
        "norm_ffn_post": 1.0 + 0.01 * n(ks[14], (DEPTH, D_MODEL), f32),
        "w_gate_up": n(ks[15], (DEPTH, D_MODEL, 2 * D_FF), f32) * D_MODEL ** -0.5,
        "w_down": n(ks[16], (DEPTH, D_FF, D_MODEL), f32) * D_FF ** -0.5,
    }


def reference(x_prompt, x_sample, cache_conv_a, cache_conv_b, norm_mix_pre, norm_mix_post,
              w_in, conv_a_w, conv_a_b, conv_a_ln_g, conv_a_ln_b, conv_b_w, w_out,
              norm_ffn_pre, norm_ffn_post, w_gate_up, w_down):
    def run(x, hists_a, hists_b):
        new_as, new_bs = [], []
        for l in range(DEPTH):
            m, na, nb = mixer(rms_norm(x, norm_mix_pre[l]), hists_a[l], hists_b[l],
                              w_in[l], conv_a_w[l], conv_a_b[l], conv_a_ln_g[l],
                              conv_a_ln_b[l], conv_b_w[l], w_out[l])
            x = x + rms_norm(m, norm_mix_post[l])
            f = swiglu(rms_norm(x, norm_ffn_pre[l]), w_gate_up[l], w_down[l])
            x = x + rms_norm(f, norm_ffn_post[l])
            new_as.append(na)
            new_bs.append(nb)
        return x, jnp.stack(new_as), jnp.stack(new_bs)

    nb_p = x_prompt.shape[0]
    zeros_a = jnp.zeros((DEPTH, nb_p, CONV_A_WIDTH - 1, D_A), x_prompt.dtype)
    zeros_b = jnp.zeros((DEPTH, nb_p, CONV_B_WIDTH - 1, D_B), x_prompt.dtype)
    y_prompt, conv_a_prompt, conv_b_prompt = run(x_prompt, zeros_a, zeros_b)
    y_sample, conv_a_sample, conv_b_sample = run(x_sample, cache_conv_a, cache_conv_b)
    return (y_prompt, y_sample, conv_a_prompt, conv_b_prompt, conv_a_sample, conv_b_sample)
```

```python
import numpy as np
import concourse.bass as bass
import concourse.mybir as mybir
from concourse.bass_utils import run_bass_kernel_spmd

F32 = mybir.dt.float32
BF16 = mybir.dt.bfloat16
AF = mybir.ActivationFunctionType
ALU = mybir.AluOpType

D = 1024
DA = 512
DB = 512
DIN = 2560
DFF = 2816
KA = 31
KB = 3
EPS = 1e-6
NDC = 8
NFC = 22
NPAR = 37
RING = 7
NTMP = 7
N_CORES = 8

E_ORDER = [0, 4, 1, 5, 2, 6, 3, 7, 12, 16, 8, 13, 17, 9, 14, 18, 10, 15, 19, 11]


class _Op:
    __slots__ = ("eng", "fn", "deps", "idx", "signal", "dma_slot", "dma_val", "waits", "sigval")


class Prog:
    ENGS = ("pe", "act", "dve", "pool", "sp")

    def __init__(self):
        self.ops = {e: [] for e in self.ENGS}
        self.lastw = {}
        self.readers = {}
        self.dma_cnt = {}

    def op(self, eng, fn, reads=(), writes=(), dma_slot=None):
        o = _Op()
        o.eng = eng
        o.fn = fn
        o.idx = len(self.ops[eng])
        o.signal = False
        deps = []
        for r in reads:
            w = self.lastw.get(r)
            if w is not None:
                deps.append(w)
        for r in writes:
            w = self.lastw.get(r)
            if w is not None:
                deps.append(w)
            deps.extend(self.readers.get(r, ()))
        o.deps = deps
        if dma_slot is not None:
            c = self.dma_cnt.get(dma_slot, 0) + 1
            self.dma_cnt[dma_slot] = c
            o.dma_slot = dma_slot
            o.dma_val = 16 * c
            tok = ("dma", dma_slot, 16 * c)
        else:
            o.dma_slot = None
            o.dma_val = 0
            tok = ("eng", eng, o.idx)
        for r in reads:
            self.readers.setdefault(r, []).append(tok)
        for r in writes:
            self.lastw[r] = tok
            self.readers[r] = []
        self.ops[eng].append(o)
        return o

    def finalize(self):
        for e in self.ENGS:
            seen = {}
            for o in self.ops[e]:
                need_eng = {}
                need_dma = {}
                for d in o.deps:
                    if d[0] == "eng":
                        _, de, di = d
                        if de == e:
                            if e in ("pe", "sp"):
                                continue
                            if o.idx - di > 2:
                                continue
                        if seen.get(de, -1) >= di:
                            continue
                        if need_eng.get(de, -1) < di:
                            need_eng[de] = di
                    else:
                        _, slot, val = d
                        if seen.get(("dma", slot), 0) >= val:
                            continue
                        if need_dma.get(slot, 0) < val:
                            need_dma[slot] = val
                waits = []
                for de, di in need_eng.items():
                    seen[de] = di
                    waits.append(("eng", de, di))
                    self.ops[de][di].signal = True
                for slot, val in need_dma.items():
                    seen[("dma", slot)] = val
                    waits.append(("dma", slot, val))
                o.waits = waits
        for e in self.ENGS:
            c = 0
            for o in self.ops[e]:
                if o.signal:
                    c += 1
                o.sigval = c

    def emit_engine(self, e, eng, esem, dsem):
        for o in self.ops[e]:
            for w in o.waits:
                if w[0] == "eng":
                    eng.wait_ge(esem[w[1]], self.ops[w[1]][w[2]].sigval)
                else:
                    eng.wait_ge(dsem[w[1]], w[2])
            ins = o.fn(eng)
            if o.dma_slot is not None:
                ins.then_inc(dsem[o.dma_slot], 16)
            elif o.signal:
                ins.then_inc(esem[e], 1)


def build_program(NP, SEQ, NS, LS=64):
    assert SEQ % 512 == 0 and (NS * LS) % 128 == 0
    nc = bass.Bass("TRN2", target_bir_lowering=False)
    P = Prog()

    def din(name, shape, dt=F32):
        return nc.dram_tensor(name, list(shape), dt, kind="ExternalInput").ap()

    def dout(name, shape, dt=F32):
        return nc.dram_tensor(name, list(shape), dt, kind="ExternalOutput").ap()

    x_p = din("x_p", [NP, SEQ, D])
    x_s = din("x_s", [NS, LS, D])
    cache_a = din("cache_a", [NS, KA - 1, DA])
    cache_b = din("cache_b", [NS, KB - 1, DB])
    g_pre1 = din("g_pre1", [1, D])
    g_post1 = din("g_post1", [1, D])
    g_pre2 = din("g_pre2", [1, D])
    g_post2 = din("g_post2", [1, D])
    w_in = din("w_in", [D, DIN])
    w_out = din("w_out", [D, D])
    w_gu = din("w_gu", [D, 2 * DFF])
    w_dn = din("w_dn", [DFF, D])
    conv_a_w = din("conv_a_w", [KA, DA])
    conv_a_b = din("conv_a_b", [1, DA])
    ln_g = din("ln_g", [1, DA])
    ln_b = din("ln_b", [1, DA])
    conv_b_w = din("conv_b_w", [KB, DB])
    ident_in = din("ident", [128, 128])

    y_p = dout("y_p", [NP, SEQ, D])
    y_s = dout("y_s", [NS, LS, D])
    na_p = dout("na_p", [NP, KA - 1, DA])
    nb_p = dout("nb_p", [NP, KB - 1, DB])
    na_s = dout("na_s", [NS, KA - 1, DA])
    nb_s = dout("nb_s", [NS, KB - 1, DB])

    ws_in = nc.dram_tensor("ws_in", [10, 128, 2048], BF16, kind="Internal").ap()
    ws_out = nc.dram_tensor("ws_out", [4, 128, 2048], BF16, kind="Internal").ap()
    ws_gu = nc.dram_tensor("ws_gu", [NFC, 128, 2048], BF16, kind="Internal").ap()
    ws_dn = nc.dram_tensor("ws_dn", [12, 128, 2048], BF16, kind="Internal").ap()

    from contextlib import ExitStack
    es = ExitStack()

    def sb(name, shape, dt=F32):
        return es.enter_context(nc.sbuf_tensor(name, list(shape), dt))

    ident_f = sb("ident_f", [128, 128])
    ident_b = sb("ident_b", [128, 128], BF16)
    ones_f = sb("ones_f", [128, 128])
    neghalf = sb("neghalf", [128, 1])
    gb_t = [sb(f"gb{i}", [128, D]) for i in range(4)]
    pstage = sb("pstage", [NPAR, DA])
    pT = sb("pT", [128, 4, NPAR])
    cstage_a = sb("cstage_a", [KA - 1, NS, DA])
    cstage_b = sb("cstage_b", [KB - 1, NS, DB])
    x_t = [sb(f"x{i}", [128, 4, D]) for i in range(2)]
    h_t = sb("h", [128, 4, D])
    xn_t = [sb(f"xn{i}", [128, D], BF16) for i in range(2)]
    junk = sb("junk", [128, D], BF16)
    xnT = sb("xnT", [128, NDC, 512], BF16)
    hnT = sb("hnT", [128, NDC, 512], BF16)
    a_ext = sb("a_ext", [128, 4, 544])
    cv_ext = sb("cv_ext", [128, 4, 516])
    ac_t = sb("ac", [128, 4, 512])
    tmp_t = [sb(f"tmp{i}", [128, 512]) for i in range(NTMP)]
    mixT = sb("mixT", [128, NDC, 512], BF16)
    actT = sb("actT", [128, NFC, 512], BF16)
    ring = [sb(f"ring{i}", [128, 2048], BF16) for i in range(RING)]
    st = sb("st", [128, 64])
    ostage_a = sb("ostage_a", [KA - 1, DA])
    ostage_b = sb("ostage_b", [KB - 1, DB])
    ps = es.enter_context(nc.psum_tensor("ps", [128, 4096], F32))

    def bank(b, n=512):
        return ps[:, b * 512:b * 512 + n]

    def pair(b, n=1024):
        return ps[:, b * 512:b * 512 + n]

    cnt = {"tmp": 0, "st": 0, "ring": 0, "xn": 0, "tb": 0, "wb": 0, "pb": 0}

    def new_tmp():
        i = cnt["tmp"] % NTMP
        cnt["tmp"] += 1
        return tmp_t[i], ("tmp", i)

    def new_st():
        i = cnt["st"] % 64
        cnt["st"] += 1
        return st[:, i:i + 1], ("st", i)

    def new_xn():
        i = cnt["xn"] % 2
        cnt["xn"] += 1
        return xn_t[i], ("xn", i)

    def new_tbank():
        i = 6 + cnt["tb"] % 2
        cnt["tb"] += 1
        return i

    def new_wbank():
        i = cnt["wb"] % 6
        cnt["wb"] += 1
        return i

    def new_pbank():
        i = 2 * (cnt["pb"] % 3)
        cnt["pb"] += 1
        return i

    def wload(src_ap, src_res, cols=2048):
        i = cnt["ring"] % RING
        cnt["ring"] += 1
        dst = ring[i]
        P.op("sp", lambda e, dst=dst, src_ap=src_ap, cols=cols: e.dma_start(out=dst[:, 0:cols], in_=src_ap[:, 0:cols]),
             reads=src_res, writes=[("ring", i)], dma_slot=("ring", i))
        return dst, ("ring", i)

    res_ws = {"in": [], "out": [], "gu": [], "dn": []}
    w_in_v = w_in.rearrange("(dc p) (j e) -> p j dc e", p=128, e=128)
    ws_in_v = ws_in.rearrange("i p (q dc e) -> i p q dc e", q=2, dc=NDC)
    for k, j in enumerate(E_ORDER):
        i, q = divmod(k, 2)
        r = ("ws", "in", k)
        res_ws["in"].append(r)
        P.op("pool", lambda e, i=i, q=q, j=j: e.dma_start(out=ws_in_v[i, :, q, :, :], in_=w_in_v[:, j, :, :]),
             writes=[r], dma_slot="prep_in")
    w_out_v = w_out.rearrange("(cc p) d -> p cc d", p=128)
    ws_out_v = ws_out.rearrange("i p (cc d) -> i p cc d", cc=4)
    for half in range(2):
        for q in range(2):
            r = ("ws", "out", half * 2 + q)
            res_ws["out"].append(r)
            P.op("pool", lambda e, half=half, q=q: e.dma_start(
                out=ws_out_v[half * 2 + q], in_=w_out_v[:, 4 * q:4 * q + 4, half * 512:(half + 1) * 512]),
                writes=[r], dma_slot="prep_out")
    w_gu_v = w_gu.rearrange("(dc p) (gu j e) -> p gu j dc e", p=128, gu=2, e=128)
    ws_gu_v = ws_gu.rearrange("j p (gu dc e) -> j p gu dc e", gu=2, dc=NDC)
    for j in range(NFC):
        for gu in range(2):
            r = ("ws", "gu", j * 2 + gu)
            res_ws["gu"].append(r)
            P.op("pool", lambda e, j=j, gu=gu: e.dma_start(out=ws_gu_v[j, :, gu, :, :], in_=w_gu_v[:, gu, j, :, :]),
                 writes=[r], dma_slot="prep_gu")
    w_dn_v = w_dn.rearrange("(fc p) d -> p fc d", p=128)
    ws_dn_v = ws_dn.rearrange("i p (fc d) -> i p fc d", fc=4)
    for half in range(2):
        for q in range(6):
            nfc = 4 if q < 5 else 2
            r = ("ws", "dn", half * 6 + q)
            res_ws["dn"].append(r)
            P.op("pool", lambda e, half=half, q=q, nfc=nfc: e.dma_start(
                out=ws_dn_v[half * 6 + q, :, 0:nfc, :], in_=w_dn_v[:, 4 * q:4 * q + nfc, half * 512:(half + 1) * 512]),
                writes=[r], dma_slot="prep_dn")

    P.op("sp", lambda e: e.dma_start(out=ident_f[:], in_=ident_in[:, :]), writes=["ident_f"], dma_slot="c0")
    for i, g in enumerate((g_pre1, g_post1, g_pre2, g_post2)):
        P.op("sp", lambda e, i=i, g=g: e.dma_start(out=gb_t[i][:], in_=g[0, :].partition_broadcast(128)),
             writes=[("gb", i)], dma_slot=("c1", i))
    P.op("sp", lambda e: e.dma_start(out=pstage[0:KA, :], in_=conv_a_w[:, :]), writes=["ps0"], dma_slot="c2")
    P.op("sp", lambda e: e.dma_start(out=pstage[31:32, :], in_=conv_a_b[:, :]), writes=["ps1"], dma_slot="c3")
    P.op("sp", lambda e: e.dma_start(out=pstage[32:33, :], in_=ln_g[:, :]), writes=["ps2"], dma_slot="c4")
    P.op("sp", lambda e: e.dma_start(out=pstage[33:34, :], in_=ln_b[:, :]), writes=["ps3"], dma_slot="c5")
    P.op("sp", lambda e: e.dma_start(out=pstage[34:37, :], in_=conv_b_w[:, :]), writes=["ps4"], dma_slot="c6")
    P.op("sp", lambda e: e.dma_start(out=cstage_a[:], in_=cache_a.rearrange("s t c -> t s c")),
         writes=["cstage_a"], dma_slot="c7")
    P.op("sp", lambda e: e.dma_start(out=cstage_b[:], in_=cache_b.rearrange("s t c -> t s c")),
         writes=["cstage_b"], dma_slot="c8")
    P.op("dve", lambda e: e.tensor_copy(out=ident_b[:], in_=ident_f[:]), reads=["ident_f"], writes=["ident_b"])
    P.op("dve", lambda e: e.memset(ones_f[:], 1.0 / DA), writes=["ones_f"])
    P.op("dve", lambda e: e.memset(neghalf[:], -0.5), writes=["neghalf"])

    tb = new_tbank()

    def _ptr(e):
        ins = None
        for c in range(4):
            ins = e.transpose(out=bank(tb)[:, c * 64:c * 64 + NPAR], in_=pstage[0:NPAR, c * 128:(c + 1) * 128],
                              identity=ident_f[0:NPAR, 0:NPAR])
        return ins
    P.op("pe", _ptr, reads=["ps0", "ps1", "ps2", "ps3", "ps4", "ident_f"], writes=[("ps", tb)])
    P.op("dve", lambda e: e.tensor_copy(out=pT[:], in_=bank(tb).rearrange("p (c k) -> p c k", k=64)[:, 0:4, 0:NPAR]),
         reads=[("ps", tb)], writes=["pT"])

    def rstd_ops(ss, ss_r, scale):
        t, t_r = new_st()
        r, r_r = new_st()
        P.op("pool", lambda e: e.tensor_scalar(out=t, in0=ss, scalar1=scale, scalar2=EPS, op0=ALU.mult, op1=ALU.add),
             reads=[ss_r], writes=[t_r])
        P.op("pool", lambda e: e.tensor_tensor(out=r, in0=t, in1=neghalf[:, 0:1], op=ALU.pow),
             reads=[t_r, "neghalf"], writes=[r_r])
        return r, r_r

    def norm_transpose(src, src_r, gi, dstT, dst_name, tt):
        ss, ss_r = new_st()
        P.op("act", lambda e: e.activation(out=junk[:], in_=src, func=AF.Square, accum_out=ss),
             reads=[src_r], writes=[ss_r])
        r, r_r = rstd_ops(ss, ss_r, 1.0 / D)
        xn, xn_r = new_xn()
        P.op("dve", lambda e: e.scalar_tensor_tensor(out=xn[:], in0=src, scalar=r, in1=gb_t[gi][:],
                                                     op0=ALU.mult, op1=ALU.mult),
             reads=[src_r, r_r, ("gb", gi)], writes=[xn_r])
        b = new_tbank()
        pb = bank(b).bitcast(BF16)

        def _tr(e):
            ins = None
            for c in range(NDC):
                ins = e.transpose(out=pb[:, c * 128:(c + 1) * 128], in_=xn[:, c * 128:(c + 1) * 128], identity=ident_b[:])
            return ins
        P.op("pe", _tr, reads=[xn_r, "ident_b"], writes=[("ps", b)])
        P.op("act", lambda e: e.activation(out=dstT[:, :, tt * 128:(tt + 1) * 128],
                                           in_=pb.rearrange("p (c t) -> p c t", t=128), func=AF.Copy),
             reads=[("ps", b)], writes=[(dst_name, tt)])

    class Blk:
        pass

    def make_blocks():
        blks = []
        for s in range(NP):
            nb = SEQ // 512
            for k in range(nb):
                b = Blk()
                b.kind = "p"
                b.nseg, b.L, b.TT = 1, 512, 4
                b.first, b.last = (k == 0), (k == nb - 1)
                b.seqs = [s]
                b.xrows = [x_p[s, k * 512 + t * 128:k * 512 + (t + 1) * 128, :] for t in range(4)]
                b.yrows = [y_p[s, k * 512 + t * 128:k * 512 + (t + 1) * 128, :] for t in range(4)]
                blks.append(b)
        xs = x_s.rearrange("s t d -> (s t) d")
        ys = y_s.rearrange("s t d -> (s t) d")
        spb = 512 // LS
        for k0 in range(0, NS, spb):
            b = Blk()
            b.kind = "s"
            b.nseg = min(spb, NS - k0)
            b.L = LS
            b.TT = b.nseg * LS // 128
            b.first, b.last = True, True
            b.seqs = list(range(k0, k0 + b.nseg))
            b.xrows = [xs[k0 * LS + t * 128:k0 * LS + (t + 1) * 128, :] for t in range(b.TT)]
            b.yrows = [ys[k0 * LS + t * 128:k0 * LS + (t + 1) * 128, :] for t in range(b.TT)]
            blks.append(b)
        return blks

    blocks = make_blocks()

    def aview(buf, c, b, lo, n, hist):
        w = hist + b.L
        v = buf[:, c, 0:b.nseg * w].rearrange("p (s l) -> p s l", s=b.nseg)
        return v[:, :, lo:lo + n]

    def nview(ap2d, b):
        return ap2d.rearrange("p (s l) -> p s l", s=b.nseg)

    def load_x(b, bi):
        buf = bi % 2
        for tt in range(b.TT):
            P.op("sp", lambda e, tt=tt: e.dma_start(out=x_t[buf][:, tt, :], in_=b.xrows[tt]),
                 writes=[("x", buf, tt)], dma_slot=("x", buf, tt))

    def phase_hist(b):
        if not b.first:
            return
        if b.kind == "p":
            for c in range(4):
                P.op("pool", lambda e, c=c: e.memset(aview(a_ext, c, b, 0, KA - 1, KA - 1), 0.0), writes=[("ahist", c)])
                P.op("pool", lambda e, c=c: e.memset(aview(cv_ext, c, b, 0, KB - 1, KB - 1), 0.0), writes=[("bhist", c)])
        else:
            tb1 = new_tbank()

            def _tra(e):
                ins = None
                for si, s in enumerate(b.seqs):
                    for c in range(4):
                        o = (si * 4 + c) * 32
                        ins = e.transpose(out=bank(tb1)[:, o:o + KA - 1], in_=cstage_a[0:KA - 1, s, c * 128:(c + 1) * 128],
                                          identity=ident_f[0:KA - 1, 0:KA - 1])
                return ins
            P.op("pe", _tra, reads=["cstage_a", "ident_f"], writes=[("ps", tb1)])
            for c in range(4):
                P.op("dve", lambda e, c=c: e.tensor_copy(
                    out=aview(a_ext, c, b, 0, KA - 1, KA - 1),
                    in_=bank(tb1).rearrange("p (s c k) -> p s c k", c=4, k=32)[:, 0:b.nseg, c, 0:KA - 1]),
                    reads=[("ps", tb1)], writes=[("ahist", c)])
            tb2 = new_tbank()

            def _trb(e):
                ins = None
                for si, s in enumerate(b.seqs):
                    for c in range(4):
                        o = (si * 4 + c) * 32
                        ins = e.transpose(out=bank(tb2)[:, o:o + KB - 1], in_=cstage_b[0:KB - 1, s, c * 128:(c + 1) * 128],
                                          identity=ident_f[0:KB - 1, 0:KB - 1])
                return ins
            P.op("pe", _trb, reads=["cstage_b", "ident_f"], writes=[("ps", tb2)])
            for c in range(4):
                P.op("dve", lambda e, c=c: e.tensor_copy(
                    out=aview(cv_ext, c, b, 0, KB - 1, KB - 1),
                    in_=bank(tb2).rearrange("p (s c k) -> p s c k", c=4, k=32)[:, 0:b.nseg, c, 0:KB - 1]),
                    reads=[("ps", tb2)], writes=[("bhist", c)])

    def phase0(b, bi):
        buf = bi % 2
        for tt in range(b.TT):
            norm_transpose(x_t[buf][:, tt, :], ("x", buf, tt), 0, xnT, "xnT", tt)

    def win_mm(b, slot, slot_r, q):
        N = b.TT * 128
        bk = new_wbank()
        wv = slot.rearrange("p (q dc e) -> p q dc e", q=2, dc=NDC)

        def _mm(e):
            ins = None
            for dc in range(NDC):
                ins = e.matmul(out=bank(bk, N), lhsT=wv[:, q, dc, :], rhs=xnT[:, dc, 0:N], start=(dc == 0), stop=(dc == NDC - 1))
            return ins
        P.op("pe", _mm, reads=[slot_r] + [("xnT", t) for t in range(b.TT)], writes=[("ps", bk)])
        return bk

    def phaseA1(b):
        N = b.TT * 128
        pieces = {}

        def get_chunk(k):
            i, q = divmod(k, 2)
            if i not in pieces:
                pieces[i] = wload(ws_in[i], res_ws["in"])
            slot, slot_r = pieces[i]
            return win_mm(b, slot, slot_r, q)

        for c in range(4):
            bv = get_chunk(2 * c)
            bg = get_chunk(2 * c + 1)
            sg, sg_r = new_tmp()
            P.op("act", lambda e, bg=bg, sg=sg: e.activation(out=sg[:, 0:N], in_=bank(bg, N), func=AF.Sigmoid),
                 reads=[("ps", bg)], writes=[sg_r])
            P.op("dve", lambda e, bv=bv, sg=sg, c=c: e.tensor_tensor(
                out=aview(a_ext, c, b, KA - 1, b.L, KA - 1), in0=nview(bank(bv, N), b), in1=nview(sg[:, 0:N], b), op=ALU.mult),
                reads=[("ps", bv), sg_r], writes=[("abody", c)])
            yield
        for c in range(4):
            bgc = get_chunk(8 + 3 * c)
            bvv = get_chunk(8 + 3 * c + 1)
            bgb = get_chunk(8 + 3 * c + 2)
            vs, vs_r = new_tmp()
            P.op("act", lambda e, bvv=bvv, vs=vs: e.activation(out=vs[:, 0:N], in_=bank(bvv, N), func=AF.Copy),
                 reads=[("ps", bvv)], writes=[vs_r])
            P.op("dve", lambda e, bgc=bgc, vs=vs, c=c: e.tensor_tensor(
                out=aview(cv_ext, c, b, KB - 1, b.L, KB - 1), in0=nview(bank(bgc, N), b), in1=nview(vs[:, 0:N], b), op=ALU.mult),
                reads=[("ps", bgc), vs_r], writes=[("bbody", c)])
            u, u_r = new_tmp()
            P.op("dve", lambda e, u=u, c=c: e.tensor_scalar(
                out=nview(u[:, 0:N], b), in0=aview(cv_ext, c, b, 0, b.L, KB - 1), scalar1=pT[:, c, 34:35], scalar2=None, op0=ALU.mult),
                reads=[("bbody", c), ("bhist", c), "pT"], writes=[u_r])
            for k in range(1, KB):
                P.op("dve", lambda e, u=u, c=c, k=k: e.scalar_tensor_tensor(
                    out=nview(u[:, 0:N], b), in0=aview(cv_ext, c, b, k, b.L, KB - 1), scalar=pT[:, c, 34 + k:35 + k],
                    in1=nview(u[:, 0:N], b), op0=ALU.mult, op1=ALU.add),
                    reads=[("bbody", c), ("bhist", c), "pT", u_r], writes=[u_r])
            P.op("dve", lambda e, u=u, bgb=bgb, c=c: e.tensor_tensor(
                out=mixT[:, 4 + c, 0:N], in0=bank(bgb, N), in1=u[:, 0:N], op=ALU.mult),
                reads=[("ps", bgb), u_r], writes=[("mixT", 4 + c)])
            yield

    def phase_conv(b):
        N = b.TT * 128
        for c in range(4):
            acv = nview(ac_t[:, c, 0:N], b)
            P.op("dve", lambda e, c=c, acv=acv: e.tensor_scalar(
                out=acv, in0=aview(a_ext, c, b, 0, b.L, KA - 1), scalar1=pT[:, c, 0:1], scalar2=pT[:, c, 31:32],
                op0=ALU.mult, op1=ALU.add),
                reads=[("abody", c), ("ahist", c), "pT"], writes=[("ac", c)])
            for k in range(1, KA):
                P.op("dve", lambda e, c=c, k=k, acv=acv: e.scalar_tensor_tensor(
                    out=acv, in0=aview(a_ext, c, b, k, b.L, KA - 1), scalar=pT[:, c, k:k + 1], in1=acv,
                    op0=ALU.mult, op1=ALU.add),
                    reads=[("abody", c), ("ahist", c), "pT", ("ac", c)], writes=[("ac", c)])
                if k % 4 == 0:
                    yield
            yield

    def phase_tail(b):
        if b.last:
            na, nb_ = (na_p, nb_p) if b.kind == "p" else (na_s, nb_s)
            for si, s in enumerate(b.seqs):
                tb1 = new_tbank()

                def _t1(e, si=si, tb1=tb1):
                    ins = None
                    for c in range(4):
                        ins = e.transpose(out=bank(tb1)[0:KA - 1, c * 128:(c + 1) * 128],
                                          in_=aview(a_ext, c, b, b.L, KA - 1, KA - 1)[:, si, :], identity=ident_f[:])
                    return ins
                P.op("pe", _t1, reads=[("abody", c) for c in range(4)] + [("ahist", c) for c in range(4)] + ["ident_f"],
                     writes=[("ps", tb1)])
                P.op("act", lambda e, tb1=tb1: e.activation(out=ostage_a[:], in_=bank(tb1)[0:KA - 1, :], func=AF.Copy),
                     reads=[("ps", tb1)], writes=["ostage_a"])
                P.op("sp", lambda e, s=s, na=na: e.dma_start(out=na[s, :, :], in_=ostage_a[:]),
                     reads=["ostage_a"], writes=[("out", "na", b.kind, s)], dma_slot="oa")
                tb2 = new_tbank()

                def _t2(e, si=si, tb2=tb2):
                    ins = None
                    for c in range(4):
                        ins = e.transpose(out=bank(tb2)[0:KB - 1, c * 128:(c + 1) * 128],
                                          in_=aview(cv_ext, c, b, b.L, KB - 1, KB - 1)[:, si, :], identity=ident_f[:])
                    return ins
                P.op("pe", _t2, reads=[("bbody", c) for c in range(4)] + [("bhist", c) for c in range(4)] + ["ident_f"],
                     writes=[("ps", tb2)])
                P.op("act", lambda e, tb2=tb2: e.activation(out=ostage_b[:], in_=bank(tb2)[0:KB - 1, :], func=AF.Copy),
                     reads=[("ps", tb2)], writes=["ostage_b"])
                P.op("sp", lambda e, s=s, nb_=nb_: e.dma_start(out=nb_[s, :, :], in_=ostage_b[:]),
                     reads=["ostage_b"], writes=[("out", "nb", b.kind, s)], dma_slot="ob")
        else:
            for c in range(4):
                P.op("pool", lambda e, c=c: e.tensor_copy(out=a_ext[:, c, 0:KA - 1], in_=a_ext[:, c, b.L:b.L + KA - 1]),
                     reads=[("abody", c), ("ahist", c)], writes=[("ahist", c)])
                P.op("pool", lambda e, c=c: e.tensor_copy(out=cv_ext[:, c, 0:KB - 1], in_=cv_ext[:, c, b.L:b.L + KB - 1]),
                     reads=[("bbody", c), ("bhist", c)], writes=[("bhist", c)])

    def phaseA2(b):
        N = b.TT * 128
        bm, be = 6, 7
        sqs = []
        for c in range(4):
            sq, sq_r = new_tmp()
            P.op("act", lambda e, c=c, sq=sq: e.activation(out=sq[:, 0:N], in_=ac_t[:, c, 0:N], func=AF.Square),
                 reads=[("ac", c)], writes=[sq_r])
            sqs.append((sq, sq_r))

        def _m1(e):
            ins = None
            for c in range(4):
                ins = e.matmul(out=bank(bm, N), lhsT=ones_f[:], rhs=ac_t[:, c, 0:N], start=(c == 0), stop=(c == 3))
            return ins
        P.op("pe", _m1, reads=[("ac", c) for c in range(4)] + ["ones_f"], writes=[("ps", bm)])

        def _m2(e):
            ins = None
            for c in range(4):
                ins = e.matmul(out=bank(be, N), lhsT=ones_f[:], rhs=sqs[c][0][:, 0:N], start=(c == 0), stop=(c == 3))
            return ins
        P.op("pe", _m2, reads=[s[1] for s in sqs] + ["ones_f"], writes=[("ps", be)])
        msq, msq_r = new_tmp()
        P.op("act", lambda e: e.activation(out=msq[:, 0:N], in_=bank(bm, N), func=AF.Square), reads=[("ps", bm)], writes=[msq_r])
        var, var_r = new_tmp()
        P.op("dve", lambda e: e.tensor_tensor(out=var[:, 0:N], in0=bank(be, N), in1=msq[:, 0:N], op=ALU.subtract),
             reads=[("ps", be), msq_r], writes=[var_r])
        P.op("pool", lambda e: e.tensor_scalar(out=var[:, 0:N], in0=var[:, 0:N], scalar1=EPS, scalar2=None, op0=ALU.add),
             reads=[var_r], writes=[var_r])
        rs, rs_r = new_tmp()
        P.op("pool", lambda e: e.tensor_tensor(out=rs[:, 0:N], in0=var[:, 0:N], in1=neghalf[:, 0:1].broadcast_to([128, N]), op=ALU.pow),
             reads=[var_r, "neghalf"], writes=[rs_r])
        for c in range(4):
            z, z_r = new_tmp()
            P.op("dve", lambda e, c=c, z=z: e.tensor_tensor(out=z[:, 0:N], in0=ac_t[:, c, 0:N], in1=bank(bm, N), op=ALU.subtract),
                 reads=[("ac", c), ("ps", bm)], writes=[z_r])
            P.op("dve", lambda e, z=z: e.tensor_tensor(out=z[:, 0:N], in0=z[:, 0:N], in1=rs[:, 0:N], op=ALU.mult),
                 reads=[z_r, rs_r], writes=[z_r])
            P.op("act", lambda e, c=c, z=z: e.activation(out=mixT[:, c, 0:N], in_=z[:, 0:N], func=AF.Silu,
                                                        scale=pT[:, c, 32:33], bias=pT[:, c, 33:34]),
                 reads=[z_r, "pT"], writes=[("mixT", c)])
            yield

    def phaseB(b, bi):
        buf = bi % 2
        slots = [wload(ws_out[i], res_ws["out"]) for i in range(4)]
        for tt in range(b.TT):
            pb = new_pbank()

            def _mm(e, tt=tt, pb=pb):
                ins = None
                for half in range(2):
                    for cc in range(NDC):
                        sl = slots[half * 2 + cc // 4][0].rearrange("p (cc d) -> p cc d", cc=4)
                        ins = e.matmul(out=bank(pb + half), lhsT=mixT[:, cc, tt * 128:(tt + 1) * 128], rhs=sl[:, cc % 4, :],
                                       start=(cc == 0), stop=(cc == NDC - 1))
                return ins
            P.op("pe", _mm, reads=[s[1] for s in slots] + [("mixT", cc) for cc in range(NDC)],
                 writes=[("ps", pb), ("ps", pb + 1)])
            ss, ss_r = new_st()
            P.op("act", lambda e, pb=pb, ss=ss: e.activation(out=junk[:], in_=pair(pb), func=AF.Square, accum_out=ss),
                 reads=[("ps", pb), ("ps", pb + 1)], writes=[ss_r])
            r, r_r = rstd_ops(ss, ss_r, 1.0 / D)
            P.op("dve", lambda e, tt=tt, pb=pb, r=r: e.scalar_tensor_tensor(
                out=h_t[:, tt, :], in0=pair(pb), scalar=r, in1=gb_t[1][:], op0=ALU.mult, op1=ALU.mult),
                reads=[("ps", pb), ("ps", pb + 1), r_r, ("gb", 1)], writes=[("h", tt)])
            P.op("dve", lambda e, tt=tt: e.tensor_tensor(out=h_t[:, tt, :], in0=h_t[:, tt, :], in1=x_t[buf][:, tt, :], op=ALU.add),
                 reads=[("h", tt), ("x", buf, tt)], writes=[("h", tt)])
            norm_transpose(h_t[:, tt, :], ("h", tt), 2, hnT, "hnT", tt)
            yield

    def phaseD(b):
        N = b.TT * 128
        for j in range(NFC):
            slot, slot_r = wload(ws_gu[j], res_ws["gu"])
            wv = slot.rearrange("p (gu dc e) -> p gu dc e", gu=2, dc=NDC)
            pb = new_pbank()

            def _mm(e, wv=wv, pb=pb):
                ins = None
                for gu in range(2):
                    for dc in range(NDC):
                        ins = e.matmul(out=bank(pb + gu, N), lhsT=wv[:, gu, dc, :], rhs=hnT[:, dc, 0:N],
                                       start=(dc == 0), stop=(dc == NDC - 1))
                return ins
            P.op("pe", _mm, reads=[slot_r] + [("hnT", t) for t in range(b.TT)], writes=[("ps", pb), ("ps", pb + 1)])
            sg, sg_r = new_tmp()
            P.op("act", lambda e, pb=pb, sg=sg: e.activation(out=sg[:, 0:N], in_=bank(pb, N), func=AF.Silu),
                 reads=[("ps", pb)], writes=[sg_r])
            P.op("dve", lambda e, pb=pb, sg=sg, j=j: e.tensor_tensor(out=actT[:, j, 0:N], in0=bank(pb + 1, N), in1=sg[:, 0:N], op=ALU.mult),
                 reads=[("ps", pb + 1), sg_r], writes=[("actT", j)])
            yield

    def phaseE(b):
        for half in range(2):
            for q in range(6):
                nfc = 4 if q < 5 else 2
                slot, slot_r = wload(ws_dn[half * 6 + q], res_ws["dn"], cols=nfc * 512)
                sl = slot.rearrange("p (fc d) -> p fc d", fc=4)

                def _mm(e, half=half, q=q, nfc=nfc, sl=sl):
                    ins = None
                    for f in range(nfc):
                        fc = 4 * q + f
                        for tt in range(b.TT):
                            ins = e.matmul(out=bank(2 * tt + half), lhsT=actT[:, fc, tt * 128:(tt + 1) * 128], rhs=sl[:, f, :],
                                           start=(fc == 0), stop=(fc == NFC - 1))
                    return ins
                P.op("pe", _mm, reads=[slot_r] + [("actT", 4 * q + f) for f in range(nfc)],
                     writes=[("ps", 2 * tt + half) for tt in range(b.TT)])
                yield
        for tt in range(b.TT):
            pb = 2 * tt
            ss, ss_r = new_st()
            P.op("act", lambda e, pb=pb, ss=ss: e.activation(out=junk[:], in_=pair(pb), func=AF.Square, accum_out=ss),
                 reads=[("ps", pb), ("ps", pb + 1)], writes=[ss_r])
            r, r_r = rstd_ops(ss, ss_r, 1.0 / D)
            for half in range(2):
                t, t_r = new_tmp()
                P.op("dve", lambda e, pb=pb, half=half, t=t, r=r: e.scalar_tensor_tensor(
                    out=t[:], in0=bank(pb + half), scalar=r, in1=gb_t[3][:, half * 512:(half + 1) * 512], op0=ALU.mult, op1=ALU.mult),
                    reads=[("ps", pb + half), r_r, ("gb", 3)], writes=[t_r])
                P.op("dve", lambda e, tt=tt, half=half, t=t: e.tensor_tensor(
                    out=h_t[:, tt, half * 512:(half + 1) * 512], in0=h_t[:, tt, half * 512:(half + 1) * 512], in1=t[:], op=ALU.add),
                    reads=[("h", tt), t_r], writes=[("h", tt)])
            P.op("sp", lambda e, tt=tt: e.dma_start(out=b.yrows[tt], in_=h_t[:, tt, :]),
                 reads=[("h", tt)], writes=[("out", "y", id(b), tt)], dma_slot=("y", tt))
            yield

    def run(gen):
        if gen is not None:
            for _ in gen:
                pass

    load_x(blocks[0], 0)
    for bi, b in enumerate(blocks):
        if bi + 1 < len(blocks):
            load_x(blocks[bi + 1], bi + 1)
        phase_hist(b)
        phase0(b, bi)
        run(phaseA1(b))
        run(phase_conv(b))
        phase_tail(b)
        run(phaseA2(b))
        run(phaseB(b, bi))
        run(phaseD(b))
        run(phaseE(b))

    out_res = [r for r in P.lastw if isinstance(r, tuple) and r and r[0] == "out"]
    P.op("sp", lambda e: e.nop(), reads=out_res)

    P.finalize()
    dma_slots = list(P.dma_cnt.keys())
    esem = {e: es.enter_context(nc.semaphore(f"sem_{e}")) for e in Prog.ENGS}
    dsem = {s: es.enter_context(nc.semaphore(f"dsem_{i}")) for i, s in enumerate(dma_slots)}
    with nc.Block() as block:
        @block.tensor
        def _(e):
            P.emit_engine("pe", e, esem, dsem)

        @block.scalar
        def _(e):
            P.emit_engine("act", e, esem, dsem)

        @block.vector
        def _(e):
            P.emit_engine("dve", e, esem, dsem)

        @block.gpsimd
        def _(e):
            P.emit_engine("pool", e, esem, dsem)

        @block.sync
        def _(e):
            P.emit_engine("sp", e, esem, dsem)
    es.close()
    return nc


def make_in_maps(n_cores, inputs, NP, NS):
    f = lambda a: np.ascontiguousarray(np.asarray(a, dtype=np.float32))
    ident = np.eye(128, dtype=np.float32)
    maps = []
    for c in range(n_cores):
        m = {
            "x_p": f(inputs["x_prompt"][c * NP:(c + 1) * NP]),
            "x_s": f(inputs["x_sample"][c * NS:(c + 1) * NS]),
            "cache_a": f(inputs["cache_conv_a"][0, c * NS:(c + 1) * NS]),
            "cache_b": f(inputs["cache_conv_b"][0, c * NS:(c + 1) * NS]),
            "g_pre1": f(inputs["norm_mix_pre"]),
            "g_post1": f(inputs["norm_mix_post"]),
            "g_pre2": f(inputs["norm_ffn_pre"]),
            "g_post2": f(inputs["norm_ffn_post"]),
            "w_in": f(inputs["w_in"][0]),
            "w_out": f(inputs["w_out"][0]),
            "w_gu": f(inputs["w_gate_up"][0]),
            "w_dn": f(inputs["w_down"][0]),
            "conv_a_w": f(inputs["conv_a_w"][0]),
            "conv_a_b": f(inputs["conv_a_b"]),
            "ln_g": f(inputs["conv_a_ln_g"]),
            "ln_b": f(inputs["conv_a_ln_b"]),
            "conv_b_w": f(inputs["conv_b_w"][0]),
            "ident": ident,
        }
        maps.append(m)
    return maps


def gather(results, n_cores):
    cat = lambda k: np.concatenate([np.asarray(results[c][k], dtype=np.float32) for c in range(n_cores)], axis=0)
    return (cat("y_p"), cat("y_s"), cat("na_p")[None], cat("nb_p")[None], cat("na_s")[None], cat("nb_s")[None])


def kernel(**inputs):
    B, SEQ = inputs["x_prompt"].shape[0], inputs["x_prompt"].shape[1]
    BS, LS = inputs["x_sample"].shape[0], inputs["x_sample"].shape[1]
    NP, NS = B // N_CORES, BS // N_CORES
    nc = build_program(NP, SEQ, NS, LS)
    in_maps = make_in_maps(N_CORES, inputs, NP, NS)
    res = run_bass_kernel_spmd(nc, in_maps, core_ids=list(range(N_CORES)))
    return gather(res.results, N_CORES)
```

```python
import numpy as np
import concourse.bass as bass
import concourse.mybir as mybir
from concourse.bass_utils import run_bass_kernel_spmd

F32 = mybir.dt.float32
BF16 = mybir.dt.bfloat16
AF = mybir.ActivationFunctionType
ALU = mybir.AluOpType

D = 1024
DA = 512
DB = 512
DIN = 2560
DFF = 2816
KA = 31
KB = 3
EPS = 1e-6
NDC = 8
NFC = 22
NPAR = 37
RING = 8
NTMP = 9
N_CORES = 8

E_ORDER = [0, 4, 1, 5, 2, 6, 3, 7, 12, 16, 8, 13, 17, 9, 14, 18, 10, 15, 19, 11]


class _Op:
    __slots__ = ("eng", "fn", "deps", "idx", "signal", "dma_slot", "dma_val", "waits", "sigval")


class Prog:
    ENGS = ("pe", "act", "dve", "pool", "sp")

    def __init__(self):
        self.ops = {e: [] for e in self.ENGS}
        self.lastw = {}
        self.readers = {}
        self.dma_cnt = {}

    def op(self, eng, fn, reads=(), writes=(), dma_slot=None):
        o = _Op()
        o.eng = eng
        o.fn = fn
        o.idx = len(self.ops[eng])
        o.signal = False
        deps = []
        for r in reads:
            w = self.lastw.get(r)
            if w is not None:
                deps.append(w)
        for r in writes:
            w = self.lastw.get(r)
            if w is not None:
                deps.append(w)
            deps.extend(self.readers.get(r, ()))
        o.deps = deps
        if dma_slot is not None:
            c = self.dma_cnt.get(dma_slot, 0) + 1
            self.dma_cnt[dma_slot] = c
            o.dma_slot = dma_slot
            o.dma_val = 16 * c
            tok = ("dma", dma_slot, 16 * c)
        else:
            o.dma_slot = None
            o.dma_val = 0
            tok = ("eng", eng, o.idx)
        for r in reads:
            self.readers.setdefault(r, []).append(tok)
        for r in writes:
            self.lastw[r] = tok
            self.readers[r] = []
        self.ops[eng].append(o)
        return o

    def finalize(self):
        for e in self.ENGS:
            seen = {}
            for o in self.ops[e]:
                need_eng = {}
                need_dma = {}
                for d in o.deps:
                    if d[0] == "eng":
                        _, de, di = d
                        if de == e:
                            if e in ("pe", "sp"):
                                continue
                        if seen.get(de, -1) >= di:
                            continue
                        if need_eng.get(de, -1) < di:
                            need_eng[de] = di
                    else:
                        _, slot, val = d
                        if seen.get(("dma", slot), 0) >= val:
                            continue
                        if need_dma.get(slot, 0) < val:
                            need_dma[slot] = val
                waits = []
                for de, di in need_eng.items():
                    seen[de] = di
                    waits.append(("eng", de, di))
                    self.ops[de][di].signal = True
                for slot, val in need_dma.items():
                    seen[("dma", slot)] = val
                    waits.append(("dma", slot, val))
                o.waits = waits
        for e in self.ENGS:
            c = 0
            for o in self.ops[e]:
                if o.signal:
                    c += 1
                o.sigval = c

    def emit_engine(self, e, eng, esem, dsem):
        for o in self.ops[e]:
            for w in o.waits:
                if w[0] == "eng":
                    eng.wait_ge(esem[w[1]], self.ops[w[1]][w[2]].sigval)
                else:
                    eng.wait_ge(dsem[w[1]], w[2])
            ins = o.fn(eng)
            if o.dma_slot is not None:
                ins.then_inc(dsem[o.dma_slot], 16)
            elif o.signal:
                ins.then_inc(esem[e], 1)


class _Psum:
    def __init__(self):
        self.held = [False] * 8
        self.stamp = [0] * 8
        self.clock = 0

    def alloc1(self):
        free = [b for b in range(8) if not self.held[b]]
        assert free, "out of PSUM banks"
        b = min(free, key=lambda k: self.stamp[k])
        self.held[b] = True
        return b

    def alloc_pair(self):
        free = [b for b in range(0, 8, 2) if not self.held[b] and not self.held[b + 1]]
        assert free, "out of PSUM bank pairs"
        b = min(free, key=lambda k: max(self.stamp[k], self.stamp[k + 1]))
        self.held[b] = self.held[b + 1] = True
        return b

    def release(self, *banks):
        for b in banks:
            assert self.held[b]
            self.held[b] = False
            self.clock += 1
            self.stamp[b] = self.clock


def build_program(NP, SEQ, NS, LS=64):
    assert SEQ % 512 == 0 and (NS * LS) % 128 == 0 and NS * (KA - 1) <= 128
    nc = bass.Bass("TRN2", target_bir_lowering=False)
    P = Prog()
    PS = _Psum()

    def din(name, shape, dt=F32):
        return nc.dram_tensor(name, list(shape), dt, kind="ExternalInput").ap()

    def dout(name, shape, dt=F32):
        return nc.dram_tensor(name, list(shape), dt, kind="ExternalOutput").ap()

    x_p = din("x_p", [NP, SEQ, D])
    x_s = din("x_s", [NS, LS, D])
    cache_a = din("cache_a", [NS, KA - 1, DA])
    cache_b = din("cache_b", [NS, KB - 1, DB])
    g_pre1 = din("g_pre1", [1, D])
    g_post1 = din("g_post1", [1, D])
    g_pre2 = din("g_pre2", [1, D])
    g_post2 = din("g_post2", [1, D])
    w_in = din("w_in", [D, DIN])
    w_out = din("w_out", [D, D])
    w_gu = din("w_gu", [D, 2 * DFF])
    w_dn = din("w_dn", [DFF, D])
    conv_a_w = din("conv_a_w", [KA, DA])
    conv_a_b = din("conv_a_b", [1, DA])
    ln_g = din("ln_g", [1, DA])
    ln_b = din("ln_b", [1, DA])
    conv_b_w = din("conv_b_w", [KB, DB])
    ident_in = din("ident", [128, 128])

    y_p = dout("y_p", [NP, SEQ, D])
    y_s = dout("y_s", [NS, LS, D])
    na_p = dout("na_p", [NP, KA - 1, DA])
    nb_p = dout("nb_p", [NP, KB - 1, DB])
    na_s = dout("na_s", [NS, KA - 1, DA])
    nb_s = dout("nb_s", [NS, KB - 1, DB])

    ws_in = nc.dram_tensor("ws_in", [10, 128, 2048], BF16, kind="Internal").ap()
    ws_out = nc.dram_tensor("ws_out", [4, 128, 2048], BF16, kind="Internal").ap()
    ws_gu = nc.dram_tensor("ws_gu", [NFC, 128, 2048], BF16, kind="Internal").ap()
    ws_dn = nc.dram_tensor("ws_dn", [12, 128, 2048], BF16, kind="Internal").ap()

    from contextlib import ExitStack
    es = ExitStack()

    def sb(name, shape, dt=F32):
        return es.enter_context(nc.sbuf_tensor(name, list(shape), dt))

    ident_f = sb("ident_f", [128, 128])
    ident_b = sb("ident_b", [128, 128], BF16)
    ones_f = sb("ones_f", [128, 128])
    neghalf = sb("neghalf", [128, 1])
    eps_t = sb("eps_t", [128, 1])
    gb_t = [sb(f"gb{i}", [128, D]) for i in range(4)]
    pT = sb("pT", [128, 4, NPAR])
    x_t = [sb(f"x{i}", [128, 4, D]) for i in range(2)]
    h_t = sb("h", [128, 4, D])
    xn_t = [sb(f"xn{i}", [128, D], BF16) for i in range(4)]
    hn_t = [sb(f"hn{i}", [128, D], BF16) for i in range(4)]
    xnT = sb("xnT", [128, NDC, 512], BF16)
    hnT = sb("hnT", [128, NDC, 512], BF16)
    a_ext = sb("a_ext", [128, 4, 544])
    cv_ext = sb("cv_ext", [128, 4, 516])
    ac_t = sb("ac", [128, 4, 512])
    tmp_t = [sb(f"tmp{i}", [128, 512]) for i in range(NTMP)]
    mixT_lo = sb("mixT_lo", [128, 4, 512], BF16)
    mixT_hi = [sb(f"mixT_hi{i}", [128, 4, 512], BF16) for i in range(2)]

    def mix(b, cc):
        return mixT_lo[:, cc] if cc < 4 else mixT_hi[b.buf][:, cc - 4]

    def mix_r(b, cc):
        return ("mixT", cc) if cc < 4 else ("mixT", cc, b.buf)
    actT = sb("actT", [128, NFC, 512], BF16)
    ring = [sb(f"ring{i}", [128, 2048], BF16) for i in range(RING)]
    st = sb("st", [128, 64])
    st4 = sb("st4", [128, 64])
    ps = es.enter_context(nc.psum_tensor("ps", [128, 4096], F32))
    print("SBUF bytes remaining per partition:", nc.sbuf_bytes_remaining)

    def bank(b, n=512):
        return ps[:, b * 512:b * 512 + n]

    cnt = {"tmp": 0, "st": 0, "ring": 0, "st4": 0}

    tmp_pinned = set()

    def new_tmp(pin=False):
        while True:
            i = cnt["tmp"] % NTMP
            cnt["tmp"] += 1
            if i not in tmp_pinned:
                break
        if pin:
            tmp_pinned.add(i)
        return tmp_t[i], ("tmp", i)

    def new_st():
        i = cnt["st"] % 64
        cnt["st"] += 1
        return st[:, i:i + 1], ("st", i)

    def new_st4():
        i = cnt["st4"] % 16
        cnt["st4"] += 1
        return st4[:, 4 * i:4 * i + 4], ("st4", i)

    ring_pinned = set()

    def wload(src_ap, src_res, cols=2048, pin=False):
        while True:
            i = cnt["ring"] % RING
            cnt["ring"] += 1
            if i not in ring_pinned:
                break
        if pin:
            ring_pinned.add(i)
        dst = ring[i]
        P.op("sp", lambda e, dst=dst, src_ap=src_ap, cols=cols: e.dma_start(out=dst[:, 0:cols], in_=src_ap[:, 0:cols]),
             reads=src_res, writes=[("ring", i)], dma_slot=("ring", i))
        return dst, ("ring", i)

    res_ws = {"in": [], "out": [], "gu": [], "dn": []}
    w_in_v = w_in.rearrange("(dc p) (j e) -> p j dc e", p=128, e=128)
    ws_in_v = ws_in.rearrange("i p (q dc e) -> i p q dc e", q=2, dc=NDC)
    w_out_v = w_out.rearrange("(cc p) d -> p cc d", p=128)
    ws_out_v = ws_out.rearrange("i p (cc d) -> i p cc d", cc=4)
    w_gu_v = w_gu.rearrange("(dc p) (gu j e) -> p gu j dc e", p=128, gu=2, e=128)
    ws_gu_v = ws_gu.rearrange("j p (gu dc e) -> j p gu dc e", gu=2, dc=NDC)
    w_dn_v = w_dn.rearrange("(fc p) d -> p fc d", p=128)
    ws_dn_v = ws_dn.rearrange("i p (fc d) -> i p fc d", fc=4)

    def prep_in_out():
        for k, j in enumerate(E_ORDER):
            i, q = divmod(k, 2)
            r = ("ws", "in", k)
            res_ws["in"].append(r)
            P.op("pool", lambda e, i=i, q=q, j=j: e.dma_start(out=ws_in_v[i, :, q, :, :], in_=w_in_v[:, j, :, :]),
                 writes=[r], dma_slot="prep_in")
        for half in range(2):
            for q in range(2):
                r = ("ws", "out", half * 2 + q)
                res_ws["out"].append(r)
                P.op("pool", lambda e, half=half, q=q: e.dma_start(
                    out=ws_out_v[half * 2 + q], in_=w_out_v[:, 4 * q:4 * q + 4, half * 512:(half + 1) * 512]),
                    writes=[r], dma_slot="prep_out")

    def prep_gu_dn():
        for j in range(NFC):
            for gu in range(2):
                r = ("ws", "gu", j * 2 + gu)
                res_ws["gu"].append(r)
                P.op("pool", lambda e, j=j, gu=gu: e.dma_start(out=ws_gu_v[j, :, gu, :, :], in_=w_gu_v[:, gu, j, :, :]),
                     writes=[r], dma_slot="prep_gu")
        for half in range(2):
            for q in range(6):
                nfc = 4 if q < 5 else 2
                r = ("ws", "dn", half * 6 + q)
                res_ws["dn"].append(r)
                P.op("pool", lambda e, half=half, q=q, nfc=nfc: e.dma_start(
                    out=ws_dn_v[half * 6 + q, :, 0:nfc, :], in_=w_dn_v[:, 4 * q:4 * q + nfc, half * 512:(half + 1) * 512]),
                    writes=[r], dma_slot="prep_dn")

    prep_in_out()

    P.op("sp", lambda e: e.dma_start(out=ident_f[:], in_=ident_in[:, :]), writes=["ident_f"], dma_slot="c0")
    for i, g in enumerate((g_pre1, g_post1, g_pre2, g_post2)):
        P.op("sp", lambda e, i=i, g=g: e.dma_start(out=gb_t[i][:], in_=g[0, :].partition_broadcast(128)),
             writes=[("gb", i)], dma_slot=("c1", i))
    pstage, pstage_r = new_tmp()
    P.op("sp", lambda e: e.dma_start(out=pstage[0:KA, :], in_=conv_a_w[:, :]), writes=[pstage_r], dma_slot="c2")
    P.op("sp", lambda e: e.dma_start(out=pstage[31:32, :], in_=conv_a_b[:, :]), writes=["ps1"], dma_slot="c3")
    P.op("sp", lambda e: e.dma_start(out=pstage[32:33, :], in_=ln_g[:, :]), writes=["ps2"], dma_slot="c4")
    P.op("sp", lambda e: e.dma_start(out=pstage[33:34, :], in_=ln_b[:, :]), writes=["ps3"], dma_slot="c5")
    P.op("sp", lambda e: e.dma_start(out=pstage[34:37, :], in_=conv_b_w[:, :]), writes=["ps4"], dma_slot="c6")
    P.op("dve", lambda e: e.tensor_copy(out=ident_b[:], in_=ident_f[:]), reads=["ident_f"], writes=["ident_b"])
    P.op("dve", lambda e: e.memset(ones_f[:], 1.0 / DA), writes=["ones_f"])
    P.op("dve", lambda e: e.memset(neghalf[:], -0.5), writes=["neghalf"])
    P.op("dve", lambda e: e.memset(eps_t[:], EPS), writes=["eps_t"])

    tb = PS.alloc1()

    def _ptr(e):
        ins = None
        for c in range(4):
            ins = e.transpose(out=bank(tb)[:, c * 64:c * 64 + NPAR], in_=pstage[0:NPAR, c * 128:(c + 1) * 128],
                              identity=ident_f[0:NPAR, 0:NPAR])
        return ins
    P.op("pe", _ptr, reads=[pstage_r, "ps1", "ps2", "ps3", "ps4", "ident_f"], writes=[("ps", tb)])
    P.op("dve", lambda e: e.tensor_copy(out=pT[:], in_=bank(tb).rearrange("p (c k) -> p c k", k=64)[:, 0:4, 0:NPAR]),
         reads=[("ps", tb)], writes=["pT"])
    PS.release(tb)

    def rstd_ops(ss, ss_r, scale):
        t, t_r = new_st()
        r, r_r = new_st()
        P.op("pool", lambda e: e.tensor_scalar(out=t, in0=ss, scalar1=scale, scalar2=EPS, op0=ALU.mult, op1=ALU.add),
             reads=[ss_r], writes=[t_r])
        P.op("pool", lambda e: e.tensor_tensor(out=r, in0=t, in1=neghalf[:, 0:1], op=ALU.pow),
             reads=[t_r, "neghalf"], writes=[r_r])
        return r, r_r

    def rstd4_ops(ss4, ss4_r, n, scale, add=None):
        t, t_r = new_st4()
        r, r_r = new_st4()
        src, src_rs = ss4, [ss4_r]
        if add is not None:
            a4, a4_r = add
            u, u_r = new_st4()
            P.op("pool", lambda e: e.tensor_tensor(out=u[:, 0:n], in0=ss4[:, 0:n], in1=a4[:, 0:n], op=ALU.add),
                 reads=[ss4_r, a4_r], writes=[u_r])
            src, src_rs = u, [u_r]
        P.op("pool", lambda e: e.tensor_scalar(out=t[:, 0:n], in0=src[:, 0:n], scalar1=scale, scalar2=EPS, op0=ALU.mult, op1=ALU.add),
             reads=src_rs, writes=[t_r])
        P.op("pool", lambda e: e.tensor_tensor(out=r[:, 0:n], in0=t[:, 0:n], in1=neghalf[:, 0:1].broadcast_to([128, n]), op=ALU.pow),
             reads=[t_r, "neghalf"], writes=[r_r])
        return r, r_r

    def norm_elem(src, src_r, gi, xn, xn_r):
        ss, ss_r = new_st()
        P.op("act", lambda e: e.activation(out=xn[:], in_=src, func=AF.Square, accum_out=ss),
             reads=[src_r], writes=[ss_r, xn_r])
        r, r_r = rstd_ops(ss, ss_r, 1.0 / D)
        P.op("dve", lambda e: e.scalar_tensor_tensor(out=xn[:], in0=src, scalar=r, in1=gb_t[gi][:],
                                                     op0=ALU.mult, op1=ALU.mult),
             reads=[src_r, r_r, ("gb", gi)], writes=[xn_r])

    def norm_tr(xn, xn_r, dstT, dst_name, tt):
        b = PS.alloc1()
        pb = bank(b).bitcast(BF16)

        def _tr(e):
            ins = None
            for c in range(NDC):
                ins = e.transpose(out=pb[:, c * 128:(c + 1) * 128], in_=xn[:, c * 128:(c + 1) * 128], identity=ident_b[:])
            return ins
        P.op("pe", _tr, reads=[xn_r, "ident_b"], writes=[("ps", b)])
        P.op("act", lambda e: e.activation(out=dstT[:, :, tt * 128:(tt + 1) * 128],
                                           in_=pb.rearrange("p (c t) -> p c t", t=128), func=AF.Copy),
             reads=[("ps", b)], writes=[(dst_name, tt)])
        PS.release(b)

    class Blk:
        pass

    def make_blocks():
        blks = []
        for s in range(NP):
            nb = SEQ // 512
            for k in range(nb):
                b = Blk()
                b.kind = "p"
                b.nseg, b.L, b.TT = 1, 512, 4
                b.first, b.last = (k == 0), (k == nb - 1)
                b.seqs = [s]
                b.xrows = [x_p[s, k * 512 + t * 128:k * 512 + (t + 1) * 128, :] for t in range(4)]
                b.yrows = [y_p[s, k * 512 + t * 128:k * 512 + (t + 1) * 128, :] for t in range(4)]
                blks.append(b)
        xs = x_s.rearrange("s t d -> (s t) d")
        ys = y_s.rearrange("s t d -> (s t) d")
        spb = 512 // LS
        for k0 in range(0, NS, spb):
            b = Blk()
            b.kind = "s"
            b.nseg = min(spb, NS - k0)
            b.L = LS
            b.TT = b.nseg * LS // 128
            b.first, b.last = True, True
            b.seqs = list(range(k0, k0 + b.nseg))
            b.xrows = [xs[k0 * LS + t * 128:k0 * LS + (t + 1) * 128, :] for t in range(b.TT)]
            b.yrows = [ys[k0 * LS + t * 128:k0 * LS + (t + 1) * 128, :] for t in range(b.TT)]
            blks.append(b)
        for i, b in enumerate(blks):
            b.bi = i
            b.buf = i % 2
        return blks

    blocks = make_blocks()
    import os as _os
    if _os.environ.get('KDBG_BLK'):
        blocks = [blocks[int(t)] for t in _os.environ['KDBG_BLK'].split(',')]
        for i_, b_ in enumerate(blocks):
            b_.bi = i_
            b_.buf = i_ % 2

    def aview(buf, c, b, lo, n, hist):
        w = hist + b.L
        v = buf[:, c, 0:b.nseg * w].rearrange("p (s l) -> p s l", s=b.nseg)
        return v[:, :, lo:lo + n]

    def nview(ap2d, b):
        return ap2d.rearrange("p (s l) -> p s l", s=b.nseg)

    def load_x(b):
        for tt in range(b.TT):
            P.op("sp", lambda e, tt=tt: e.dma_start(out=x_t[b.buf][:, tt, :], in_=b.xrows[tt]),
                 writes=[("x", b.buf, tt)], dma_slot=("x", b.buf, tt))

    def phase_hist(b):
        if not b.first:
            return
        if b.kind == "p":
            for c in range(4):
                P.op("pool", lambda e, c=c: e.memset(aview(a_ext, c, b, 0, KA - 1, KA - 1), 0.0), writes=[("ahist", c)])
                P.op("pool", lambda e, c=c: e.memset(aview(cv_ext, c, b, 0, KB - 1, KB - 1), 0.0), writes=[("bhist", c)])
            return
        s0, ns = b.seqs[0], b.nseg
        for (cache, K1, buf, hname) in ((cache_a, KA - 1, a_ext, "ahist"), (cache_b, KB - 1, cv_ext, "bhist")):
            stg, stg_r = new_tmp()
            rows = ns * K1
            P.op("sp", lambda e, stg=stg, cache=cache, rows=rows: e.dma_start(
                out=stg[0:rows, :], in_=cache[s0:s0 + ns].rearrange("s t c -> (s t) c")),
                writes=[stg_r], dma_slot=("cst", hname))
            tb1 = PS.alloc1()

            def _tr(e, stg=stg, rows=rows, tb1=tb1):
                ins = None
                for c in range(4):
                    ins = e.transpose(out=bank(tb1)[:, c * 128:c * 128 + rows], in_=stg[0:rows, c * 128:(c + 1) * 128],
                                      identity=ident_f[0:rows, 0:rows])
                return ins
            P.op("pe", _tr, reads=[stg_r, "ident_f"], writes=[("ps", tb1)])
            for c in range(4):
                P.op("dve", lambda e, c=c, tb1=tb1, rows=rows, K1=K1, buf=buf: e.tensor_copy(
                    out=aview(buf, c, b, 0, K1, K1),
                    in_=bank(tb1)[:, c * 128:c * 128 + rows].rearrange("p (s k) -> p s k", k=K1)),
                    reads=[("ps", tb1)], writes=[(hname, c)])
            PS.release(tb1)

    def phase0_elem(b):
        xs_ = []
        ss4, ss4_r = new_st4()
        for tt in range(b.TT):
            xn, xn_r = xn_t[tt], ("xn", tt)
            P.op("act", lambda e, tt=tt, xn=xn: e.activation(out=xn[:], in_=x_t[b.buf][:, tt, :], func=AF.Square,
                                                            accum_out=ss4[:, tt:tt + 1]),
                 reads=[("x", b.buf, tt)], writes=[ss4_r, xn_r])
            xs_.append((xn, xn_r))
        r4, r4_r = rstd4_ops(ss4, ss4_r, b.TT, 1.0 / D)
        for tt in range(b.TT):
            xn, xn_r = xs_[tt]
            P.op("dve", lambda e, tt=tt, xn=xn: e.scalar_tensor_tensor(
                out=xn[:], in0=x_t[b.buf][:, tt, :], scalar=r4[:, tt:tt + 1], in1=gb_t[0][:], op0=ALU.mult, op1=ALU.mult),
                reads=[("x", b.buf, tt), r4_r, ("gb", 0)], writes=[xn_r])
        b.xn = xs_

    def phase0_tr(b):
        for tt in range(b.TT):
            norm_tr(b.xn[tt][0], b.xn[tt][1], xnT, "xnT", tt)

    def win_mm(b, slot, slot_r, q):
        N = b.TT * 128
        bk = PS.alloc1()
        wv = slot.rearrange("p (q dc e) -> p q dc e", q=2, dc=NDC)

        def _mm(e):
            ins = None
            for dc in range(NDC):
                ins = e.matmul(out=bank(bk, N), lhsT=wv[:, q, dc, :], rhs=xnT[:, dc, 0:N], start=(dc == 0), stop=(dc == NDC - 1))
            return ins
        P.op("pe", _mm, reads=[slot_r] + [("xnT", t) for t in range(b.TT)], writes=[("ps", bk)])
        return bk

    def _get_chunk(b, k):
        if not hasattr(b, "pieces"):
            b.pieces = {}
        i, q = divmod(k, 2)
        if i not in b.pieces:
            b.pieces[i] = wload(ws_in[i], res_ws["in"])
        slot, slot_r = b.pieces[i]
        return win_mm(b, slot, slot_r, q)

    def phaseA1p1(b):
        N = b.TT * 128
        get_chunk = lambda k: _get_chunk(b, k)
        for c in range(4):
            bv = get_chunk(2 * c)
            bg = get_chunk(2 * c + 1)
            sg, sg_r = new_tmp()
            P.op("act", lambda e, bg=bg, sg=sg: e.activation(out=sg[:, 0:N], in_=bank(bg, N), func=AF.Sigmoid),
                 reads=[("ps", bg)], writes=[sg_r])
            P.op("dve", lambda e, bv=bv, sg=sg, c=c: e.tensor_tensor(
                out=aview(a_ext, c, b, KA - 1, b.L, KA - 1), in0=nview(bank(bv, N), b), in1=nview(sg[:, 0:N], b), op=ALU.mult),
                reads=[("ps", bv), sg_r], writes=[("abody", c)])
            PS.release(bv, bg)
            yield

    def phaseA1p2(b):
        N = b.TT * 128
        get_chunk = lambda k: _get_chunk(b, k)
        for c in range(4):
            bgc = get_chunk(8 + 3 * c)
            bvv = get_chunk(8 + 3 * c + 1)
            bgb = get_chunk(8 + 3 * c + 2)
            vs, vs_r = new_tmp()
            P.op("act", lambda e, bvv=bvv, vs=vs: e.activation(out=vs[:, 0:N], in_=bank(bvv, N), func=AF.Copy),
                 reads=[("ps", bvv)], writes=[vs_r])
            P.op("dve", lambda e, bgc=bgc, vs=vs, c=c: e.tensor_tensor(
                out=aview(cv_ext, c, b, KB - 1, b.L, KB - 1), in0=nview(bank(bgc, N), b), in1=nview(vs[:, 0:N], b), op=ALU.mult),
                reads=[("ps", bgc), vs_r], writes=[("bbody", c)])
            u, u_r = new_tmp()
            P.op("dve", lambda e, u=u, c=c: e.tensor_scalar(
                out=nview(u[:, 0:N], b), in0=aview(cv_ext, c, b, 0, b.L, KB - 1), scalar1=pT[:, c, 34:35], scalar2=None, op0=ALU.mult),
                reads=[("bbody", c), ("bhist", c), "pT"], writes=[u_r])
            for k in range(1, KB):
                P.op("dve", lambda e, u=u, c=c, k=k: e.scalar_tensor_tensor(
                    out=nview(u[:, 0:N], b), in0=aview(cv_ext, c, b, k, b.L, KB - 1), scalar=pT[:, c, 34 + k:35 + k],
                    in1=nview(u[:, 0:N], b), op0=ALU.mult, op1=ALU.add),
                    reads=[("bbody", c), ("bhist", c), "pT", u_r], writes=[u_r])
            P.op("dve", lambda e, u=u, bgb=bgb, c=c: e.tensor_tensor(
                out=mix(b, 4 + c)[:, 0:N], in0=bank(bgb, N), in1=u[:, 0:N], op=ALU.mult),
                reads=[("ps", bgb), u_r], writes=[mix_r(b, 4 + c)])
            PS.release(bgc, bvv, bgb)
            yield

    def phase_conv(b):
        N = b.TT * 128
        acv = [nview(ac_t[:, c, 0:N], b) for c in range(4)]
        for c in range(4):
            P.op("dve", lambda e, c=c: e.tensor_scalar(
                out=acv[c], in0=aview(a_ext, c, b, 0, b.L, KA - 1), scalar1=pT[:, c, 0:1], scalar2=pT[:, c, 31:32],
                op0=ALU.mult, op1=ALU.add),
                reads=[("abody", c), ("ahist", c), "pT"], writes=[("ac", c)])
        yield
        for k in range(1, KA):
            for c in range(4):
                P.op("dve", lambda e, c=c, k=k: e.scalar_tensor_tensor(
                    out=acv[c], in0=aview(a_ext, c, b, k, b.L, KA - 1), scalar=pT[:, c, k:k + 1], in1=acv[c],
                    op0=ALU.mult, op1=ALU.add),
                    reads=[("abody", c), ("ahist", c), "pT", ("ac", c)], writes=[("ac", c)])
            yield

    def phase_tail(b, which):
        buf, K1, body, hist = (a_ext, KA - 1, "abody", "ahist") if which == "a" else (cv_ext, KB - 1, "bbody", "bhist")
        if b.last:
            if which == "a":
                dst = na_p if b.kind == "p" else na_s
            else:
                dst = nb_p if b.kind == "p" else nb_s
            for si, s in enumerate(b.seqs):
                tb1 = PS.alloc1()

                def _t1(e, si=si, tb1=tb1):
                    ins = None
                    for c in range(4):
                        ins = e.transpose(out=bank(tb1)[0:K1, c * 128:(c + 1) * 128],
                                          in_=aview(buf, c, b, b.L, K1, K1)[:, si, :], identity=ident_f[:])
                    return ins
                P.op("pe", _t1, reads=[(body, c) for c in range(4)] + [(hist, c) for c in range(4)] + ["ident_f"],
                     writes=[("ps", tb1)])
                og, og_r = new_tmp()
                P.op("act", lambda e, tb1=tb1, og=og: e.activation(out=og[0:K1, :], in_=bank(tb1)[0:K1, :], func=AF.Copy),
                     reads=[("ps", tb1)], writes=[og_r])
                PS.release(tb1)
                P.op("sp", lambda e, s=s, og=og: e.dma_start(out=dst[s, :, :], in_=og[0:K1, :]),
                     reads=[og_r], writes=[("out", which, b.kind, s)], dma_slot=("o" + which, si))
        else:
            for c in range(4):
                P.op("pool", lambda e, c=c: e.tensor_copy(out=buf[:, c, 0:K1], in_=buf[:, c, b.L:b.L + K1]),
                     reads=[(body, c), (hist, c)], writes=[(hist, c)])

    def phaseA2_stats(b):
        N = b.TT * 128
        bm, be = PS.alloc1(), PS.alloc1()
        b.bm, b.be = bm, be
        sqs = []
        for c in range(4):
            sq, sq_r = new_tmp()
            P.op("act", lambda e, c=c, sq=sq: e.activation(out=sq[:, 0:N], in_=ac_t[:, c, 0:N], func=AF.Square),
                 reads=[("ac", c)], writes=[sq_r])
            sqs.append((sq, sq_r))

        def _m1(e):
            ins = None
            for c in range(4):
                ins = e.matmul(out=bank(bm, N), lhsT=ones_f[:], rhs=ac_t[:, c, 0:N], start=(c == 0), stop=(c == 3))
            return ins
        P.op("pe", _m1, reads=[("ac", c) for c in range(4)] + ["ones_f"], writes=[("ps", bm)])

        def _m2(e):
            ins = None
            for c in range(4):
                ins = e.matmul(out=bank(be, N), lhsT=ones_f[:], rhs=sqs[c][0][:, 0:N], start=(c == 0), stop=(c == 3))
            return ins
        P.op("pe", _m2, reads=[s_[1] for s_ in sqs] + ["ones_f"], writes=[("ps", be)])

    def phaseA2_norm(b):
        N = b.TT * 128
        bm, be = b.bm, b.be
        msq, msq_r = new_tmp()
        P.op("act", lambda e: e.activation(out=msq[:, 0:N], in_=bank(bm, N), func=AF.Square), reads=[("ps", bm)], writes=[msq_r])
        var, var_r = new_tmp()
        P.op("dve", lambda e: e.tensor_tensor(out=var[:, 0:N], in0=bank(be, N), in1=msq[:, 0:N], op=ALU.subtract),
             reads=[("ps", be), msq_r], writes=[var_r])
        P.op("act", lambda e: e.activation(out=var[:, 0:N], in_=var[:, 0:N], func=AF.Sqrt, bias=eps_t[:, 0:1]),
             reads=[var_r, "eps_t"], writes=[var_r])
        rs, rs_r = new_tmp(pin=True)
        P.op("dve", lambda e: e.reciprocal(out=rs[:, 0:N], in_=var[:, 0:N]), reads=[var_r], writes=[rs_r])
        PS.release(be)
        for c in range(4):
            z, z_r = new_tmp()
            P.op("dve", lambda e, c=c, z=z: e.tensor_tensor(out=z[:, 0:N], in0=ac_t[:, c, 0:N], in1=bank(bm, N), op=ALU.subtract),
                 reads=[("ac", c), ("ps", bm)], writes=[z_r])
            P.op("dve", lambda e, z=z: e.tensor_tensor(out=z[:, 0:N], in0=z[:, 0:N], in1=rs[:, 0:N], op=ALU.mult),
                 reads=[z_r, rs_r], writes=[z_r])
            P.op("act", lambda e, c=c, z=z: e.activation(out=mix(b, c)[:, 0:N], in_=z[:, 0:N], func=AF.Silu,
                                                        scale=pT[:, c, 32:33], bias=pT[:, c, 33:34]),
                 reads=[z_r, "pT"], writes=[("mixT", c)])
            if c == 3:
                PS.release(bm)
                tmp_pinned.discard(rs_r[1])
            yield

    def phaseB(b):
        slots = [wload(ws_out[i], res_ws["out"], pin=True) for i in range(4)]
        b.hn = [(hn_t[tt], ("hn", tt)) for tt in range(b.TT)]
        pend = {}

        def stage1(tt):
            pb = PS.alloc_pair()

            def _mm(e, tt=tt, pb=pb):
                ins = None
                for half in range(2):
                    for cc in range(NDC):
                        sl = slots[half * 2 + cc // 4][0].rearrange("p (cc d) -> p cc d", cc=4)
                        ins = e.matmul(out=bank(pb + half), lhsT=mix(b, cc)[:, tt * 128:(tt + 1) * 128], rhs=sl[:, cc % 4, :],
                                       start=(cc == 0), stop=(cc == NDC - 1))
                return ins
            P.op("pe", _mm, reads=[s_[1] for s_ in slots] + [mix_r(b, cc) for cc in range(NDC)],
                 writes=[("ps", pb), ("ps", pb + 1)])
            ss, ss_r = new_st()
            P.op("act", lambda e, pb=pb, ss=ss, tt=tt: e.activation(out=hn_t[tt][:], in_=bank(pb, 1024), func=AF.Square, accum_out=ss),
                 reads=[("ps", pb), ("ps", pb + 1)], writes=[ss_r, ("hn", tt)])
            r, r_r = rstd_ops(ss, ss_r, 1.0 / D)
            P.op("dve", lambda e, tt=tt, pb=pb, r=r: e.scalar_tensor_tensor(
                out=h_t[:, tt, :], in0=bank(pb, 1024), scalar=r, in1=gb_t[1][:], op0=ALU.mult, op1=ALU.mult),
                reads=[("ps", pb), ("ps", pb + 1), r_r, ("gb", 1)], writes=[("h", tt)])
            PS.release(pb, pb + 1)
            P.op("dve", lambda e, tt=tt: e.tensor_tensor(out=h_t[:, tt, :], in0=h_t[:, tt, :], in1=x_t[b.buf][:, tt, :], op=ALU.add),
                 reads=[("h", tt), ("x", b.buf, tt)], writes=[("h", tt)])
            ss2, ss2_r = new_st()
            P.op("act", lambda e, tt=tt, ss2=ss2: e.activation(out=hn_t[tt][:], in_=h_t[:, tt, :], func=AF.Square, accum_out=ss2),
                 reads=[("h", tt)], writes=[ss2_r, ("hn", tt)])
            pend[tt] = rstd_ops(ss2, ss2_r, 1.0 / D)

        def stage2(tt):
            r2, r2_r = pend[tt]
            P.op("dve", lambda e, tt=tt, r2=r2: e.scalar_tensor_tensor(
                out=hn_t[tt][:], in0=h_t[:, tt, :], scalar=r2, in1=gb_t[2][:], op0=ALU.mult, op1=ALU.mult),
                reads=[("h", tt), r2_r, ("gb", 2)], writes=[("hn", tt)])

        for tt in range(b.TT):
            stage1(tt)
            if tt == b.TT - 1:
                for s_ in slots:
                    ring_pinned.discard(s_[1][1])
            if tt > 0:
                stage2(tt - 1)
            yield
        stage2(b.TT - 1)
        yield

    def phaseC(b):
        for tt in range(b.TT):
            norm_tr(b.hn[tt][0], b.hn[tt][1], hnT, "hnT", tt)

    def phaseD(b):
        N = b.TT * 128
        for j in range(NFC):
            slot, slot_r = wload(ws_gu[j], res_ws["gu"])
            wv = slot.rearrange("p (gu dc e) -> p gu dc e", gu=2, dc=NDC)
            pb = PS.alloc_pair()

            def _mm(e, wv=wv, pb=pb):
                ins = None
                for gu in range(2):
                    for dc in range(NDC):
                        ins = e.matmul(out=bank(pb + gu, N), lhsT=wv[:, gu, dc, :], rhs=hnT[:, dc, 0:N],
                                       start=(dc == 0), stop=(dc == NDC - 1))
                return ins
            P.op("pe", _mm, reads=[slot_r] + [("hnT", t) for t in range(b.TT)], writes=[("ps", pb), ("ps", pb + 1)])
            sg, sg_r = new_tmp()
            P.op("act", lambda e, pb=pb, sg=sg: e.activation(out=sg[:, 0:N], in_=bank(pb, N), func=AF.Silu),
                 reads=[("ps", pb)], writes=[sg_r])
            P.op("dve", lambda e, pb=pb, sg=sg, j=j: e.tensor_tensor(out=actT[:, j, 0:N], in0=bank(pb + 1, N), in1=sg[:, 0:N], op=ALU.mult),
                 reads=[("ps", pb + 1), sg_r], writes=[("actT", j)])
            PS.release(pb, pb + 1)
            yield

    def phaseE_half(b, half):
        banks = [PS.alloc1() for _ in range(b.TT)]
        for q in range(6):
            nfc = 4 if q < 5 else 2
            slot, slot_r = wload(ws_dn[half * 6 + q], res_ws["dn"], cols=nfc * 512)
            sl = slot.rearrange("p (fc d) -> p fc d", fc=4)

            def _mm(e, q=q, nfc=nfc, sl=sl):
                ins = None
                for f in range(nfc):
                    fc = 4 * q + f
                    for tt in range(b.TT):
                        ins = e.matmul(out=bank(banks[tt]), lhsT=actT[:, fc, tt * 128:(tt + 1) * 128], rhs=sl[:, f, :],
                                       start=(fc == 0), stop=(fc == NFC - 1))
                return ins
            P.op("pe", _mm, reads=[slot_r] + [("actT", 4 * q + f) for f in range(nfc)],
                 writes=[("ps", bk) for bk in banks])
            yield
        if half == 0:
            ss0, ss0_r = new_st4()
            for tt in range(b.TT):
                jk, jk_r = new_tmp()
                P.op("act", lambda e, tt=tt, jk=jk: e.activation(out=jk[:], in_=bank(banks[tt]), func=AF.Square,
                                                                accum_out=ss0[:, tt:tt + 1]),
                     reads=[("ps", banks[tt])], writes=[ss0_r, jk_r])
                P.op("dve", lambda e, tt=tt: e.tensor_copy(out=cv_ext[:, tt, 2:514], in_=bank(banks[tt])),
                     reads=[("ps", banks[tt]), ss0_r], writes=[("bbody", tt)])
            b.ss0 = (ss0, ss0_r)
            PS.release(*banks)
        else:
            b.ebanks = banks
        yield

    def phaseE_epi(b):
        banks = b.ebanks
        ss1, ss1_r = new_st4()
        for tt in range(b.TT):
            jk, jk_r = new_tmp()
            P.op("act", lambda e, tt=tt, jk=jk: e.activation(out=jk[:], in_=bank(banks[tt]), func=AF.Square,
                                                            accum_out=ss1[:, tt:tt + 1]),
                 reads=[("ps", banks[tt])], writes=[ss1_r, jk_r])
        r4, r4_r = rstd4_ops(ss1, ss1_r, b.TT, 1.0 / D, add=b.ss0)
        for tt in range(b.TT):
            r = r4[:, tt:tt + 1]
            for half in range(2):
                t, t_r = new_tmp()
                if half == 0:
                    P.op("dve", lambda e, tt=tt, t=t, r=r: e.scalar_tensor_tensor(
                        out=t[:], in0=cv_ext[:, tt, 2:514], scalar=r, in1=gb_t[3][:, 0:512], op0=ALU.mult, op1=ALU.mult),
                        reads=[("bbody", tt), r4_r, ("gb", 3)], writes=[t_r])
                else:
                    P.op("dve", lambda e, tt=tt, t=t, r=r: e.scalar_tensor_tensor(
                        out=t[:], in0=bank(banks[tt]), scalar=r, in1=gb_t[3][:, 512:1024], op0=ALU.mult, op1=ALU.mult),
                        reads=[("ps", banks[tt]), r4_r, ("gb", 3)], writes=[t_r])
                P.op("dve", lambda e, tt=tt, half=half, t=t: e.tensor_tensor(
                    out=h_t[:, tt, half * 512:(half + 1) * 512], in0=h_t[:, tt, half * 512:(half + 1) * 512], in1=t[:], op=ALU.add),
                    reads=[("h", tt), t_r], writes=[("h", tt)])
            PS.release(banks[tt])
            P.op("sp", lambda e, tt=tt: e.dma_start(out=b.yrows[tt], in_=h_t[:, tt, :]),
                 reads=[("h", tt)], writes=[("out", "y", b.bi, tt)], dma_slot=("y", tt))

    def run(*gens):
        for g in gens:
            if g is not None:
                for _ in g:
                    pass

    def chain(*gens):
        for g in gens:
            if g is not None:
                yield from g

    def interleave(ga, gb):
        da = db = False
        while not (da and db):
            if not da:
                try:
                    next(ga)
                except StopIteration:
                    da = True
            if not db:
                try:
                    next(gb)
                except StopIteration:
                    db = True

    nblk = len(blocks)
    import os as _os
    _stop = int(_os.environ.get("KDBG_STOP", "999"))

    def driver():
        load_x(blocks[0])
        if nblk > 1:
            load_x(blocks[1])
        b0 = blocks[0]
        phase0_elem(b0)
        phase0_tr(b0)
        phase_hist(b0)
        run(phaseA1p1(b0), phaseA1p2(b0))
        phase_tail(b0, "b")
        if nblk > 1:
            phase0_elem(blocks[1])
        prep_gu_dn()
        for i in range(nblk + 1):
            b = blocks[i] if i < nblk else None
            pb_ = blocks[i - 1] if i > 0 else None
            nb_ = blocks[i + 1] if i + 1 < nblk else None
            if pb_ is not None:
                phaseC(pb_)
            if i > 0 and nb_ is not None:
                phase0_elem(nb_)
            ffn = chain(phaseD(pb_), phaseE_half(pb_, 0), phaseE_half(pb_, 1)) if pb_ is not None else iter(())
            conv = phase_conv(b) if b is not None else iter(())
            interleave(ffn, conv)
            if b is not None:
                phase_tail(b, "a")
            if pb_ is not None:
                phaseE_epi(pb_)
            if b is None:
                break
            phaseA2_stats(b)
            if nb_ is not None:
                phase0_tr(nb_)
                phase_hist(nb_)
            interleave(phaseA2_norm(b), phaseA1p1(nb_) if nb_ is not None else iter(()))
            interleave(phaseB(b), phaseA1p2(nb_) if nb_ is not None else iter(()))
            if nb_ is not None:
                phase_tail(nb_, "b")
            if i + 2 < nblk:
                load_x(blocks[i + 2])

    driver()

    out_res = [r for r in P.lastw if isinstance(r, tuple) and r and r[0] == "out"]
    P.op("sp", lambda e: e.nop(), reads=out_res)

    P.finalize()
    dma_slots = list(P.dma_cnt.keys())
    esem = {e: es.enter_context(nc.semaphore(f"sem_{e}")) for e in Prog.ENGS}
    dsem = {s: es.enter_context(nc.semaphore(f"dsem_{i}")) for i, s in enumerate(dma_slots)}
    with nc.Block() as block:
        @block.tensor
        def _(e):
            P.emit_engine("pe", e, esem, dsem)

        @block.scalar
        def _(e):
            P.emit_engine("act", e, esem, dsem)

        @block.vector
        def _(e):
            P.emit_engine("dve", e, esem, dsem)

        @block.gpsimd
        def _(e):
            P.emit_engine("pool", e, esem, dsem)

        @block.sync
        def _(e):
            P.emit_engine("sp", e, esem, dsem)
    es.close()
    return nc


def make_in_maps(n_cores, inputs, NP, NS):
    f = lambda a: np.ascontiguousarray(np.asarray(a, dtype=np.float32))
    ident = np.eye(128, dtype=np.float32)
    maps = []
    for c in range(n_cores):
        m = {
            "x_p": f(inputs["x_prompt"][c * NP:(c + 1) * NP]),
            "x_s": f(inputs["x_sample"][c * NS:(c + 1) * NS]),
            "cache_a": f(inputs["cache_conv_a"][0, c * NS:(c + 1) * NS]),
            "cache_b": f(inputs["cache_conv_b"][0, c * NS:(c + 1) * NS]),
            "g_pre1": f(inputs["norm_mix_pre"]),
            "g_post1": f(inputs["norm_mix_post"]),
            "g_pre2": f(inputs["norm_ffn_pre"]),
            "g_post2": f(inputs["norm_ffn_post"]),
            "w_in": f(inputs["w_in"][0]),
            "w_out": f(inputs["w_out"][0]),
            "w_gu": f(inputs["w_gate_up"][0]),
            "w_dn": f(inputs["w_down"][0]),
            "conv_a_w": f(inputs["conv_a_w"][0]),
            "conv_a_b": f(inputs["conv_a_b"]),
            "ln_g": f(inputs["conv_a_ln_g"]),
            "ln_b": f(inputs["conv_a_ln_b"]),
            "conv_b_w": f(inputs["conv_b_w"][0]),
            "ident": ident,
        }
        maps.append(m)
    return maps


def gather(results, n_cores):
    cat = lambda k: np.concatenate([np.asarray(results[c][k], dtype=np.float32) for c in range(n_cores)], axis=0)
    return (cat("y_p"), cat("y_s"), cat("na_p")[None], cat("nb_p")[None], cat("na_s")[None], cat("nb_s")[None])


def kernel(**inputs):
    B, SEQ = inputs["x_prompt"].shape[0], inputs["x_prompt"].shape[1]
    BS, LS = inputs["x_sample"].shape[0], inputs["x_sample"].shape[1]
    NP, NS = B // N_CORES, BS // N_CORES
    nc = build_program(NP, SEQ, NS, LS)
    in_maps = make_in_maps(N_CORES, inputs, NP, NS)
    res = run_bass_kernel_spmd(nc, in_maps, core_ids=list(range(N_CORES)))
    return gather(res.results, N_CORES)
```

```python
import numpy as np
import concourse.bass as bass
import concourse.mybir as mybir
from concourse.bass_utils import run_bass_kernel_spmd

F32 = mybir.dt.float32
BF16 = mybir.dt.bfloat16
AF = mybir.ActivationFunctionType
ALU = mybir.AluOpType

D = 1024
DA = 512
DB = 512
DIN = 2560
DFF = 2816
KA = 31
KB = 3
EPS = 1e-6
NDC = 8
NFC = 22
NPAR = 37
RING = 8
NTMP = 9
N_CORES = 8

E_ORDER = [0, 4, 1, 5, 2, 6, 3, 7, 12, 16, 8, 13, 17, 9, 14, 18, 10, 15, 19, 11]


class _Op:
    __slots__ = ("eng", "fn", "deps", "idx", "signal", "dma_slot", "dma_val", "waits", "sigval")


class Prog:
    ENGS = ("pe", "act", "dve", "pool", "sp")

    def __init__(self):
        self.ops = {e: [] for e in self.ENGS}
        self.lastw = {}
        self.readers = {}
        self.dma_cnt = {}

    def op(self, eng, fn, reads=(), writes=(), dma_slot=None):
        o = _Op()
        o.eng = eng
        o.fn = fn
        o.idx = len(self.ops[eng])
        o.signal = False
        deps = []
        for r in reads:
            w = self.lastw.get(r)
            if w is not None:
                deps.append(w)
        for r in writes:
            w = self.lastw.get(r)
            if w is not None:
                deps.append(w)
            deps.extend(self.readers.get(r, ()))
        o.deps = deps
        if dma_slot is not None:
            c = self.dma_cnt.get(dma_slot, 0) + 1
            self.dma_cnt[dma_slot] = c
            o.dma_slot = dma_slot
            o.dma_val = 16 * c
            tok = ("dma", dma_slot, 16 * c)
        else:
            o.dma_slot = None
            o.dma_val = 0
            tok = ("eng", eng, o.idx)
        for r in reads:
            self.readers.setdefault(r, []).append(tok)
        for r in writes:
            self.lastw[r] = tok
            self.readers[r] = []
        self.ops[eng].append(o)
        return o

    def finalize(self):
        for e in self.ENGS:
            seen = {}
            for o in self.ops[e]:
                need_eng = {}
                need_dma = {}
                for d in o.deps:
                    if d[0] == "eng":
                        _, de, di = d
                        if de == e:
                            if e in ("pe", "sp"):
                                continue
                        if seen.get(de, -1) >= di:
                            continue
                        if need_eng.get(de, -1) < di:
                            need_eng[de] = di
                    else:
                        _, slot, val = d
                        if seen.get(("dma", slot), 0) >= val:
                            continue
                        if need_dma.get(slot, 0) < val:
                            need_dma[slot] = val
                waits = []
                for de, di in need_eng.items():
                    seen[de] = di
                    waits.append(("eng", de, di))
                    self.ops[de][di].signal = True
                for slot, val in need_dma.items():
                    seen[("dma", slot)] = val
                    waits.append(("dma", slot, val))
                o.waits = waits
        for e in self.ENGS:
            c = 0
            for o in self.ops[e]:
                if o.signal:
                    c += 1
                o.sigval = c

    def emit_engine(self, e, eng, esem, dsem):
        for o in self.ops[e]:
            for w in o.waits:
                if w[0] == "eng":
                    eng.wait_ge(esem[w[1]], self.ops[w[1]][w[2]].sigval)
                else:
                    eng.wait_ge(dsem[w[1]], w[2])
            ins = o.fn(eng)
            if o.dma_slot is not None:
                ins.then_inc(dsem[o.dma_slot], 16)
            elif o.signal:
                ins.then_inc(esem[e], 1)


class _Psum:
    def __init__(self):
        self.held = [False] * 8
        self.stamp = [0] * 8
        self.clock = 0

    def alloc1(self):
        free = [b for b in range(8) if not self.held[b]]
        assert free, "out of PSUM banks"
        b = min(free, key=lambda k: self.stamp[k])
        self.held[b] = True
        return b

    def alloc_pair(self):
        free = [b for b in range(0, 8, 2) if not self.held[b] and not self.held[b + 1]]
        assert free, "out of PSUM bank pairs"
        b = min(free, key=lambda k: max(self.stamp[k], self.stamp[k + 1]))
        self.held[b] = self.held[b + 1] = True
        return b

    def release(self, *banks):
        for b in banks:
            assert self.held[b]
            self.held[b] = False
            self.clock += 1
            self.stamp[b] = self.clock


def build_program(NP, SEQ, NS, LS=64):
    assert SEQ % 512 == 0 and (NS * LS) % 128 == 0 and NS * (KA - 1) <= 128
    nc = bass.Bass("TRN2", target_bir_lowering=False)
    P = Prog()
    PS = _Psum()

    def din(name, shape, dt=F32):
        return nc.dram_tensor(name, list(shape), dt, kind="ExternalInput").ap()

    def dout(name, shape, dt=F32):
        return nc.dram_tensor(name, list(shape), dt, kind="ExternalOutput").ap()

    x_p = din("x_p", [NP, SEQ, D])
    x_s = din("x_s", [NS, LS, D])
    cache_a = din("cache_a", [NS, KA - 1, DA])
    cache_b = din("cache_b", [NS, KB - 1, DB])
    g_pre1 = din("g_pre1", [1, D])
    g_post1 = din("g_post1", [1, D])
    g_pre2 = din("g_pre2", [1, D])
    g_post2 = din("g_post2", [1, D])
    w_in = din("w_in", [D, DIN])
    w_out = din("w_out", [D, D])
    w_gu = din("w_gu", [D, 2 * DFF])
    w_dn = din("w_dn", [DFF, D])
    conv_a_w = din("conv_a_w", [KA, DA])
    conv_a_b = din("conv_a_b", [1, DA])
    ln_g = din("ln_g", [1, DA])
    ln_b = din("ln_b", [1, DA])
    conv_b_w = din("conv_b_w", [KB, DB])
    ident_in = din("ident", [128, 128])

    y_p = dout("y_p", [NP, SEQ, D])
    y_s = dout("y_s", [NS, LS, D])
    na_p = dout("na_p", [NP, KA - 1, DA])
    nb_p = dout("nb_p", [NP, KB - 1, DB])
    na_s = dout("na_s", [NS, KA - 1, DA])
    nb_s = dout("nb_s", [NS, KB - 1, DB])

    ws_in = nc.dram_tensor("ws_in", [10, 128, 2048], BF16, kind="Internal").ap()
    ws_out = nc.dram_tensor("ws_out", [4, 128, 2048], BF16, kind="Internal").ap()
    ws_gu = nc.dram_tensor("ws_gu", [NFC, 128, 2048], BF16, kind="Internal").ap()
    ws_dn = nc.dram_tensor("ws_dn", [12, 128, 2048], BF16, kind="Internal").ap()

    from contextlib import ExitStack
    es = ExitStack()

    def sb(name, shape, dt=F32):
        return es.enter_context(nc.sbuf_tensor(name, list(shape), dt))

    ident_f = sb("ident_f", [128, 128])
    ident_b = sb("ident_b", [128, 128], BF16)
    ones_f = sb("ones_f", [128, 128])
    neghalf = sb("neghalf", [128, 1])
    eps_t = sb("eps_t", [128, 1])
    gb_t = [sb(f"gb{i}", [128, D]) for i in range(4)]
    pT = sb("pT", [128, 4, NPAR])
    x_t = [sb(f"x{i}", [128, 4, D]) for i in range(2)]
    h_t = sb("h", [128, 4, D])
    xn_t = [sb(f"xn{i}", [128, D], BF16) for i in range(4)]
    hn_t = [sb(f"hn{i}", [128, D], BF16) for i in range(4)]
    xnT = sb("xnT", [128, NDC, 512], BF16)
    hnT = sb("hnT", [128, NDC, 512], BF16)
    a_ext = sb("a_ext", [128, 4, 544])
    cv_ext = sb("cv_ext", [128, 4, 516])
    ac_t = sb("ac", [128, 4, 512])
    tmp_t = [sb(f"tmp{i}", [128, 512]) for i in range(NTMP)]
    mixT_lo = sb("mixT_lo", [128, 4, 512], BF16)
    mixT_hi = [sb(f"mixT_hi{i}", [128, 4, 512], BF16) for i in range(2)]

    def mix(b, cc):
        return mixT_lo[:, cc] if cc < 4 else mixT_hi[b.buf][:, cc - 4]

    def mix_r(b, cc):
        return ("mixT", cc) if cc < 4 else ("mixT", cc, b.buf)
    actT = sb("actT", [128, NFC, 512], BF16)
    ring = [sb(f"ring{i}", [128, 2048], BF16) for i in range(RING)]
    st = sb("st", [128, 64])
    st4 = sb("st4", [128, 64])
    ps = es.enter_context(nc.psum_tensor("ps", [128, 4096], F32))
    print("SBUF bytes remaining per partition:", nc.sbuf_bytes_remaining)

    def bank(b, n=512):
        return ps[:, b * 512:b * 512 + n]

    cnt = {"tmp": 0, "st": 0, "ring": 0, "st4": 0}

    tmp_pinned = set()

    def new_tmp(pin=False):
        while True:
            i = cnt["tmp"] % NTMP
            cnt["tmp"] += 1
            if i not in tmp_pinned:
                break
        if pin:
            tmp_pinned.add(i)
        return tmp_t[i], ("tmp", i)

    def new_st():
        i = cnt["st"] % 64
        cnt["st"] += 1
        return st[:, i:i + 1], ("st", i)

    def new_st4():
        i = cnt["st4"] % 16
        cnt["st4"] += 1
        return st4[:, 4 * i:4 * i + 4], ("st4", i)

    ring_pinned = set()

    def wload(src_ap, src_res, cols=2048, pin=False):
        while True:
            i = cnt["ring"] % RING
            cnt["ring"] += 1
            if i not in ring_pinned:
                break
        if pin:
            ring_pinned.add(i)
        dst = ring[i]
        P.op("sp", lambda e, dst=dst, src_ap=src_ap, cols=cols: e.dma_start(out=dst[:, 0:cols], in_=src_ap[:, 0:cols]),
             reads=src_res, writes=[("ring", i)], dma_slot=("ring", i))
        return dst, ("ring", i)

    res_ws = {"in": [], "out": [], "gu": [], "dn": []}
    w_in_v = w_in.rearrange("(dc p) (j e) -> p j dc e", p=128, e=128)
    ws_in_v = ws_in.rearrange("i p (q dc e) -> i p q dc e", q=2, dc=NDC)
    w_out_v = w_out.rearrange("(cc p) d -> p cc d", p=128)
    ws_out_v = ws_out.rearrange("i p (cc d) -> i p cc d", cc=4)
    w_gu_v = w_gu.rearrange("(dc p) (gu j e) -> p gu j dc e", p=128, gu=2, e=128)
    ws_gu_v = ws_gu.rearrange("j p (gu dc e) -> j p gu dc e", gu=2, dc=NDC)
    w_dn_v = w_dn.rearrange("(fc p) d -> p fc d", p=128)
    ws_dn_v = ws_dn.rearrange("i p (fc d) -> i p fc d", fc=4)

    def prep_in_out():
        for k, j in enumerate(E_ORDER):
            i, q = divmod(k, 2)
            r = ("ws", "in", k)
            res_ws["in"].append(r)
            P.op("pool", lambda e, i=i, q=q, j=j: e.dma_start(out=ws_in_v[i, :, q, :, :], in_=w_in_v[:, j, :, :]),
                 writes=[r], dma_slot="prep_in")
        for half in range(2):
            for q in range(2):
                r = ("ws", "out", half * 2 + q)
                res_ws["out"].append(r)
                P.op("pool", lambda e, half=half, q=q: e.dma_start(
                    out=ws_out_v[half * 2 + q], in_=w_out_v[:, 4 * q:4 * q + 4, half * 512:(half + 1) * 512]),
                    writes=[r], dma_slot="prep_out")

    def prep_gu_dn():
        for j in range(NFC):
            for gu in range(2):
                r = ("ws", "gu", j * 2 + gu)
                res_ws["gu"].append(r)
                P.op("pool", lambda e, j=j, gu=gu: e.dma_start(out=ws_gu_v[j, :, gu, :, :], in_=w_gu_v[:, gu, j, :, :]),
                     writes=[r], dma_slot="prep_gu")
        for half in range(2):
            for q in range(6):
                nfc = 4 if q < 5 else 2
                r = ("ws", "dn", half * 6 + q)
                res_ws["dn"].append(r)
                P.op("pool", lambda e, half=half, q=q, nfc=nfc: e.dma_start(
                    out=ws_dn_v[half * 6 + q, :, 0:nfc, :], in_=w_dn_v[:, 4 * q:4 * q + nfc, half * 512:(half + 1) * 512]),
                    writes=[r], dma_slot="prep_dn")

    prep_in_out()

    P.op("sp", lambda e: e.dma_start(out=ident_f[:], in_=ident_in[:, :]), writes=["ident_f"], dma_slot="c0")
    for i, g in enumerate((g_pre1, g_post1, g_pre2, g_post2)):
        P.op("sp", lambda e, i=i, g=g: e.dma_start(out=gb_t[i][:], in_=g[0, :].partition_broadcast(128)),
             writes=[("gb", i)], dma_slot=("c1", i))
    pstage, pstage_r = new_tmp()
    P.op("sp", lambda e: e.dma_start(out=pstage[0:KA, :], in_=conv_a_w[:, :]), writes=[pstage_r], dma_slot="c2")
    P.op("sp", lambda e: e.dma_start(out=pstage[31:32, :], in_=conv_a_b[:, :]), writes=["ps1"], dma_slot="c3")
    P.op("sp", lambda e: e.dma_start(out=pstage[32:33, :], in_=ln_g[:, :]), writes=["ps2"], dma_slot="c4")
    P.op("sp", lambda e: e.dma_start(out=pstage[33:34, :], in_=ln_b[:, :]), writes=["ps3"], dma_slot="c5")
    P.op("sp", lambda e: e.dma_start(out=pstage[34:37, :], in_=conv_b_w[:, :]), writes=["ps4"], dma_slot="c6")
    P.op("dve", lambda e: e.tensor_copy(out=ident_b[:], in_=ident_f[:]), reads=["ident_f"], writes=["ident_b"])
    P.op("dve", lambda e: e.memset(ones_f[:], 1.0 / DA), writes=["ones_f"])
    P.op("dve", lambda e: e.memset(neghalf[:], -0.5), writes=["neghalf"])
    P.op("dve", lambda e: e.memset(eps_t[:], EPS), writes=["eps_t"])

    tb = PS.alloc1()

    def _ptr(e):
        ins = None
        for c in range(4):
            ins = e.transpose(out=bank(tb)[:, c * 64:c * 64 + NPAR], in_=pstage[0:NPAR, c * 128:(c + 1) * 128],
                              identity=ident_f[0:NPAR, 0:NPAR])
        return ins
    P.op("pe", _ptr, reads=[pstage_r, "ps1", "ps2", "ps3", "ps4", "ident_f"], writes=[("ps", tb)])
    P.op("dve", lambda e: e.tensor_copy(out=pT[:], in_=bank(tb).rearrange("p (c k) -> p c k", k=64)[:, 0:4, 0:NPAR]),
         reads=[("ps", tb)], writes=["pT"])
    PS.release(tb)

    def rstd_ops(ss, ss_r, scale):
        t, t_r = new_st()
        r, r_r = new_st()
        P.op("pool", lambda e: e.tensor_scalar(out=t, in0=ss, scalar1=scale, scalar2=EPS, op0=ALU.mult, op1=ALU.add),
             reads=[ss_r], writes=[t_r])
        P.op("pool", lambda e: e.tensor_tensor(out=r, in0=t, in1=neghalf[:, 0:1], op=ALU.pow),
             reads=[t_r, "neghalf"], writes=[r_r])
        return r, r_r

    def rstd4_ops(ss4, ss4_r, n, scale, add=None):
        t, t_r = new_st4()
        r, r_r = new_st4()
        src, src_rs = ss4, [ss4_r]
        if add is not None:
            a4, a4_r = add
            u, u_r = new_st4()
            P.op("pool", lambda e: e.tensor_tensor(out=u[:, 0:n], in0=ss4[:, 0:n], in1=a4[:, 0:n], op=ALU.add),
                 reads=[ss4_r, a4_r], writes=[u_r])
            src, src_rs = u, [u_r]
        P.op("pool", lambda e: e.tensor_scalar(out=t[:, 0:n], in0=src[:, 0:n], scalar1=scale, scalar2=EPS, op0=ALU.mult, op1=ALU.add),
             reads=src_rs, writes=[t_r])
        P.op("pool", lambda e: e.tensor_tensor(out=r[:, 0:n], in0=t[:, 0:n], in1=neghalf[:, 0:1].broadcast_to([128, n]), op=ALU.pow),
             reads=[t_r, "neghalf"], writes=[r_r])
        return r, r_r

    def norm_elem(src, src_r, gi, xn, xn_r):
        ss, ss_r = new_st()
        P.op("act", lambda e: e.activation(out=xn[:], in_=src, func=AF.Square, accum_out=ss),
             reads=[src_r], writes=[ss_r, xn_r])
        r, r_r = rstd_ops(ss, ss_r, 1.0 / D)
        P.op("dve", lambda e: e.scalar_tensor_tensor(out=xn[:], in0=src, scalar=r, in1=gb_t[gi][:],
                                                     op0=ALU.mult, op1=ALU.mult),
             reads=[src_r, r_r, ("gb", gi)], writes=[xn_r])

    def norm_tr(xn, xn_r, dstT, dst_name, tt):
        b = PS.alloc1()
        pb = bank(b).bitcast(BF16)

        def _tr(e):
            ins = None
            for c in range(NDC):
                ins = e.transpose(out=pb[:, c * 128:(c + 1) * 128], in_=xn[:, c * 128:(c + 1) * 128], identity=ident_b[:])
            return ins
        P.op("pe", _tr, reads=[xn_r, "ident_b"], writes=[("ps", b)])
        P.op("act", lambda e: e.activation(out=dstT[:, :, tt * 128:(tt + 1) * 128],
                                           in_=pb.rearrange("p (c t) -> p c t", t=128), func=AF.Copy),
             reads=[("ps", b)], writes=[(dst_name, tt)])
        PS.release(b)

    class Blk:
        pass

    def make_blocks():
        blks = []
        for s in range(NP):
            nb = SEQ // 512
            for k in range(nb):
                b = Blk()
                b.kind = "p"
                b.nseg, b.L, b.TT = 1, 512, 4
                b.first, b.last = (k == 0), (k == nb - 1)
                b.seqs = [s]
                b.xrows = [x_p[s, k * 512 + t * 128:k * 512 + (t + 1) * 128, :] for t in range(4)]
                b.yrows = [y_p[s, k * 512 + t * 128:k * 512 + (t + 1) * 128, :] for t in range(4)]
                blks.append(b)
        xs = x_s.rearrange("s t d -> (s t) d")
        ys = y_s.rearrange("s t d -> (s t) d")
        spb = 512 // LS
        for k0 in range(0, NS, spb):
            b = Blk()
            b.kind = "s"
            b.nseg = min(spb, NS - k0)
            b.L = LS
            b.TT = b.nseg * LS // 128
            b.first, b.last = True, True
            b.seqs = list(range(k0, k0 + b.nseg))
            b.xrows = [xs[k0 * LS + t * 128:k0 * LS + (t + 1) * 128, :] for t in range(b.TT)]
            b.yrows = [ys[k0 * LS + t * 128:k0 * LS + (t + 1) * 128, :] for t in range(b.TT)]
            blks.append(b)
        for i, b in enumerate(blks):
            b.bi = i
            b.buf = i % 2
        return blks

    blocks = make_blocks()
    import os as _os
    if _os.environ.get('KDBG_BLK'):
        blocks = [blocks[int(t)] for t in _os.environ['KDBG_BLK'].split(',')]
        for i_, b_ in enumerate(blocks):
            b_.bi = i_
            b_.buf = i_ % 2

    def aview(buf, c, b, lo, n, hist):
        w = hist + b.L
        v = buf[:, c, 0:b.nseg * w].rearrange("p (s l) -> p s l", s=b.nseg)
        return v[:, :, lo:lo + n]

    def nview(ap2d, b):
        return ap2d.rearrange("p (s l) -> p s l", s=b.nseg)

    def load_x(b):
        for tt in range(b.TT):
            P.op("sp", lambda e, tt=tt: e.dma_start(out=x_t[b.buf][:, tt, :], in_=b.xrows[tt]),
                 writes=[("x", b.buf, tt)], dma_slot=("x", b.buf, tt))

    def phase_hist(b):
        if not b.first:
            return
        if b.kind == "p":
            for c in range(4):
                P.op("pool", lambda e, c=c: e.memset(aview(a_ext, c, b, 0, KA - 1, KA - 1), 0.0), writes=[("ahist", c)])
                P.op("pool", lambda e, c=c: e.memset(aview(cv_ext, c, b, 0, KB - 1, KB - 1), 0.0), writes=[("bhist", c)])
            return
        s0, ns = b.seqs[0], b.nseg
        for (cache, K1, buf, hname) in ((cache_a, KA - 1, a_ext, "ahist"), (cache_b, KB - 1, cv_ext, "bhist")):
            stg, stg_r = new_tmp()
            rows = ns * K1
            P.op("sp", lambda e, stg=stg, cache=cache, rows=rows: e.dma_start(
                out=stg[0:rows, :], in_=cache[s0:s0 + ns].rearrange("s t c -> (s t) c")),
                writes=[stg_r], dma_slot=("cst", hname))
            tb1 = PS.alloc1()

            def _tr(e, stg=stg, rows=rows, tb1=tb1):
                ins = None
                for c in range(4):
                    ins = e.transpose(out=bank(tb1)[:, c * 128:c * 128 + rows], in_=stg[0:rows, c * 128:(c + 1) * 128],
                                      identity=ident_f[0:rows, 0:rows])
                return ins
            P.op("pe", _tr, reads=[stg_r, "ident_f"], writes=[("ps", tb1)])
            for c in range(4):
                P.op("dve", lambda e, c=c, tb1=tb1, rows=rows, K1=K1, buf=buf: e.tensor_copy(
                    out=aview(buf, c, b, 0, K1, K1),
                    in_=bank(tb1)[:, c * 128:c * 128 + rows].rearrange("p (s k) -> p s k", k=K1)),
                    reads=[("ps", tb1)], writes=[(hname, c)])
            PS.release(tb1)

    def phase0_elem(b):
        xs_ = []
        ss4, ss4_r = new_st4()
        for tt in range(b.TT):
            xn, xn_r = xn_t[tt], ("xn", tt)
            P.op("act", lambda e, tt=tt, xn=xn: e.activation(out=xn[:], in_=x_t[b.buf][:, tt, :], func=AF.Square,
                                                            accum_out=ss4[:, tt:tt + 1]),
                 reads=[("x", b.buf, tt)], writes=[ss4_r, xn_r])
            xs_.append((xn, xn_r))
        r4, r4_r = rstd4_ops(ss4, ss4_r, b.TT, 1.0 / D)
        for tt in range(b.TT):
            xn, xn_r = xs_[tt]
            P.op("dve", lambda e, tt=tt, xn=xn: e.scalar_tensor_tensor(
                out=xn[:], in0=x_t[b.buf][:, tt, :], scalar=r4[:, tt:tt + 1], in1=gb_t[0][:], op0=ALU.mult, op1=ALU.mult),
                reads=[("x", b.buf, tt), r4_r, ("gb", 0)], writes=[xn_r])
        b.xn = xs_

    def phase0_tr(b):
        for tt in range(b.TT):
            norm_tr(b.xn[tt][0], b.xn[tt][1], xnT, "xnT", tt)

    def win_mm(b, slot, slot_r, q):
        N = b.TT * 128
        bk = PS.alloc1()
        wv = slot.rearrange("p (q dc e) -> p q dc e", q=2, dc=NDC)

        def _mm(e):
            ins = None
            for dc in range(NDC):
                ins = e.matmul(out=bank(bk, N), lhsT=wv[:, q, dc, :], rhs=xnT[:, dc, 0:N], start=(dc == 0), stop=(dc == NDC - 1))
            return ins
        P.op("pe", _mm, reads=[slot_r] + [("xnT", t) for t in range(b.TT)], writes=[("ps", bk)])
        return bk

    def _get_piece(b, i, pin=False):
        if not hasattr(b, "pieces"):
            b.pieces = {}
        if i not in b.pieces:
            b.pieces[i] = wload(ws_in[i], res_ws["in"], pin=pin)

    def _get_chunk(b, k):
        i, q = divmod(k, 2)
        _get_piece(b, i)
        slot, slot_r = b.pieces[i]
        if q == 1:
            ring_pinned.discard(slot_r[1])
        return win_mm(b, slot, slot_r, q)

    def phaseA1p1(b):
        N = b.TT * 128
        get_chunk = lambda k: _get_chunk(b, k)
        for c in range(4):
            bv = get_chunk(2 * c)
            bg = get_chunk(2 * c + 1)
            sg, sg_r = new_tmp()
            P.op("act", lambda e, bg=bg, sg=sg: e.activation(out=sg[:, 0:N], in_=bank(bg, N), func=AF.Sigmoid),
                 reads=[("ps", bg)], writes=[sg_r])
            P.op("dve", lambda e, bv=bv, sg=sg, c=c: e.tensor_tensor(
                out=aview(a_ext, c, b, KA - 1, b.L, KA - 1), in0=nview(bank(bv, N), b), in1=nview(sg[:, 0:N], b), op=ALU.mult),
                reads=[("ps", bv), sg_r], writes=[("abody", c)])
            PS.release(bv, bg)
            yield

    def phaseA1p2(b):
        N = b.TT * 128
        get_chunk = lambda k: _get_chunk(b, k)
        for c in range(4):
            bgc = get_chunk(8 + 3 * c)
            bvv = get_chunk(8 + 3 * c + 1)
            bgb = get_chunk(8 + 3 * c + 2)
            vs, vs_r = new_tmp()
            P.op("act", lambda e, bvv=bvv, vs=vs: e.activation(out=vs[:, 0:N], in_=bank(bvv, N), func=AF.Copy),
                 reads=[("ps", bvv)], writes=[vs_r])
            P.op("dve", lambda e, bgc=bgc, vs=vs, c=c: e.tensor_tensor(
                out=aview(cv_ext, c, b, KB - 1, b.L, KB - 1), in0=nview(bank(bgc, N), b), in1=nview(vs[:, 0:N], b), op=ALU.mult),
                reads=[("ps", bgc), vs_r], writes=[("bbody", c)])
            u, u_r = new_tmp()
            P.op("dve", lambda e, u=u, c=c: e.tensor_scalar(
                out=nview(u[:, 0:N], b), in0=aview(cv_ext, c, b, 0, b.L, KB - 1), scalar1=pT[:, c, 34:35], scalar2=None, op0=ALU.mult),
                reads=[("bbody", c), ("bhist", c), "pT"], writes=[u_r])
            for k in range(1, KB):
                P.op("dve", lambda e, u=u, c=c, k=k: e.scalar_tensor_tensor(
                    out=nview(u[:, 0:N], b), in0=aview(cv_ext, c, b, k, b.L, KB - 1), scalar=pT[:, c, 34 + k:35 + k],
                    in1=nview(u[:, 0:N], b), op0=ALU.mult, op1=ALU.add),
                    reads=[("bbody", c), ("bhist", c), "pT", u_r], writes=[u_r])
            P.op("dve", lambda e, u=u, bgb=bgb, c=c: e.tensor_tensor(
                out=mix(b, 4 + c)[:, 0:N], in0=bank(bgb, N), in1=u[:, 0:N], op=ALU.mult),
                reads=[("ps", bgb), u_r], writes=[mix_r(b, 4 + c)])
            PS.release(bgc, bvv, bgb)
            yield

    def phase_conv(b):
        N = b.TT * 128
        acv = [nview(ac_t[:, c, 0:N], b) for c in range(4)]
        for c in range(4):
            P.op("dve", lambda e, c=c: e.tensor_scalar(
                out=acv[c], in0=aview(a_ext, c, b, 0, b.L, KA - 1), scalar1=pT[:, c, 0:1], scalar2=pT[:, c, 31:32],
                op0=ALU.mult, op1=ALU.add),
                reads=[("abody", c), ("ahist", c), "pT"], writes=[("ac", c)])
        yield
        for k in range(1, KA):
            for c in range(4):
                P.op("dve", lambda e, c=c, k=k: e.scalar_tensor_tensor(
                    out=acv[c], in0=aview(a_ext, c, b, k, b.L, KA - 1), scalar=pT[:, c, k:k + 1], in1=acv[c],
                    op0=ALU.mult, op1=ALU.add),
                    reads=[("abody", c), ("ahist", c), "pT", ("ac", c)], writes=[("ac", c)])
            yield

    def phase_tail(b, which):
        buf, K1, body, hist = (a_ext, KA - 1, "abody", "ahist") if which == "a" else (cv_ext, KB - 1, "bbody", "bhist")
        if b.last:
            if which == "a":
                dst = na_p if b.kind == "p" else na_s
            else:
                dst = nb_p if b.kind == "p" else nb_s
            for si, s in enumerate(b.seqs):
                tb1 = PS.alloc1()

                def _t1(e, si=si, tb1=tb1):
                    ins = None
                    for c in range(4):
                        ins = e.transpose(out=bank(tb1)[0:K1, c * 128:(c + 1) * 128],
                                          in_=aview(buf, c, b, b.L, K1, K1)[:, si, :], identity=ident_f[:])
                    return ins
                P.op("pe", _t1, reads=[(body, c) for c in range(4)] + [(hist, c) for c in range(4)] + ["ident_f"],
                     writes=[("ps", tb1)])
                og, og_r = new_tmp()
                P.op("act", lambda e, tb1=tb1, og=og: e.activation(out=og[0:K1, :], in_=bank(tb1)[0:K1, :], func=AF.Copy),
                     reads=[("ps", tb1)], writes=[og_r])
                PS.release(tb1)
                P.op("sp", lambda e, s=s, og=og: e.dma_start(out=dst[s, :, :], in_=og[0:K1, :]),
                     reads=[og_r], writes=[("out", which, b.kind, s)], dma_slot=("o" + which, si))
        else:
            for c in range(4):
                P.op("pool", lambda e, c=c: e.tensor_copy(out=buf[:, c, 0:K1], in_=buf[:, c, b.L:b.L + K1]),
                     reads=[(body, c), (hist, c)], writes=[(hist, c)])

    def phaseA2_stats(b):
        N = b.TT * 128
        bm, be = PS.alloc1(), PS.alloc1()
        b.bm, b.be = bm, be
        sqs = []
        for c in range(4):
            sq, sq_r = new_tmp()
            P.op("act", lambda e, c=c, sq=sq: e.activation(out=sq[:, 0:N], in_=ac_t[:, c, 0:N], func=AF.Square),
                 reads=[("ac", c)], writes=[sq_r])
            sqs.append((sq, sq_r))

        def _m1(e):
            ins = None
            for c in range(4):
                ins = e.matmul(out=bank(bm, N), lhsT=ones_f[:], rhs=ac_t[:, c, 0:N], start=(c == 0), stop=(c == 3))
            return ins
        P.op("pe", _m1, reads=[("ac", c) for c in range(4)] + ["ones_f"], writes=[("ps", bm)])

        def _m2(e):
            ins = None
            for c in range(4):
                ins = e.matmul(out=bank(be, N), lhsT=ones_f[:], rhs=sqs[c][0][:, 0:N], start=(c == 0), stop=(c == 3))
            return ins
        P.op("pe", _m2, reads=[s_[1] for s_ in sqs] + ["ones_f"], writes=[("ps", be)])

    def phaseA2_norm(b):
        N = b.TT * 128
        bm, be = b.bm, b.be
        msq, msq_r = new_tmp()
        P.op("act", lambda e: e.activation(out=msq[:, 0:N], in_=bank(bm, N), func=AF.Square), reads=[("ps", bm)], writes=[msq_r])
        var, var_r = new_tmp()
        P.op("dve", lambda e: e.tensor_tensor(out=var[:, 0:N], in0=bank(be, N), in1=msq[:, 0:N], op=ALU.subtract),
             reads=[("ps", be), msq_r], writes=[var_r])
        P.op("act", lambda e: e.activation(out=var[:, 0:N], in_=var[:, 0:N], func=AF.Ln, bias=eps_t[:, 0:1]),
             reads=[var_r, "eps_t"], writes=[var_r])
        rs, rs_r = new_tmp(pin=True)
        P.op("act", lambda e: e.activation(out=rs[:, 0:N], in_=var[:, 0:N], func=AF.Exp, scale=-0.5), reads=[var_r], writes=[rs_r])
        PS.release(be)
        for c in range(4):
            z, z_r = new_tmp()
            P.op("dve", lambda e, c=c, z=z: e.tensor_tensor(out=z[:, 0:N], in0=ac_t[:, c, 0:N], in1=bank(bm, N), op=ALU.subtract),
                 reads=[("ac", c), ("ps", bm)], writes=[z_r])
            P.op("dve", lambda e, z=z: e.tensor_tensor(out=z[:, 0:N], in0=z[:, 0:N], in1=rs[:, 0:N], op=ALU.mult),
                 reads=[z_r, rs_r], writes=[z_r])
            P.op("act", lambda e, c=c, z=z: e.activation(out=mix(b, c)[:, 0:N], in_=z[:, 0:N], func=AF.Silu,
                                                        scale=pT[:, c, 32:33], bias=pT[:, c, 33:34]),
                 reads=[z_r, "pT"], writes=[("mixT", c)])
            if c == 3:
                PS.release(bm)
                tmp_pinned.discard(rs_r[1])
            yield

    def phaseB(b):
        slots = b.wout_slots
        b.hn = [(hn_t[tt], ("hn", tt)) for tt in range(b.TT)]
        pend = {}

        def stage1(tt):
            pb = PS.alloc_pair()

            def _mm(e, tt=tt, pb=pb):
                ins = None
                for half in range(2):
                    for cc in range(NDC):
                        sl = slots[half * 2 + cc // 4][0].rearrange("p (cc d) -> p cc d", cc=4)
                        ins = e.matmul(out=bank(pb + half), lhsT=mix(b, cc)[:, tt * 128:(tt + 1) * 128], rhs=sl[:, cc % 4, :],
                                       start=(cc == 0), stop=(cc == NDC - 1))
                return ins
            P.op("pe", _mm, reads=[s_[1] for s_ in slots] + [mix_r(b, cc) for cc in range(NDC)],
                 writes=[("ps", pb), ("ps", pb + 1)])
            ss, ss_r = new_st()
            P.op("act", lambda e, pb=pb, ss=ss, tt=tt: e.activation(out=hn_t[tt][:], in_=bank(pb, 1024), func=AF.Square, accum_out=ss),
                 reads=[("ps", pb), ("ps", pb + 1)], writes=[ss_r, ("hn", tt)])
            r, r_r = rstd_ops(ss, ss_r, 1.0 / D)
            P.op("dve", lambda e, tt=tt, pb=pb, r=r: e.scalar_tensor_tensor(
                out=h_t[:, tt, :], in0=bank(pb, 1024), scalar=r, in1=gb_t[1][:], op0=ALU.mult, op1=ALU.mult),
                reads=[("ps", pb), ("ps", pb + 1), r_r, ("gb", 1)], writes=[("h", tt)])
            PS.release(pb, pb + 1)
            P.op("dve", lambda e, tt=tt: e.tensor_tensor(out=h_t[:, tt, :], in0=h_t[:, tt, :], in1=x_t[b.buf][:, tt, :], op=ALU.add),
                 reads=[("h", tt), ("x", b.buf, tt)], writes=[("h", tt)])
            ss2, ss2_r = new_st()
            P.op("act", lambda e, tt=tt, ss2=ss2: e.activation(out=hn_t[tt][:], in_=h_t[:, tt, :], func=AF.Square, accum_out=ss2),
                 reads=[("h", tt)], writes=[ss2_r, ("hn", tt)])
            pend[tt] = rstd_ops(ss2, ss2_r, 1.0 / D)

        def stage2(tt):
            r2, r2_r = pend[tt]
            P.op("dve", lambda e, tt=tt, r2=r2: e.scalar_tensor_tensor(
                out=hn_t[tt][:], in0=h_t[:, tt, :], scalar=r2, in1=gb_t[2][:], op0=ALU.mult, op1=ALU.mult),
                reads=[("h", tt), r2_r, ("gb", 2)], writes=[("hn", tt)])

        for tt in range(b.TT):
            stage1(tt)
            if tt == b.TT - 1:
                for s_ in slots:
                    ring_pinned.discard(s_[1][1])
            if tt > 0:
                stage2(tt - 1)
            yield
        stage2(b.TT - 1)
        yield

    def preload_wout(b):
        b.wout_slots = [wload(ws_out[i], res_ws["out"], pin=True) for i in range(4)]

    def phaseC(b):
        for tt in range(b.TT):
            norm_tr(b.hn[tt][0], b.hn[tt][1], hnT, "hnT", tt)

    def phaseD(b):
        N = b.TT * 128
        for j in range(NFC):
            slot, slot_r = wload(ws_gu[j], res_ws["gu"])
            wv = slot.rearrange("p (gu dc e) -> p gu dc e", gu=2, dc=NDC)
            pb = PS.alloc_pair()

            def _mm(e, wv=wv, pb=pb):
                ins = None
                for gu in range(2):
                    for dc in range(NDC):
                        ins = e.matmul(out=bank(pb + gu, N), lhsT=wv[:, gu, dc, :], rhs=hnT[:, dc, 0:N],
                                       start=(dc == 0), stop=(dc == NDC - 1))
                return ins
            P.op("pe", _mm, reads=[slot_r] + [("hnT", t) for t in range(b.TT)], writes=[("ps", pb), ("ps", pb + 1)])
            sg, sg_r = new_tmp()
            P.op("act", lambda e, pb=pb, sg=sg: e.activation(out=sg[:, 0:N], in_=bank(pb, N), func=AF.Silu),
                 reads=[("ps", pb)], writes=[sg_r])
            P.op("dve", lambda e, pb=pb, sg=sg, j=j: e.tensor_tensor(out=actT[:, j, 0:N], in0=bank(pb + 1, N), in1=sg[:, 0:N], op=ALU.mult),
                 reads=[("ps", pb + 1), sg_r], writes=[("actT", j)])
            PS.release(pb, pb + 1)
            yield

    def phaseE_half(b, half):
        banks = [PS.alloc1() for _ in range(b.TT)]
        for q in range(6):
            nfc = 4 if q < 5 else 2
            slot, slot_r = wload(ws_dn[half * 6 + q], res_ws["dn"], cols=nfc * 512)
            sl = slot.rearrange("p (fc d) -> p fc d", fc=4)

            def _mm(e, q=q, nfc=nfc, sl=sl):
                ins = None
                for f in range(nfc):
                    fc = 4 * q + f
                    for tt in range(b.TT):
                        ins = e.matmul(out=bank(banks[tt]), lhsT=actT[:, fc, tt * 128:(tt + 1) * 128], rhs=sl[:, f, :],
                                       start=(fc == 0), stop=(fc == NFC - 1))
                return ins
            P.op("pe", _mm, reads=[slot_r] + [("actT", 4 * q + f) for f in range(nfc)],
                 writes=[("ps", bk) for bk in banks])
            yield
        if half == 0:
            ss0, ss0_r = new_st4()
            for tt in range(b.TT):
                jk, jk_r = new_tmp()
                P.op("act", lambda e, tt=tt, jk=jk: e.activation(out=jk[:], in_=bank(banks[tt]), func=AF.Square,
                                                                accum_out=ss0[:, tt:tt + 1]),
                     reads=[("ps", banks[tt])], writes=[ss0_r, jk_r])
                P.op("act", lambda e, tt=tt: e.activation(out=cv_ext[:, tt, 2:514], in_=bank(banks[tt]), func=AF.Copy),
                     reads=[("ps", banks[tt]), ss0_r], writes=[("bbody", tt)])
            b.ss0 = (ss0, ss0_r)
            PS.release(*banks)
        else:
            b.ebanks = banks
        yield

    def phaseE_epi(b):
        banks = b.ebanks
        ss1, ss1_r = new_st4()
        for tt in range(b.TT):
            jk, jk_r = new_tmp()
            P.op("act", lambda e, tt=tt, jk=jk: e.activation(out=jk[:], in_=bank(banks[tt]), func=AF.Square,
                                                            accum_out=ss1[:, tt:tt + 1]),
                 reads=[("ps", banks[tt])], writes=[ss1_r, jk_r])
        r4, r4_r = rstd4_ops(ss1, ss1_r, b.TT, 1.0 / D, add=b.ss0)
        for tt in range(b.TT):
            r = r4[:, tt:tt + 1]
            for half in range(2):
                t, t_r = new_tmp()
                if half == 0:
                    P.op("dve", lambda e, tt=tt, t=t, r=r: e.scalar_tensor_tensor(
                        out=t[:], in0=cv_ext[:, tt, 2:514], scalar=r, in1=gb_t[3][:, 0:512], op0=ALU.mult, op1=ALU.mult),
                        reads=[("bbody", tt), r4_r, ("gb", 3)], writes=[t_r])
                else:
                    P.op("dve", lambda e, tt=tt, t=t, r=r: e.scalar_tensor_tensor(
                        out=t[:], in0=bank(banks[tt]), scalar=r, in1=gb_t[3][:, 512:1024], op0=ALU.mult, op1=ALU.mult),
                        reads=[("ps", banks[tt]), r4_r, ("gb", 3)], writes=[t_r])
                P.op("dve", lambda e, tt=tt, half=half, t=t: e.tensor_tensor(
                    out=h_t[:, tt, half * 512:(half + 1) * 512], in0=h_t[:, tt, half * 512:(half + 1) * 512], in1=t[:], op=ALU.add),
                    reads=[("h", tt), t_r], writes=[("h", tt)])
            PS.release(banks[tt])
            P.op("sp", lambda e, tt=tt: e.dma_start(out=b.yrows[tt], in_=h_t[:, tt, :]),
                 reads=[("h", tt)], writes=[("out", "y", b.bi, tt)], dma_slot=("y", tt))

    def run(*gens):
        for g in gens:
            if g is not None:
                for _ in g:
                    pass

    def chain(*gens):
        for g in gens:
            if g is not None:
                yield from g

    def interleave(ga, gb, on_b_done=None):
        da = db = False
        while not (da and db):
            if not da:
                try:
                    next(ga)
                except StopIteration:
                    da = True
            if not db:
                try:
                    next(gb)
                except StopIteration:
                    db = True
                    if on_b_done is not None:
                        on_b_done()

    nblk = len(blocks)
    import os as _os
    _stop = int(_os.environ.get("KDBG_STOP", "999"))

    def driver():
        load_x(blocks[0])
        if nblk > 1:
            load_x(blocks[1])
        b0 = blocks[0]
        phase0_elem(b0)
        phase0_tr(b0)
        phase_hist(b0)
        run(phaseA1p1(b0), phaseA1p2(b0))
        phase_tail(b0, "b")
        if nblk > 1:
            phase0_elem(blocks[1])
        prep_gu_dn()
        for i in range(nblk + 1):
            b = blocks[i] if i < nblk else None
            pb_ = blocks[i - 1] if i > 0 else None
            nb_ = blocks[i + 1] if i + 1 < nblk else None
            if pb_ is not None:
                phaseC(pb_)
            ffn = chain(phaseD(pb_), phaseE_half(pb_, 0), phaseE_half(pb_, 1)) if pb_ is not None else iter(())
            conv = phase_conv(b) if b is not None else iter(())

            def after_conv(b=b):
                if b is not None:
                    phase_tail(b, "a")
                    phaseA2_stats(b)
                    run(phaseA2_norm(b))
            interleave(ffn, conv, on_b_done=after_conv)
            if b is not None:
                preload_wout(b)
            if nb_ is not None:
                _get_piece(nb_, 0, pin=True)
                _get_piece(nb_, 1, pin=True)
            if pb_ is not None:
                phaseE_epi(pb_)
            if b is None:
                break
            if nb_ is not None:
                phase0_tr(nb_)
                phase_hist(nb_)
            interleave(phaseB(b), phaseA1p1(nb_) if nb_ is not None else iter(()))
            if nb_ is not None:
                run(phaseA1p2(nb_))
                phase_tail(nb_, "b")
            if i + 2 < nblk:
                load_x(blocks[i + 2])
                phase0_elem(blocks[i + 2])

    driver()

    out_res = [r for r in P.lastw if isinstance(r, tuple) and r and r[0] == "out"]
    P.op("sp", lambda e: e.nop(), reads=out_res)

    P.finalize()
    dma_slots = list(P.dma_cnt.keys())
    esem = {e: es.enter_context(nc.semaphore(f"sem_{e}")) for e in Prog.ENGS}
    dsem = {s: es.enter_context(nc.semaphore(f"dsem_{i}")) for i, s in enumerate(dma_slots)}
    with nc.Block() as block:
        @block.tensor
        def _(e):
            P.emit_engine("pe", e, esem, dsem)

        @block.scalar
        def _(e):
            P.emit_engine("act", e, esem, dsem)

        @block.vector
        def _(e):
            P.emit_engine("dve", e, esem, dsem)

        @block.gpsimd
        def _(e):
            P.emit_engine("pool", e, esem, dsem)

        @block.sync
        def _(e):
            P.emit_engine("sp", e, esem, dsem)
    es.close()
    return nc


def make_in_maps(n_cores, inputs, NP, NS):
    f = lambda a: np.ascontiguousarray(np.asarray(a, dtype=np.float32))
    ident = np.eye(128, dtype=np.float32)
    maps = []
    for c in range(n_cores):
        m = {
            "x_p": f(inputs["x_prompt"][c * NP:(c + 1) * NP]),
            "x_s": f(inputs["x_sample"][c * NS:(c + 1) * NS]),
            "cache_a": f(inputs["cache_conv_a"][0, c * NS:(c + 1) * NS]),
            "cache_b": f(inputs["cache_conv_b"][0, c * NS:(c + 1) * NS]),
            "g_pre1": f(inputs["norm_mix_pre"]),
            "g_post1": f(inputs["norm_mix_post"]),
            "g_pre2": f(inputs["norm_ffn_pre"]),
            "g_post2": f(inputs["norm_ffn_post"]),
            "w_in": f(inputs["w_in"][0]),
            "w_out": f(inputs["w_out"][0]),
            "w_gu": f(inputs["w_gate_up"][0]),
            "w_dn": f(inputs["w_down"][0]),
            "conv_a_w": f(inputs["conv_a_w"][0]),
            "conv_a_b": f(inputs["conv_a_b"]),
            "ln_g": f(inputs["conv_a_ln_g"]),
            "ln_b": f(inputs["conv_a_ln_b"]),
            "conv_b_w": f(inputs["conv_b_w"][0]),
            "ident": ident,
        }
        maps.append(m)
    return maps


def gather(results, n_cores):
    cat = lambda k: np.concatenate([np.asarray(results[c][k], dtype=np.float32) for c in range(n_cores)], axis=0)
    return (cat("y_p"), cat("y_s"), cat("na_p")[None], cat("nb_p")[None], cat("na_s")[None], cat("nb_s")[None])


def kernel(**inputs):
    B, SEQ = inputs["x_prompt"].shape[0], inputs["x_prompt"].shape[1]
    BS, LS = inputs["x_sample"].shape[0], inputs["x_sample"].shape[1]
    NP, NS = B // N_CORES, BS // N_CORES
    nc = build_program(NP, SEQ, NS, LS)
    in_maps = make_in_maps(N_CORES, inputs, NP, NS)
    res = run_bass_kernel_spmd(nc, in_maps, core_ids=list(range(N_CORES)))
    return gather(res.results, N_CORES)
```

```python
import numpy as np
import concourse.bass as bass
import concourse.mybir as mybir
from concourse.bass_utils import run_bass_kernel_spmd

F32 = mybir.dt.float32
BF16 = mybir.dt.bfloat16
AF = mybir.ActivationFunctionType
ALU = mybir.AluOpType

D = 1024
DA = 512
DB = 512
DIN = 2560
DFF = 2816
KA = 31
KB = 3
EPS = 1e-6
NDC = 8
NFC = 22
NPAR = 37
RING = 8
NTMP = 9
N_CORES = 8

E_ORDER = [0, 4, 1, 5, 2, 6, 3, 7, 12, 16, 8, 13, 17, 9, 14, 18, 10, 15, 19, 11]


class _Op:
    __slots__ = ("eng", "fn", "deps", "idx", "signal", "dma_slot", "dma_val", "waits", "sigval")


class Prog:
    ENGS = ("pe", "act", "dve", "pool", "sp")

    def __init__(self):
        self.ops = {e: [] for e in self.ENGS}
        self.lastw = {}
        self.readers = {}
        self.dma_cnt = {}

    def op(self, eng, fn, reads=(), writes=(), dma_slot=None):
        o = _Op()
        o.eng = eng
        o.fn = fn
        o.idx = len(self.ops[eng])
        o.signal = False
        deps = []
        for r in reads:
            w = self.lastw.get(r)
            if w is not None:
                deps.append(w)
        for r in writes:
            w = self.lastw.get(r)
            if w is not None:
                deps.append(w)
            deps.extend(self.readers.get(r, ()))
        o.deps = deps
        if dma_slot is not None:
            c = self.dma_cnt.get(dma_slot, 0) + 1
            self.dma_cnt[dma_slot] = c
            o.dma_slot = dma_slot
            o.dma_val = 16 * c
            tok = ("dma", dma_slot, 16 * c)
        else:
            o.dma_slot = None
            o.dma_val = 0
            tok = ("eng", eng, o.idx)
        for r in reads:
            self.readers.setdefault(r, []).append(tok)
        for r in writes:
            self.lastw[r] = tok
            self.readers[r] = []
        self.ops[eng].append(o)
        return o

    def finalize(self):
        for e in self.ENGS:
            seen = {}
            for o in self.ops[e]:
                need_eng = {}
                need_dma = {}
                for d in o.deps:
                    if d[0] == "eng":
                        _, de, di = d
                        if de == e:
                            if e in ("pe", "sp"):
                                continue
                        if seen.get(de, -1) >= di:
                            continue
                        if need_eng.get(de, -1) < di:
                            need_eng[de] = di
                    else:
                        _, slot, val = d
                        if seen.get(("dma", slot), 0) >= val:
                            continue
                        if need_dma.get(slot, 0) < val:
                            need_dma[slot] = val
                waits = []
                for de, di in need_eng.items():
                    seen[de] = di
                    waits.append(("eng", de, di))
                    self.ops[de][di].signal = True
                for slot, val in need_dma.items():
                    seen[("dma", slot)] = val
                    waits.append(("dma", slot, val))
                o.waits = waits
        for e in self.ENGS:
            c = 0
            for o in self.ops[e]:
                if o.signal:
                    c += 1
                o.sigval = c

    def emit_engine(self, e, eng, esem, dsem):
        for o in self.ops[e]:
            for w in o.waits:
                if w[0] == "eng":
                    eng.wait_ge(esem[w[1]], self.ops[w[1]][w[2]].sigval)
                else:
                    eng.wait_ge(dsem[w[1]], w[2])
            ins = o.fn(eng)
            if o.dma_slot is not None:
                ins.then_inc(dsem[o.dma_slot], 16)
            elif o.signal:
                ins.then_inc(esem[e], 1)


class _Psum:
    def __init__(self):
        self.held = [False] * 8
        self.stamp = [0] * 8
        self.clock = 0

    def alloc1(self):
        free = [b for b in range(8) if not self.held[b]]
        assert free, "out of PSUM banks"
        b = min(free, key=lambda k: self.stamp[k])
        self.held[b] = True
        return b

    def alloc_pair(self):
        free = [b for b in range(0, 8, 2) if not self.held[b] and not self.held[b + 1]]
        assert free, "out of PSUM bank pairs"
        b = min(free, key=lambda k: max(self.stamp[k], self.stamp[k + 1]))
        self.held[b] = self.held[b + 1] = True
        return b

    def release(self, *banks):
        for b in banks:
            assert self.held[b]
            self.held[b] = False
            self.clock += 1
            self.stamp[b] = self.clock


def build_program(NP, SEQ, NS, LS=64):
    assert SEQ % 512 == 0 and (NS * LS) % 128 == 0 and NS * (KA - 1) <= 128
    nc = bass.Bass("TRN2", target_bir_lowering=False)
    P = Prog()
    PS = _Psum()

    def din(name, shape, dt=F32):
        return nc.dram_tensor(name, list(shape), dt, kind="ExternalInput").ap()

    def dout(name, shape, dt=F32):
        return nc.dram_tensor(name, list(shape), dt, kind="ExternalOutput").ap()

    x_p = din("x_p", [NP, SEQ, D])
    x_s = din("x_s", [NS, LS, D])
    cache_a = din("cache_a", [NS, KA - 1, DA])
    cache_b = din("cache_b", [NS, KB - 1, DB])
    g_pre1 = din("g_pre1", [1, D])
    g_post1 = din("g_post1", [1, D])
    g_pre2 = din("g_pre2", [1, D])
    g_post2 = din("g_post2", [1, D])
    w_in = din("w_in", [D, DIN])
    w_out = din("w_out", [D, D])
    w_gu = din("w_gu", [D, 2 * DFF])
    w_dn = din("w_dn", [DFF, D])
    conv_a_w = din("conv_a_w", [KA, DA])
    conv_a_b = din("conv_a_b", [1, DA])
    ln_g = din("ln_g", [1, DA])
    ln_b = din("ln_b", [1, DA])
    conv_b_w = din("conv_b_w", [KB, DB])
    ident_in = din("ident", [128, 128])

    y_p = dout("y_p", [NP, SEQ, D])
    y_s = dout("y_s", [NS, LS, D])
    na_p = dout("na_p", [NP, KA - 1, DA])
    nb_p = dout("nb_p", [NP, KB - 1, DB])
    na_s = dout("na_s", [NS, KA - 1, DA])
    nb_s = dout("nb_s", [NS, KB - 1, DB])

    ws_in = nc.dram_tensor("ws_in", [10, 128, 2048], BF16, kind="Internal").ap()
    ws_out = nc.dram_tensor("ws_out", [4, 128, 2048], BF16, kind="Internal").ap()
    ws_gu = nc.dram_tensor("ws_gu", [NFC, 128, 2048], BF16, kind="Internal").ap()
    ws_dn = nc.dram_tensor("ws_dn", [12, 128, 2048], BF16, kind="Internal").ap()

    from contextlib import ExitStack
    es = ExitStack()

    def sb(name, shape, dt=F32):
        return es.enter_context(nc.sbuf_tensor(name, list(shape), dt))

    ident_f = sb("ident_f", [128, 128])
    ident_b = sb("ident_b", [128, 128], BF16)
    ones_f = sb("ones_f", [128, 128])
    neghalf = sb("neghalf", [128, 1])
    eps_t = sb("eps_t", [128, 1])
    gb_t = [sb(f"gb{i}", [128, D]) for i in range(4)]
    pT = sb("pT", [128, 4, NPAR])
    x_t = [sb(f"x{i}", [128, 4, D]) for i in range(2)]
    h_t = sb("h", [128, 4, D])
    xn_t = [sb(f"xn{i}", [128, D], BF16) for i in range(4)]
    hn_t = [sb(f"hn{i}", [128, D], BF16) for i in range(4)]
    xnT = sb("xnT", [128, NDC, 512], BF16)
    hnT = sb("hnT", [128, NDC, 512], BF16)
    a_ext = sb("a_ext", [128, 4, 544])
    cv_ext = sb("cv_ext", [128, 4, 516])
    ac_t = sb("ac", [128, 4, 512])
    tmp_t = [sb(f"tmp{i}", [128, 512]) for i in range(NTMP)]
    mixT_lo = sb("mixT_lo", [128, 4, 512], BF16)
    mixT_hi = [sb(f"mixT_hi{i}", [128, 4, 512], BF16) for i in range(2)]

    def mix(b, cc):
        return mixT_lo[:, cc] if cc < 4 else mixT_hi[b.buf][:, cc - 4]

    def mix_r(b, cc):
        return ("mixT", cc) if cc < 4 else ("mixT", cc, b.buf)
    actT = sb("actT", [128, NFC, 512], BF16)
    ring = [sb(f"ring{i}", [128, 2048], BF16) for i in range(RING)]
    st = sb("st", [128, 64])
    st4 = sb("st4", [128, 64])
    ps = es.enter_context(nc.psum_tensor("ps", [128, 4096], F32))
    print("SBUF bytes remaining per partition:", nc.sbuf_bytes_remaining)

    def bank(b, n=512):
        return ps[:, b * 512:b * 512 + n]

    cnt = {"tmp": 0, "st": 0, "ring": 0, "st4": 0}

    tmp_pinned = set()

    def new_tmp(pin=False):
        while True:
            i = cnt["tmp"] % NTMP
            cnt["tmp"] += 1
            if i not in tmp_pinned:
                break
        if pin:
            tmp_pinned.add(i)
        return tmp_t[i], ("tmp", i)

    def new_st():
        i = cnt["st"] % 64
        cnt["st"] += 1
        return st[:, i:i + 1], ("st", i)

    def new_st4():
        i = cnt["st4"] % 16
        cnt["st4"] += 1
        return st4[:, 4 * i:4 * i + 4], ("st4", i)

    ring_pinned = set()

    def wload(src_ap, src_res, cols=2048, pin=False):
        while True:
            i = cnt["ring"] % RING
            cnt["ring"] += 1
            if i not in ring_pinned:
                break
        if pin:
            ring_pinned.add(i)
        dst = ring[i]
        P.op("sp", lambda e, dst=dst, src_ap=src_ap, cols=cols: e.dma_start(out=dst[:, 0:cols], in_=src_ap[:, 0:cols]),
             reads=src_res, writes=[("ring", i)], dma_slot=("ring", i))
        return dst, ("ring", i)

    res_ws = {"in": [], "out": [], "gu": [], "dn": []}
    w_in_v = w_in.rearrange("(dc p) (j e) -> p j dc e", p=128, e=128)
    ws_in_v = ws_in.rearrange("i p (q dc e) -> i p q dc e", q=2, dc=NDC)
    w_out_v = w_out.rearrange("(cc p) d -> p cc d", p=128)
    ws_out_v = ws_out.rearrange("i p (cc d) -> i p cc d", cc=4)
    w_gu_v = w_gu.rearrange("(dc p) (gu j e) -> p gu j dc e", p=128, gu=2, e=128)
    ws_gu_v = ws_gu.rearrange("j p (gu dc e) -> j p gu dc e", gu=2, dc=NDC)
    w_dn_v = w_dn.rearrange("(fc p) d -> p fc d", p=128)
    ws_dn_v = ws_dn.rearrange("i p (fc d) -> i p fc d", fc=4)

    def prep_in_out():
        for k, j in enumerate(E_ORDER):
            i, q = divmod(k, 2)
            r = ("ws", "in", k)
            res_ws["in"].append(r)
            P.op("pool", lambda e, i=i, q=q, j=j: e.dma_start(out=ws_in_v[i, :, q, :, :], in_=w_in_v[:, j, :, :]),
                 writes=[r], dma_slot="prep_in")
        for half in range(2):
            for q in range(2):
                r = ("ws", "out", half * 2 + q)
                res_ws["out"].append(r)
                P.op("pool", lambda e, half=half, q=q: e.dma_start(
                    out=ws_out_v[half * 2 + q], in_=w_out_v[:, 4 * q:4 * q + 4, half * 512:(half + 1) * 512]),
                    writes=[r], dma_slot="prep_out")

    def prep_gu_dn():
        for j in range(NFC):
            for gu in range(2):
                r = ("ws", "gu", j * 2 + gu)
                res_ws["gu"].append(r)
                P.op("pool", lambda e, j=j, gu=gu: e.dma_start(out=ws_gu_v[j, :, gu, :, :], in_=w_gu_v[:, gu, j, :, :]),
                     writes=[r], dma_slot="prep_gu")
        for half in range(2):
            for q in range(6):
                nfc = 4 if q < 5 else 2
                r = ("ws", "dn", half * 6 + q)
                res_ws["dn"].append(r)
                P.op("pool", lambda e, half=half, q=q, nfc=nfc: e.dma_start(
                    out=ws_dn_v[half * 6 + q, :, 0:nfc, :], in_=w_dn_v[:, 4 * q:4 * q + nfc, half * 512:(half + 1) * 512]),
                    writes=[r], dma_slot="prep_dn")

    prep_in_out()

    P.op("sp", lambda e: e.dma_start(out=ident_f[:], in_=ident_in[:, :]), writes=["ident_f"], dma_slot="c0")
    for i, g in enumerate((g_pre1, g_post1, g_pre2, g_post2)):
        P.op("sp", lambda e, i=i, g=g: e.dma_start(out=gb_t[i][:], in_=g[0, :].partition_broadcast(128)),
             writes=[("gb", i)], dma_slot=("c1", i))
    pstage, pstage_r = new_tmp()
    P.op("sp", lambda e: e.dma_start(out=pstage[0:KA, :], in_=conv_a_w[:, :]), writes=[pstage_r], dma_slot="c2")
    P.op("sp", lambda e: e.dma_start(out=pstage[31:32, :], in_=conv_a_b[:, :]), writes=["ps1"], dma_slot="c3")
    P.op("sp", lambda e: e.dma_start(out=pstage[32:33, :], in_=ln_g[:, :]), writes=["ps2"], dma_slot="c4")
    P.op("sp", lambda e: e.dma_start(out=pstage[33:34, :], in_=ln_b[:, :]), writes=["ps3"], dma_slot="c5")
    P.op("sp", lambda e: e.dma_start(out=pstage[34:37, :], in_=conv_b_w[:, :]), writes=["ps4"], dma_slot="c6")
    P.op("dve", lambda e: e.tensor_copy(out=ident_b[:], in_=ident_f[:]), reads=["ident_f"], writes=["ident_b"])
    P.op("dve", lambda e: e.memset(ones_f[:], 1.0 / DA), writes=["ones_f"])
    P.op("dve", lambda e: e.memset(neghalf[:], -0.5), writes=["neghalf"])
    P.op("dve", lambda e: e.memset(eps_t[:], EPS), writes=["eps_t"])

    tb = PS.alloc1()

    def _ptr(e):
        ins = None
        for c in range(4):
            ins = e.transpose(out=bank(tb)[:, c * 64:c * 64 + NPAR], in_=pstage[0:NPAR, c * 128:(c + 1) * 128],
                              identity=ident_f[0:NPAR, 0:NPAR])
        return ins
    P.op("pe", _ptr, reads=[pstage_r, "ps1", "ps2", "ps3", "ps4", "ident_f"], writes=[("ps", tb)])
    P.op("dve", lambda e: e.tensor_copy(out=pT[:], in_=bank(tb).rearrange("p (c k) -> p c k", k=64)[:, 0:4, 0:NPAR]),
         reads=[("ps", tb)], writes=["pT"])
    PS.release(tb)

    def rstd_ops(ss, ss_r, scale):
        t, t_r = new_st()
        r, r_r = new_st()
        P.op("pool", lambda e: e.tensor_scalar(out=t, in0=ss, scalar1=scale, scalar2=EPS, op0=ALU.mult, op1=ALU.add),
             reads=[ss_r], writes=[t_r])
        P.op("pool", lambda e: e.tensor_tensor(out=r, in0=t, in1=neghalf[:, 0:1], op=ALU.pow),
             reads=[t_r, "neghalf"], writes=[r_r])
        return r, r_r

    def rstd4_ops(ss4, ss4_r, n, scale, add=None):
        t, t_r = new_st4()
        r, r_r = new_st4()
        src, src_rs = ss4, [ss4_r]
        if add is not None:
            a4, a4_r = add
            u, u_r = new_st4()
            P.op("pool", lambda e: e.tensor_tensor(out=u[:, 0:n], in0=ss4[:, 0:n], in1=a4[:, 0:n], op=ALU.add),
                 reads=[ss4_r, a4_r], writes=[u_r])
            src, src_rs = u, [u_r]
        P.op("pool", lambda e: e.tensor_scalar(out=t[:, 0:n], in0=src[:, 0:n], scalar1=scale, scalar2=EPS, op0=ALU.mult, op1=ALU.add),
             reads=src_rs, writes=[t_r])
        P.op("pool", lambda e: e.tensor_tensor(out=r[:, 0:n], in0=t[:, 0:n], in1=neghalf[:, 0:1].broadcast_to([128, n]), op=ALU.pow),
             reads=[t_r, "neghalf"], writes=[r_r])
        return r, r_r

    def norm_elem(src, src_r, gi, xn, xn_r):
        ss, ss_r = new_st()
        P.op("act", lambda e: e.activation(out=xn[:], in_=src, func=AF.Square, accum_out=ss),
             reads=[src_r], writes=[ss_r, xn_r])
        r, r_r = rstd_ops(ss, ss_r, 1.0 / D)
        P.op("dve", lambda e: e.scalar_tensor_tensor(out=xn[:], in0=src, scalar=r, in1=gb_t[gi][:],
                                                     op0=ALU.mult, op1=ALU.mult),
             reads=[src_r, r_r, ("gb", gi)], writes=[xn_r])

    def norm_tr(xn, xn_r, dstT, dst_name, tt):
        b = PS.alloc1()
        pb = bank(b).bitcast(BF16)

        def _tr(e):
            ins = None
            for c in range(NDC):
                ins = e.transpose(out=pb[:, c * 128:(c + 1) * 128], in_=xn[:, c * 128:(c + 1) * 128], identity=ident_b[:])
            return ins
        P.op("pe", _tr, reads=[xn_r, "ident_b"], writes=[("ps", b)])
        P.op("act", lambda e: e.activation(out=dstT[:, :, tt * 128:(tt + 1) * 128],
                                           in_=pb.rearrange("p (c t) -> p c t", t=128), func=AF.Copy),
             reads=[("ps", b)], writes=[(dst_name, tt)])
        PS.release(b)

    class Blk:
        pass

    def make_blocks():
        blks = []
        for s in range(NP):
            nb = SEQ // 512
            for k in range(nb):
                b = Blk()
                b.kind = "p"
                b.nseg, b.L, b.TT = 1, 512, 4
                b.first, b.last = (k == 0), (k == nb - 1)
                b.seqs = [s]
                b.xrows = [x_p[s, k * 512 + t * 128:k * 512 + (t + 1) * 128, :] for t in range(4)]
                b.yrows = [y_p[s, k * 512 + t * 128:k * 512 + (t + 1) * 128, :] for t in range(4)]
                blks.append(b)
        xs = x_s.rearrange("s t d -> (s t) d")
        ys = y_s.rearrange("s t d -> (s t) d")
        spb = 512 // LS
        for k0 in range(0, NS, spb):
            b = Blk()
            b.kind = "s"
            b.nseg = min(spb, NS - k0)
            b.L = LS
            b.TT = b.nseg * LS // 128
            b.first, b.last = True, True
            b.seqs = list(range(k0, k0 + b.nseg))
            b.xrows = [xs[k0 * LS + t * 128:k0 * LS + (t + 1) * 128, :] for t in range(b.TT)]
            b.yrows = [ys[k0 * LS + t * 128:k0 * LS + (t + 1) * 128, :] for t in range(b.TT)]
            blks.append(b)
        for i, b in enumerate(blks):
            b.bi = i
            b.buf = i % 2
        return blks

    blocks = make_blocks()
    import os as _os
    if _os.environ.get('KDBG_BLK'):
        blocks = [blocks[int(t)] for t in _os.environ['KDBG_BLK'].split(',')]
        for i_, b_ in enumerate(blocks):
            b_.bi = i_
            b_.buf = i_ % 2

    def aview(buf, c, b, lo, n, hist):
        w = hist + b.L
        v = buf[:, c, 0:b.nseg * w].rearrange("p (s l) -> p s l", s=b.nseg)
        return v[:, :, lo:lo + n]

    def nview(ap2d, b):
        return ap2d.rearrange("p (s l) -> p s l", s=b.nseg)

    def load_x(b):
        for tt in range(b.TT):
            P.op("sp", lambda e, tt=tt: e.dma_start(out=x_t[b.buf][:, tt, :], in_=b.xrows[tt]),
                 writes=[("x", b.buf, tt)], dma_slot=("x", b.buf, tt))

    def phase_hist(b):
        if not b.first:
            return
        if b.kind == "p":
            for c in range(4):
                P.op("pool", lambda e, c=c: e.memset(aview(a_ext, c, b, 0, KA - 1, KA - 1), 0.0), writes=[("ahist", c)])
                P.op("pool", lambda e, c=c: e.memset(aview(cv_ext, c, b, 0, KB - 1, KB - 1), 0.0), writes=[("bhist", c)])
            return
        s0, ns = b.seqs[0], b.nseg
        for (cache, K1, buf, hname) in ((cache_a, KA - 1, a_ext, "ahist"), (cache_b, KB - 1, cv_ext, "bhist")):
            stg, stg_r = new_tmp()
            rows = ns * K1
            P.op("sp", lambda e, stg=stg, cache=cache, rows=rows: e.dma_start(
                out=stg[0:rows, :], in_=cache[s0:s0 + ns].rearrange("s t c -> (s t) c")),
                writes=[stg_r], dma_slot=("cst", hname))
            tb1 = PS.alloc1()

            def _tr(e, stg=stg, rows=rows, tb1=tb1):
                ins = None
                for c in range(4):
                    ins = e.transpose(out=bank(tb1)[:, c * 128:c * 128 + rows], in_=stg[0:rows, c * 128:(c + 1) * 128],
                                      identity=ident_f[0:rows, 0:rows])
                return ins
            P.op("pe", _tr, reads=[stg_r, "ident_f"], writes=[("ps", tb1)])
            for c in range(4):
                P.op("dve", lambda e, c=c, tb1=tb1, rows=rows, K1=K1, buf=buf: e.tensor_copy(
                    out=aview(buf, c, b, 0, K1, K1),
                    in_=bank(tb1)[:, c * 128:c * 128 + rows].rearrange("p (s k) -> p s k", k=K1)),
                    reads=[("ps", tb1)], writes=[(hname, c)])
            PS.release(tb1)

    def phase0_elem(b):
        xs_ = []
        ss4, ss4_r = new_st4()
        for tt in range(b.TT):
            xn, xn_r = xn_t[tt], ("xn", tt)
            P.op("act", lambda e, tt=tt, xn=xn: e.activation(out=xn[:], in_=x_t[b.buf][:, tt, :], func=AF.Square,
                                                            accum_out=ss4[:, tt:tt + 1]),
                 reads=[("x", b.buf, tt)], writes=[ss4_r, xn_r])
            xs_.append((xn, xn_r))
        r4, r4_r = rstd4_ops(ss4, ss4_r, b.TT, 1.0 / D)
        for tt in range(b.TT):
            xn, xn_r = xs_[tt]
            P.op("dve", lambda e, tt=tt, xn=xn: e.scalar_tensor_tensor(
                out=xn[:], in0=x_t[b.buf][:, tt, :], scalar=r4[:, tt:tt + 1], in1=gb_t[0][:], op0=ALU.mult, op1=ALU.mult),
                reads=[("x", b.buf, tt), r4_r, ("gb", 0)], writes=[xn_r])
        b.xn = xs_

    def phase0_tr(b):
        for tt in range(b.TT):
            norm_tr(b.xn[tt][0], b.xn[tt][1], xnT, "xnT", tt)

    def win_mm(b, slot, slot_r, q):
        N = b.TT * 128
        bk = PS.alloc1()
        wv = slot.rearrange("p (q dc e) -> p q dc e", q=2, dc=NDC)

        def _mm(e):
            ins = None
            for dc in range(NDC):
                ins = e.matmul(out=bank(bk, N), lhsT=wv[:, q, dc, :], rhs=xnT[:, dc, 0:N], start=(dc == 0), stop=(dc == NDC - 1))
            return ins
        P.op("pe", _mm, reads=[slot_r] + [("xnT", t) for t in range(b.TT)], writes=[("ps", bk)])
        return bk

    def _get_piece(b, i, pin=False):
        if not hasattr(b, "pieces"):
            b.pieces = {}
        if i not in b.pieces:
            b.pieces[i] = wload(ws_in[i], res_ws["in"], pin=pin)

    def _get_chunk(b, k):
        i, q = divmod(k, 2)
        _get_piece(b, i)
        slot, slot_r = b.pieces[i]
        if q == 1:
            ring_pinned.discard(slot_r[1])
        return win_mm(b, slot, slot_r, q)

    def phaseA1p1(b):
        N = b.TT * 128
        get_chunk = lambda k: _get_chunk(b, k)
        for c in range(4):
            bv = get_chunk(2 * c)
            bg = get_chunk(2 * c + 1)
            sg, sg_r = new_tmp()
            P.op("act", lambda e, bg=bg, sg=sg: e.activation(out=sg[:, 0:N], in_=bank(bg, N), func=AF.Sigmoid),
                 reads=[("ps", bg)], writes=[sg_r])
            P.op("dve", lambda e, bv=bv, sg=sg, c=c: e.tensor_tensor(
                out=aview(a_ext, c, b, KA - 1, b.L, KA - 1), in0=nview(bank(bv, N), b), in1=nview(sg[:, 0:N], b), op=ALU.mult),
                reads=[("ps", bv), sg_r], writes=[("abody", c)])
            PS.release(bv, bg)
            yield

    def phaseA1p2(b):
        N = b.TT * 128
        get_chunk = lambda k: _get_chunk(b, k)
        for c in range(4):
            bgc = get_chunk(8 + 3 * c)
            bvv = get_chunk(8 + 3 * c + 1)
            bgb = get_chunk(8 + 3 * c + 2)
            vs, vs_r = new_tmp()
            P.op("act", lambda e, bvv=bvv, vs=vs: e.activation(out=vs[:, 0:N], in_=bank(bvv, N), func=AF.Copy),
                 reads=[("ps", bvv)], writes=[vs_r])
            P.op("dve", lambda e, bgc=bgc, vs=vs, c=c: e.tensor_tensor(
                out=aview(cv_ext, c, b, KB - 1, b.L, KB - 1), in0=nview(bank(bgc, N), b), in1=nview(vs[:, 0:N], b), op=ALU.mult),
                reads=[("ps", bgc), vs_r], writes=[("bbody", c)])
            u, u_r = new_tmp()
            P.op("dve", lambda e, u=u, c=c: e.tensor_scalar(
                out=nview(u[:, 0:N], b), in0=aview(cv_ext, c, b, 0, b.L, KB - 1), scalar1=pT[:, c, 34:35], scalar2=None, op0=ALU.mult),
                reads=[("bbody", c), ("bhist", c), "pT"], writes=[u_r])
            for k in range(1, KB):
                P.op("dve", lambda e, u=u, c=c, k=k: e.scalar_tensor_tensor(
                    out=nview(u[:, 0:N], b), in0=aview(cv_ext, c, b, k, b.L, KB - 1), scalar=pT[:, c, 34 + k:35 + k],
                    in1=nview(u[:, 0:N], b), op0=ALU.mult, op1=ALU.add),
                    reads=[("bbody", c), ("bhist", c), "pT", u_r], writes=[u_r])
            P.op("dve", lambda e, u=u, bgb=bgb, c=c: e.tensor_tensor(
                out=mix(b, 4 + c)[:, 0:N], in0=bank(bgb, N), in1=u[:, 0:N], op=ALU.mult),
                reads=[("ps", bgb), u_r], writes=[mix_r(b, 4 + c)])
            PS.release(bgc, bvv, bgb)
            yield

    def phase_conv(b):
        N = b.TT * 128
        acv = [nview(ac_t[:, c, 0:N], b) for c in range(4)]
        for c in range(4):
            P.op("dve", lambda e, c=c: e.tensor_scalar(
                out=acv[c], in0=aview(a_ext, c, b, 0, b.L, KA - 1), scalar1=pT[:, c, 0:1], scalar2=pT[:, c, 31:32],
                op0=ALU.mult, op1=ALU.add),
                reads=[("abody", c), ("ahist", c), "pT"], writes=[("ac", c)])
        yield
        for k in range(1, KA):
            for c in range(4):
                P.op("dve", lambda e, c=c, k=k: e.scalar_tensor_tensor(
                    out=acv[c], in0=aview(a_ext, c, b, k, b.L, KA - 1), scalar=pT[:, c, k:k + 1], in1=acv[c],
                    op0=ALU.mult, op1=ALU.add),
                    reads=[("abody", c), ("ahist", c), "pT", ("ac", c)], writes=[("ac", c)])
            yield

    def phase_tail(b, which):
        buf, K1, body, hist = (a_ext, KA - 1, "abody", "ahist") if which == "a" else (cv_ext, KB - 1, "bbody", "bhist")
        if b.last:
            if which == "a":
                dst = na_p if b.kind == "p" else na_s
            else:
                dst = nb_p if b.kind == "p" else nb_s
            for si, s in enumerate(b.seqs):
                tb1 = PS.alloc1()

                def _t1(e, si=si, tb1=tb1):
                    ins = None
                    for c in range(4):
                        ins = e.transpose(out=bank(tb1)[0:K1, c * 128:(c + 1) * 128],
                                          in_=aview(buf, c, b, b.L, K1, K1)[:, si, :], identity=ident_f[:])
                    return ins
                P.op("pe", _t1, reads=[(body, c) for c in range(4)] + [(hist, c) for c in range(4)] + ["ident_f"],
                     writes=[("ps", tb1)])
                og, og_r = new_tmp()
                P.op("act", lambda e, tb1=tb1, og=og: e.activation(out=og[0:K1, :], in_=bank(tb1)[0:K1, :], func=AF.Copy),
                     reads=[("ps", tb1)], writes=[og_r])
                PS.release(tb1)
                P.op("sp", lambda e, s=s, og=og: e.dma_start(out=dst[s, :, :], in_=og[0:K1, :]),
                     reads=[og_r], writes=[("out", which, b.kind, s)], dma_slot=("o" + which, si))
        else:
            for c in range(4):
                P.op("pool", lambda e, c=c: e.tensor_copy(out=buf[:, c, 0:K1], in_=buf[:, c, b.L:b.L + K1]),
                     reads=[(body, c), (hist, c)], writes=[(hist, c)])

    def phaseA2_stats(b):
        N = b.TT * 128
        bm = PS.alloc_pair()
        be = bm + 1
        b.bm, b.be = bm, be
        sqs = []
        for c in range(4):
            sq, sq_r = new_tmp()
            P.op("act", lambda e, c=c, sq=sq: e.activation(out=sq[:, 0:N], in_=ac_t[:, c, 0:N], func=AF.Square),
                 reads=[("ac", c)], writes=[sq_r])
            sqs.append((sq, sq_r))

        def _m1(e):
            ins = None
            for c in range(4):
                ins = e.matmul(out=bank(bm, N), lhsT=ones_f[:], rhs=ac_t[:, c, 0:N], start=(c == 0), stop=(c == 3))
            return ins
        P.op("pe", _m1, reads=[("ac", c) for c in range(4)] + ["ones_f"], writes=[("ps", bm)])

        def _m2(e):
            ins = None
            for c in range(4):
                ins = e.matmul(out=bank(be, N), lhsT=ones_f[:], rhs=sqs[c][0][:, 0:N], start=(c == 0), stop=(c == 3))
            return ins
        P.op("pe", _m2, reads=[s_[1] for s_ in sqs] + ["ones_f"], writes=[("ps", be)])

    def phaseA2_norm(b):
        N = b.TT * 128
        bm, be = b.bm, b.be
        msq, msq_r = new_tmp()
        P.op("act", lambda e: e.activation(out=msq[:, 0:N], in_=bank(bm, N), func=AF.Square), reads=[("ps", bm)], writes=[msq_r])
        var, var_r = new_tmp()
        P.op("dve", lambda e: e.tensor_tensor(out=var[:, 0:N], in0=bank(be, N), in1=msq[:, 0:N], op=ALU.subtract),
             reads=[("ps", be), msq_r], writes=[var_r])
        P.op("act", lambda e: e.activation(out=var[:, 0:N], in_=var[:, 0:N], func=AF.Ln, bias=eps_t[:, 0:1]),
             reads=[var_r, "eps_t"], writes=[var_r])
        rs, rs_r = new_tmp(pin=True)
        P.op("act", lambda e: e.activation(out=rs[:, 0:N], in_=var[:, 0:N], func=AF.Exp, scale=-0.5), reads=[var_r], writes=[rs_r])
        PS.release(be)
        for c in range(4):
            z, z_r = new_tmp()
            P.op("dve", lambda e, c=c, z=z: e.tensor_tensor(out=z[:, 0:N], in0=ac_t[:, c, 0:N], in1=bank(bm, N), op=ALU.subtract),
                 reads=[("ac", c), ("ps", bm)], writes=[z_r])
            P.op("dve", lambda e, z=z: e.tensor_tensor(out=z[:, 0:N], in0=z[:, 0:N], in1=rs[:, 0:N], op=ALU.mult),
                 reads=[z_r, rs_r], writes=[z_r])
            P.op("act", lambda e, c=c, z=z: e.activation(out=mix(b, c)[:, 0:N], in_=z[:, 0:N], func=AF.Silu,
                                                        scale=pT[:, c, 32:33], bias=pT[:, c, 33:34]),
                 reads=[z_r, "pT"], writes=[("mixT", c)])
            if c == 3:
                PS.release(bm)
                tmp_pinned.discard(rs_r[1])
            yield

    def phaseB(b):
        slots = b.wout_slots
        b.hn = [(hn_t[tt], ("hn", tt)) for tt in range(b.TT)]
        pend = {}

        def stage1(tt):
            pb = PS.alloc_pair()

            def _mm(e, tt=tt, pb=pb):
                ins = None
                for half in range(2):
                    for cc in range(NDC):
                        sl = slots[half * 2 + cc // 4][0].rearrange("p (cc d) -> p cc d", cc=4)
                        ins = e.matmul(out=bank(pb + half), lhsT=mix(b, cc)[:, tt * 128:(tt + 1) * 128], rhs=sl[:, cc % 4, :],
                                       start=(cc == 0), stop=(cc == NDC - 1))
                return ins
            P.op("pe", _mm, reads=[s_[1] for s_ in slots] + [mix_r(b, cc) for cc in range(NDC)],
                 writes=[("ps", pb), ("ps", pb + 1)])
            ss, ss_r = new_st()
            P.op("act", lambda e, pb=pb, ss=ss, tt=tt: e.activation(out=hn_t[tt][:], in_=bank(pb, 1024), func=AF.Square, accum_out=ss),
                 reads=[("ps", pb), ("ps", pb + 1)], writes=[ss_r, ("hn", tt)])
            r, r_r = rstd_ops(ss, ss_r, 1.0 / D)
            P.op("dve", lambda e, tt=tt, pb=pb, r=r: e.scalar_tensor_tensor(
                out=h_t[:, tt, :], in0=bank(pb, 1024), scalar=r, in1=gb_t[1][:], op0=ALU.mult, op1=ALU.mult),
                reads=[("ps", pb), ("ps", pb + 1), r_r, ("gb", 1)], writes=[("h", tt)])
            PS.release(pb, pb + 1)
            P.op("dve", lambda e, tt=tt: e.tensor_tensor(out=h_t[:, tt, :], in0=h_t[:, tt, :], in1=x_t[b.buf][:, tt, :], op=ALU.add),
                 reads=[("h", tt), ("x", b.buf, tt)], writes=[("h", tt)])
            ss2, ss2_r = new_st()
            P.op("act", lambda e, tt=tt, ss2=ss2: e.activation(out=hn_t[tt][:], in_=h_t[:, tt, :], func=AF.Square, accum_out=ss2),
                 reads=[("h", tt)], writes=[ss2_r, ("hn", tt)])
            pend[tt] = rstd_ops(ss2, ss2_r, 1.0 / D)

        def stage2(tt):
            r2, r2_r = pend[tt]
            P.op("dve", lambda e, tt=tt, r2=r2: e.scalar_tensor_tensor(
                out=hn_t[tt][:], in0=h_t[:, tt, :], scalar=r2, in1=gb_t[2][:], op0=ALU.mult, op1=ALU.mult),
                reads=[("h", tt), r2_r, ("gb", 2)], writes=[("hn", tt)])

        for tt in range(b.TT):
            stage1(tt)
            if tt == b.TT - 1:
                for s_ in slots:
                    ring_pinned.discard(s_[1][1])
            if tt > 0:
                stage2(tt - 1)
            yield
        stage2(b.TT - 1)
        yield

    def preload_wout(b):
        b.wout_slots = [wload(ws_out[i], res_ws["out"], pin=True) for i in range(4)]

    def phaseC(b):
        for tt in range(b.TT):
            norm_tr(b.hn[tt][0], b.hn[tt][1], hnT, "hnT", tt)

    def phaseD(b):
        N = b.TT * 128
        for j in range(NFC):
            slot, slot_r = wload(ws_gu[j], res_ws["gu"])
            wv = slot.rearrange("p (gu dc e) -> p gu dc e", gu=2, dc=NDC)
            pb = PS.alloc_pair()

            def _mm(e, wv=wv, pb=pb):
                ins = None
                for gu in range(2):
                    for dc in range(NDC):
                        ins = e.matmul(out=bank(pb + gu, N), lhsT=wv[:, gu, dc, :], rhs=hnT[:, dc, 0:N],
                                       start=(dc == 0), stop=(dc == NDC - 1))
                return ins
            P.op("pe", _mm, reads=[slot_r] + [("hnT", t) for t in range(b.TT)], writes=[("ps", pb), ("ps", pb + 1)])
            sg, sg_r = new_tmp()
            P.op("act", lambda e, pb=pb, sg=sg: e.activation(out=sg[:, 0:N], in_=bank(pb, N), func=AF.Silu),
                 reads=[("ps", pb)], writes=[sg_r])
            P.op("dve", lambda e, pb=pb, sg=sg, j=j: e.tensor_tensor(out=actT[:, j, 0:N], in0=bank(pb + 1, N), in1=sg[:, 0:N], op=ALU.mult),
                 reads=[("ps", pb + 1), sg_r], writes=[("actT", j)])
            PS.release(pb, pb + 1)
            yield

    def phaseE_half(b, half):
        banks = []
        for _ in range((b.TT + 1) // 2):
            p_ = PS.alloc_pair()
            banks += [p_, p_ + 1]
        for extra in banks[b.TT:]:
            PS.release(extra)
        banks = banks[:b.TT]
        for q in range(6):
            nfc = 4 if q < 5 else 2
            slot, slot_r = wload(ws_dn[half * 6 + q], res_ws["dn"], cols=nfc * 512)
            sl = slot.rearrange("p (fc d) -> p fc d", fc=4)

            def _mm(e, q=q, nfc=nfc, sl=sl):
                ins = None
                for f in range(nfc):
                    fc = 4 * q + f
                    for tt in range(b.TT):
                        ins = e.matmul(out=bank(banks[tt]), lhsT=actT[:, fc, tt * 128:(tt + 1) * 128], rhs=sl[:, f, :],
                                       start=(fc == 0), stop=(fc == NFC - 1))
                return ins
            P.op("pe", _mm, reads=[slot_r] + [("actT", 4 * q + f) for f in range(nfc)],
                 writes=[("ps", bk) for bk in banks])
            yield
        if half == 0:
            ss0, ss0_r = new_st4()
            for tt in range(b.TT):
                jk, jk_r = new_tmp()
                P.op("act", lambda e, tt=tt, jk=jk: e.activation(out=jk[:], in_=bank(banks[tt]), func=AF.Square,
                                                                accum_out=ss0[:, tt:tt + 1]),
                     reads=[("ps", banks[tt])], writes=[ss0_r, jk_r])
                P.op("act", lambda e, tt=tt: e.activation(out=cv_ext[:, tt, 2:514], in_=bank(banks[tt]), func=AF.Copy),
                     reads=[("ps", banks[tt]), ss0_r], writes=[("bbody", tt)])
            b.ss0 = (ss0, ss0_r)
            PS.release(*banks)
        else:
            b.ebanks = banks
        yield

    def phaseE_epi(b):
        banks = b.ebanks
        ss1, ss1_r = new_st4()
        for tt in range(b.TT):
            jk, jk_r = new_tmp()
            P.op("act", lambda e, tt=tt, jk=jk: e.activation(out=jk[:], in_=bank(banks[tt]), func=AF.Square,
                                                            accum_out=ss1[:, tt:tt + 1]),
                 reads=[("ps", banks[tt])], writes=[ss1_r, jk_r])
        r4, r4_r = rstd4_ops(ss1, ss1_r, b.TT, 1.0 / D, add=b.ss0)
        for tt in range(b.TT):
            r = r4[:, tt:tt + 1]
            for half in range(2):
                t, t_r = new_tmp()
                if half == 0:
                    P.op("dve", lambda e, tt=tt, t=t, r=r: e.scalar_tensor_tensor(
                        out=t[:], in0=cv_ext[:, tt, 2:514], scalar=r, in1=gb_t[3][:, 0:512], op0=ALU.mult, op1=ALU.mult),
                        reads=[("bbody", tt), r4_r, ("gb", 3)], writes=[t_r])
                else:
                    P.op("dve", lambda e, tt=tt, t=t, r=r: e.scalar_tensor_tensor(
                        out=t[:], in0=bank(banks[tt]), scalar=r, in1=gb_t[3][:, 512:1024], op0=ALU.mult, op1=ALU.mult),
                        reads=[("ps", banks[tt]), r4_r, ("gb", 3)], writes=[t_r])
                P.op("dve", lambda e, tt=tt, half=half, t=t: e.tensor_tensor(
                    out=h_t[:, tt, half * 512:(half + 1) * 512], in0=h_t[:, tt, half * 512:(half + 1) * 512], in1=t[:], op=ALU.add),
                    reads=[("h", tt), t_r], writes=[("h", tt)])
            PS.release(banks[tt])
            P.op("sp", lambda e, tt=tt: e.dma_start(out=b.yrows[tt], in_=h_t[:, tt, :]),
                 reads=[("h", tt)], writes=[("out", "y", b.bi, tt)], dma_slot=("y", tt))

    def run(*gens):
        for g in gens:
            if g is not None:
                for _ in g:
                    pass

    def chain(*gens):
        for g in gens:
            if g is not None:
                yield from g

    def interleave(ga, gb, on_b_done=None):
        da = db = False
        while not (da and db):
            if not da:
                try:
                    next(ga)
                except StopIteration:
                    da = True
            if not db:
                try:
                    next(gb)
                except StopIteration:
                    db = True
                    if on_b_done is not None:
                        on_b_done()

    nblk = len(blocks)
    import os as _os
    _stop = int(_os.environ.get("KDBG_STOP", "999"))

    def driver():
        load_x(blocks[0])
        if nblk > 1:
            load_x(blocks[1])
        b0 = blocks[0]
        phase0_elem(b0)
        phase0_tr(b0)
        phase_hist(b0)
        run(phaseA1p1(b0), phaseA1p2(b0))
        phase_tail(b0, "b")
        if nblk > 1:
            phase0_elem(blocks[1])
        prep_gu_dn()
        for i in range(nblk + 1):
            b = blocks[i] if i < nblk else None
            pb_ = blocks[i - 1] if i > 0 else None
            nb_ = blocks[i + 1] if i + 1 < nblk else None
            if pb_ is not None:
                phaseC(pb_)
            if nb_ is not None:
                phase0_tr(nb_)
            ffn = chain(phaseD(pb_), phaseE_half(pb_, 0), phaseE_half(pb_, 1)) if pb_ is not None else iter(())
            conv = phase_conv(b) if b is not None else iter(())

            def after_conv(b=b):
                if b is not None:
                    phase_tail(b, "a")
                    phaseA2_stats(b)
                    run(phaseA2_norm(b))
            interleave(ffn, conv, on_b_done=after_conv)
            if b is not None:
                preload_wout(b)
            if nb_ is not None:
                _get_piece(nb_, 0, pin=True)
                _get_piece(nb_, 1, pin=True)
            if pb_ is not None:
                phaseE_epi(pb_)
            if b is None:
                break
            if nb_ is not None:
                phase_hist(nb_)
            interleave(phaseB(b), phaseA1p1(nb_) if nb_ is not None else iter(()))
            if nb_ is not None:
                run(phaseA1p2(nb_))
                phase_tail(nb_, "b")
            if i + 2 < nblk:
                load_x(blocks[i + 2])
                phase0_elem(blocks[i + 2])

    driver()

    out_res = [r for r in P.lastw if isinstance(r, tuple) and r and r[0] == "out"]
    P.op("sp", lambda e: e.nop(), reads=out_res)

    P.finalize()
    dma_slots = list(P.dma_cnt.keys())
    esem = {e: es.enter_context(nc.semaphore(f"sem_{e}")) for e in Prog.ENGS}
    dsem = {s: es.enter_context(nc.semaphore(f"dsem_{i}")) for i, s in enumerate(dma_slots)}
    with nc.Block() as block:
        @block.tensor
        def _(e):
            P.emit_engine("pe", e, esem, dsem)

        @block.scalar
        def _(e):
            P.emit_engine("act", e, esem, dsem)

        @block.vector
        def _(e):
            P.emit_engine("dve", e, esem, dsem)

        @block.gpsimd
        def _(e):
            P.emit_engine("pool", e, esem, dsem)

        @block.sync
        def _(e):
            P.emit_engine("sp", e, esem, dsem)
    es.close()
    return nc


def make_in_maps(n_cores, inputs, NP, NS):
    f = lambda a: np.ascontiguousarray(np.asarray(a, dtype=np.float32))
    ident = np.eye(128, dtype=np.float32)
    maps = []
    for c in range(n_cores):
        m = {
            "x_p": f(inputs["x_prompt"][c * NP:(c + 1) * NP]),
            "x_s": f(inputs["x_sample"][c * NS:(c + 1) * NS]),
            "cache_a": f(inputs["cache_conv_a"][0, c * NS:(c + 1) * NS]),
            "cache_b": f(inputs["cache_conv_b"][0, c * NS:(c + 1) * NS]),
            "g_pre1": f(inputs["norm_mix_pre"]),
            "g_post1": f(inputs["norm_mix_post"]),
            "g_pre2": f(inputs["norm_ffn_pre"]),
            "g_post2": f(inputs["norm_ffn_post"]),
            "w_in": f(inputs["w_in"][0]),
            "w_out": f(inputs["w_out"][0]),
            "w_gu": f(inputs["w_gate_up"][0]),
            "w_dn": f(inputs["w_down"][0]),
            "conv_a_w": f(inputs["conv_a_w"][0]),
            "conv_a_b": f(inputs["conv_a_b"]),
            "ln_g": f(inputs["conv_a_ln_g"]),
            "ln_b": f(inputs["conv_a_ln_b"]),
            "conv_b_w": f(inputs["conv_b_w"][0]),
            "ident": ident,
        }
        maps.append(m)
    return maps


def gather(results, n_cores):
    cat = lambda k: np.concatenate([np.asarray(results[c][k], dtype=np.float32) for c in range(n_cores)], axis=0)
    return (cat("y_p"), cat("y_s"), cat("na_p")[None], cat("nb_p")[None], cat("na_s")[None], cat("nb_s")[None])


def kernel(**inputs):
    B, SEQ = inputs["x_prompt"].shape[0], inputs["x_prompt"].shape[1]
    BS, LS = inputs["x_sample"].shape[0], inputs["x_sample"].shape[1]
    NP, NS = B // N_CORES, BS // N_CORES
    nc = build_program(NP, SEQ, NS, LS)
    in_maps = make_in_maps(N_CORES, inputs, NP, NS)
    res = run_bass_kernel_spmd(nc, in_maps, core_ids=list(range(N_CORES)))
    return gather(res.results, N_CORES)
```

```python
import numpy as np
import concourse.bass as bass
import concourse.mybir as mybir
from concourse.bass_utils import run_bass_kernel_spmd

F32 = mybir.dt.float32
BF16 = mybir.dt.bfloat16
AF = mybir.ActivationFunctionType
ALU = mybir.AluOpType

D = 1024
DA = 512
DB = 512
DIN = 2560
DFF = 2816
KA = 31
KB = 3
EPS = 1e-6
NDC = 8
NFC = 22
NPAR = 37
RING = 8
NTMP = 9
N_CORES = 8

E_ORDER = [0, 4, 1, 5, 2, 6, 3, 7, 12, 16, 8, 13, 17, 9, 14, 18, 10, 15, 19, 11]


class _Op:
    __slots__ = ("eng", "fn", "deps", "idx", "signal", "dma_slot", "dma_val", "waits", "sigval")


class Prog:
    ENGS = ("pe", "act", "dve", "pool", "sp")

    def __init__(self):
        self.ops = {e: [] for e in self.ENGS}
        self.lastw = {}
        self.readers = {}
        self.dma_cnt = {}

    def op(self, eng, fn, reads=(), writes=(), dma_slot=None):
        o = _Op()
        o.eng = eng
        o.fn = fn
        o.idx = len(self.ops[eng])
        o.signal = False
        deps = []
        for r in reads:
            w = self.lastw.get(r)
            if w is not None:
                deps.append(w)
        for r in writes:
            w = self.lastw.get(r)
            if w is not None:
                deps.append(w)
            deps.extend(self.readers.get(r, ()))
        o.deps = deps
        if dma_slot is not None:
            c = self.dma_cnt.get(dma_slot, 0) + 1
            self.dma_cnt[dma_slot] = c
            o.dma_slot = dma_slot
            o.dma_val = 16 * c
            tok = ("dma", dma_slot, 16 * c)
        else:
            o.dma_slot = None
            o.dma_val = 0
            tok = ("eng", eng, o.idx)
        for r in reads:
            self.readers.setdefault(r, []).append(tok)
        for r in writes:
            self.lastw[r] = tok
            self.readers[r] = []
        self.ops[eng].append(o)
        return o

    def finalize(self):
        for e in self.ENGS:
            seen = {}
            for o in self.ops[e]:
                need_eng = {}
                need_dma = {}
                for d in o.deps:
                    if d[0] == "eng":
                        _, de, di = d
                        if de == e:
                            if e in ("pe", "sp"):
                                continue
                        if seen.get(de, -1) >= di:
                            continue
                        if need_eng.get(de, -1) < di:
                            need_eng[de] = di
                    else:
                        _, slot, val = d
                        if seen.get(("dma", slot), 0) >= val:
                            continue
                        if need_dma.get(slot, 0) < val:
                            need_dma[slot] = val
                waits = []
                for de, di in need_eng.items():
                    seen[de] = di
                    waits.append(("eng", de, di))
                    self.ops[de][di].signal = True
                for slot, val in need_dma.items():
                    seen[("dma", slot)] = val
                    waits.append(("dma", slot, val))
                o.waits = waits
        for e in self.ENGS:
            c = 0
            for o in self.ops[e]:
                if o.signal:
                    c += 1
                o.sigval = c

    def emit_engine(self, e, eng, esem, dsem):
        for o in self.ops[e]:
            for w in o.waits:
                if w[0] == "eng":
                    eng.wait_ge(esem[w[1]], self.ops[w[1]][w[2]].sigval)
                else:
                    eng.wait_ge(dsem[w[1]], w[2])
            ins = o.fn(eng)
            if o.dma_slot is not None:
                ins.then_inc(dsem[o.dma_slot], 16)
            elif o.signal:
                ins.then_inc(esem[e], 1)


class _Psum:
    def __init__(self):
        self.held = [False] * 8
        self.stamp = [0] * 8
        self.clock = 0

    def alloc1(self):
        free = [b for b in range(8) if not self.held[b]]
        assert free, "out of PSUM banks"
        b = min(free, key=lambda k: self.stamp[k])
        self.held[b] = True
        return b

    def alloc_pair(self):
        free = [b for b in range(0, 8, 2) if not self.held[b] and not self.held[b + 1]]
        assert free, "out of PSUM bank pairs"
        b = min(free, key=lambda k: max(self.stamp[k], self.stamp[k + 1]))
        self.held[b] = self.held[b + 1] = True
        return b

    def release(self, *banks):
        for b in banks:
            assert self.held[b]
            self.held[b] = False
            self.clock += 1
            self.stamp[b] = self.clock


def build_program(NP, SEQ, NS, LS=64):
    assert SEQ % 512 == 0 and (NS * LS) % 128 == 0 and NS * (KA - 1) <= 128
    nc = bass.Bass("TRN2", target_bir_lowering=False)
    P = Prog()
    PS = _Psum()

    def din(name, shape, dt=F32):
        return nc.dram_tensor(name, list(shape), dt, kind="ExternalInput").ap()

    def dout(name, shape, dt=F32):
        return nc.dram_tensor(name, list(shape), dt, kind="ExternalOutput").ap()

    x_p = din("x_p", [NP, SEQ, D])
    x_s = din("x_s", [NS, LS, D])
    cache_a = din("cache_a", [NS, KA - 1, DA])
    cache_b = din("cache_b", [NS, KB - 1, DB])
    g_pre1 = din("g_pre1", [1, D])
    g_post1 = din("g_post1", [1, D])
    g_pre2 = din("g_pre2", [1, D])
    g_post2 = din("g_post2", [1, D])
    w_in = din("w_in", [D, DIN])
    w_out = din("w_out", [D, D])
    w_gu = din("w_gu", [D, 2 * DFF])
    w_dn = din("w_dn", [DFF, D])
    conv_a_w = din("conv_a_w", [KA, DA])
    conv_a_b = din("conv_a_b", [1, DA])
    ln_g = din("ln_g", [1, DA])
    ln_b = din("ln_b", [1, DA])
    conv_b_w = din("conv_b_w", [KB, DB])
    ident_in = din("ident", [128, 128])

    y_p = dout("y_p", [NP, SEQ, D])
    y_s = dout("y_s", [NS, LS, D])
    na_p = dout("na_p", [NP, KA - 1, DA])
    nb_p = dout("nb_p", [NP, KB - 1, DB])
    na_s = dout("na_s", [NS, KA - 1, DA])
    nb_s = dout("nb_s", [NS, KB - 1, DB])

    ws_in = nc.dram_tensor("ws_in", [10, 128, 2048], BF16, kind="Internal").ap()
    ws_out = nc.dram_tensor("ws_out", [4, 128, 2048], BF16, kind="Internal").ap()
    ws_gu = nc.dram_tensor("ws_gu", [NFC, 128, 2048], BF16, kind="Internal").ap()
    ws_dn = nc.dram_tensor("ws_dn", [12, 128, 2048], BF16, kind="Internal").ap()

    from contextlib import ExitStack
    es = ExitStack()

    def sb(name, shape, dt=F32):
        return es.enter_context(nc.sbuf_tensor(name, list(shape), dt))

    ident_f = sb("ident_f", [128, 128])
    ident_b = sb("ident_b", [128, 128], BF16)
    ones_f = sb("ones_f", [128, 128])
    neghalf = sb("neghalf", [128, 1])
    eps_t = sb("eps_t", [128, 1])
    gb_t = [sb(f"gb{i}", [128, D]) for i in range(4)]
    pT = sb("pT", [128, 4, NPAR])
    x_t = [sb(f"x{i}", [128, 4, D]) for i in range(2)]
    h_t = sb("h", [128, 4, D])
    xn_t = [sb(f"xn{i}", [128, D], BF16) for i in range(4)]
    hn_t = [sb(f"hn{i}", [128, D], BF16) for i in range(4)]
    xnT = sb("xnT", [128, NDC, 512], BF16)
    hnT = sb("hnT", [128, NDC, 512], BF16)
    a_ext = sb("a_ext", [128, 4, 544])
    cv_ext = sb("cv_ext", [128, 4, 516])
    ac_t = sb("ac", [128, 4, 512])
    tmp_t = [sb(f"tmp{i}", [128, 512]) for i in range(NTMP)]
    mixT_lo = sb("mixT_lo", [128, 4, 512], BF16)
    mixT_hi = [sb(f"mixT_hi{i}", [128, 4, 512], BF16) for i in range(2)]

    def mix(b, cc):
        return mixT_lo[:, cc] if cc < 4 else mixT_hi[b.buf][:, cc - 4]

    def mix_r(b, cc):
        return ("mixT", cc) if cc < 4 else ("mixT", cc, b.buf)
    actT = sb("actT", [128, NFC, 512], BF16)
    ring = [sb(f"ring{i}", [128, 2048], BF16) for i in range(RING)]
    st = sb("st", [128, 64])
    st4 = sb("st4", [128, 64])
    ps = es.enter_context(nc.psum_tensor("ps", [128, 4096], F32))
    print("SBUF bytes remaining per partition:", nc.sbuf_bytes_remaining)

    def bank(b, n=512):
        return ps[:, b * 512:b * 512 + n]

    cnt = {"tmp": 0, "st": 0, "ring": 0, "st4": 0}

    tmp_pinned = set()

    def new_tmp(pin=False):
        while True:
            i = cnt["tmp"] % NTMP
            cnt["tmp"] += 1
            if i not in tmp_pinned:
                break
        if pin:
            tmp_pinned.add(i)
        return tmp_t[i], ("tmp", i)

    def new_st():
        i = cnt["st"] % 64
        cnt["st"] += 1
        return st[:, i:i + 1], ("st", i)

    def new_st4():
        i = cnt["st4"] % 16
        cnt["st4"] += 1
        return st4[:, 4 * i:4 * i + 4], ("st4", i)

    ring_pinned = set()

    def wload(src_ap, src_res, cols=2048, pin=False):
        while True:
            i = cnt["ring"] % RING
            cnt["ring"] += 1
            if i not in ring_pinned:
                break
        if pin:
            ring_pinned.add(i)
        dst = ring[i]
        P.op("sp", lambda e, dst=dst, src_ap=src_ap, cols=cols: e.dma_start(out=dst[:, 0:cols], in_=src_ap[:, 0:cols]),
             reads=src_res, writes=[("ring", i)], dma_slot=("ring", i))
        return dst, ("ring", i)

    res_ws = {"in": [], "out": [], "gu": [], "dn": []}
    w_in_v = w_in.rearrange("(dc p) (j e) -> p j dc e", p=128, e=128)
    ws_in_v = ws_in.rearrange("i p (q dc e) -> i p q dc e", q=2, dc=NDC)
    w_out_v = w_out.rearrange("(cc p) d -> p cc d", p=128)
    ws_out_v = ws_out.rearrange("i p (cc d) -> i p cc d", cc=4)
    w_gu_v = w_gu.rearrange("(dc p) (gu j e) -> p gu j dc e", p=128, gu=2, e=128)
    ws_gu_v = ws_gu.rearrange("j p (gu dc e) -> j p gu dc e", gu=2, dc=NDC)
    w_dn_v = w_dn.rearrange("(fc p) d -> p fc d", p=128)
    ws_dn_v = ws_dn.rearrange("i p (fc d) -> i p fc d", fc=4)

    def prep_in_out():
        for k, j in enumerate(E_ORDER):
            i, q = divmod(k, 2)
            r = ("ws", "in", k)
            res_ws["in"].append(r)
            P.op("pool", lambda e, i=i, q=q, j=j: e.dma_start(out=ws_in_v[i, :, q, :, :], in_=w_in_v[:, j, :, :]),
                 writes=[r], dma_slot="prep_in0" if k < 8 else "prep_in1")
        for half in range(2):
            for q in range(2):
                r = ("ws", "out", half * 2 + q)
                res_ws["out"].append(r)
                P.op("pool", lambda e, half=half, q=q: e.dma_start(
                    out=ws_out_v[half * 2 + q], in_=w_out_v[:, 4 * q:4 * q + 4, half * 512:(half + 1) * 512]),
                    writes=[r], dma_slot="prep_out")

    def prep_gu_dn():
        for j in range(NFC):
            for gu in range(2):
                r = ("ws", "gu", j * 2 + gu)
                res_ws["gu"].append(r)
                P.op("pool", lambda e, j=j, gu=gu: e.dma_start(out=ws_gu_v[j, :, gu, :, :], in_=w_gu_v[:, gu, j, :, :]),
                     writes=[r], dma_slot="prep_gu")
        for half in range(2):
            for q in range(6):
                nfc = 4 if q < 5 else 2
                r = ("ws", "dn", half * 6 + q)
                res_ws["dn"].append(r)
                P.op("pool", lambda e, half=half, q=q, nfc=nfc: e.dma_start(
                    out=ws_dn_v[half * 6 + q, :, 0:nfc, :], in_=w_dn_v[:, 4 * q:4 * q + nfc, half * 512:(half + 1) * 512]),
                    writes=[r], dma_slot="prep_dn")


    P.op("sp", lambda e: e.dma_start(out=ident_f[:], in_=ident_in[:, :]), writes=["ident_f"], dma_slot="c0")
    for i, g in enumerate((g_pre1, g_post1, g_pre2, g_post2)):
        P.op("sp", lambda e, i=i, g=g: e.dma_start(out=gb_t[i][:], in_=g[0, :].partition_broadcast(128)),
             writes=[("gb", i)], dma_slot=("c1", i))
    pstage, pstage_r = new_tmp()
    P.op("sp", lambda e: e.dma_start(out=pstage[0:KA, :], in_=conv_a_w[:, :]), writes=[pstage_r], dma_slot="c2")
    P.op("sp", lambda e: e.dma_start(out=pstage[31:32, :], in_=conv_a_b[:, :]), writes=["ps1"], dma_slot="c3")
    P.op("sp", lambda e: e.dma_start(out=pstage[32:33, :], in_=ln_g[:, :]), writes=["ps2"], dma_slot="c4")
    P.op("sp", lambda e: e.dma_start(out=pstage[33:34, :], in_=ln_b[:, :]), writes=["ps3"], dma_slot="c5")
    P.op("sp", lambda e: e.dma_start(out=pstage[34:37, :], in_=conv_b_w[:, :]), writes=["ps4"], dma_slot="c6")
    P.op("dve", lambda e: e.tensor_copy(out=ident_b[:], in_=ident_f[:]), reads=["ident_f"], writes=["ident_b"])
    P.op("dve", lambda e: e.memset(ones_f[:], 1.0 / DA), writes=["ones_f"])
    P.op("dve", lambda e: e.memset(neghalf[:], -0.5), writes=["neghalf"])
    P.op("dve", lambda e: e.memset(eps_t[:], EPS), writes=["eps_t"])

    tb = PS.alloc1()

    def _ptr(e):
        ins = None
        for c in range(4):
            ins = e.transpose(out=bank(tb)[:, c * 64:c * 64 + NPAR], in_=pstage[0:NPAR, c * 128:(c + 1) * 128],
                              identity=ident_f[0:NPAR, 0:NPAR])
        return ins
    P.op("pe", _ptr, reads=[pstage_r, "ps1", "ps2", "ps3", "ps4", "ident_f"], writes=[("ps", tb)])
    P.op("dve", lambda e: e.tensor_copy(out=pT[:], in_=bank(tb).rearrange("p (c k) -> p c k", k=64)[:, 0:4, 0:NPAR]),
         reads=[("ps", tb)], writes=["pT"])
    PS.release(tb)

    def rstd_ops(ss, ss_r, scale):
        t, t_r = new_st()
        r, r_r = new_st()
        P.op("pool", lambda e: e.tensor_scalar(out=t, in0=ss, scalar1=scale, scalar2=EPS, op0=ALU.mult, op1=ALU.add),
             reads=[ss_r], writes=[t_r])
        P.op("pool", lambda e: e.tensor_tensor(out=r, in0=t, in1=neghalf[:, 0:1], op=ALU.pow),
             reads=[t_r, "neghalf"], writes=[r_r])
        return r, r_r

    def rstd4_ops(ss4, ss4_r, n, scale, add=None):
        t, t_r = new_st4()
        r, r_r = new_st4()
        src, src_rs = ss4, [ss4_r]
        if add is not None:
            a4, a4_r = add
            u, u_r = new_st4()
            P.op("pool", lambda e: e.tensor_tensor(out=u[:, 0:n], in0=ss4[:, 0:n], in1=a4[:, 0:n], op=ALU.add),
                 reads=[ss4_r, a4_r], writes=[u_r])
            src, src_rs = u, [u_r]
        P.op("pool", lambda e: e.tensor_scalar(out=t[:, 0:n], in0=src[:, 0:n], scalar1=scale, scalar2=EPS, op0=ALU.mult, op1=ALU.add),
             reads=src_rs, writes=[t_r])
        P.op("pool", lambda e: e.tensor_tensor(out=r[:, 0:n], in0=t[:, 0:n], in1=neghalf[:, 0:1].broadcast_to([128, n]), op=ALU.pow),
             reads=[t_r, "neghalf"], writes=[r_r])
        return r, r_r

    def norm_elem(src, src_r, gi, xn, xn_r):
        ss, ss_r = new_st()
        P.op("act", lambda e: e.activation(out=xn[:], in_=src, func=AF.Square, accum_out=ss),
             reads=[src_r], writes=[ss_r, xn_r])
        r, r_r = rstd_ops(ss, ss_r, 1.0 / D)
        P.op("dve", lambda e: e.scalar_tensor_tensor(out=xn[:], in0=src, scalar=r, in1=gb_t[gi][:],
                                                     op0=ALU.mult, op1=ALU.mult),
             reads=[src_r, r_r, ("gb", gi)], writes=[xn_r])

    def norm_tr(xn, xn_r, dstT, dst_name, tt):
        b = PS.alloc1()
        pb = bank(b).bitcast(BF16)

        def _tr(e):
            ins = None
            for c in range(NDC):
                ins = e.transpose(out=pb[:, c * 128:(c + 1) * 128], in_=xn[:, c * 128:(c + 1) * 128], identity=ident_b[:])
            return ins
        P.op("pe", _tr, reads=[xn_r, "ident_b"], writes=[("ps", b)])
        P.op("act", lambda e: e.activation(out=dstT[:, :, tt * 128:(tt + 1) * 128],
                                           in_=pb.rearrange("p (c t) -> p c t", t=128), func=AF.Copy),
             reads=[("ps", b)], writes=[(dst_name, tt)])
        PS.release(b)

    class Blk:
        pass

    def make_blocks():
        blks = []
        for s in range(NP):
            nb = SEQ // 512
            for k in range(nb):
                b = Blk()
                b.kind = "p"
                b.nseg, b.L, b.TT = 1, 512, 4
                b.first, b.last = (k == 0), (k == nb - 1)
                b.seqs = [s]
                b.xrows = [x_p[s, k * 512 + t * 128:k * 512 + (t + 1) * 128, :] for t in range(4)]
                b.yrows = [y_p[s, k * 512 + t * 128:k * 512 + (t + 1) * 128, :] for t in range(4)]
                blks.append(b)
        xs = x_s.rearrange("s t d -> (s t) d")
        ys = y_s.rearrange("s t d -> (s t) d")
        spb = 512 // LS
        for k0 in range(0, NS, spb):
            b = Blk()
            b.kind = "s"
            b.nseg = min(spb, NS - k0)
            b.L = LS
            b.TT = b.nseg * LS // 128
            b.first, b.last = True, True
            b.seqs = list(range(k0, k0 + b.nseg))
            b.xrows = [xs[k0 * LS + t * 128:k0 * LS + (t + 1) * 128, :] for t in range(b.TT)]
            b.yrows = [ys[k0 * LS + t * 128:k0 * LS + (t + 1) * 128, :] for t in range(b.TT)]
            blks.append(b)
        for i, b in enumerate(blks):
            b.bi = i
            b.buf = i % 2
        return blks

    blocks = make_blocks()
    import os as _os
    if _os.environ.get('KDBG_BLK'):
        blocks = [blocks[int(t)] for t in _os.environ['KDBG_BLK'].split(',')]
        for i_, b_ in enumerate(blocks):
            b_.bi = i_
            b_.buf = i_ % 2

    def aview(buf, c, b, lo, n, hist):
        w = hist + b.L
        v = buf[:, c, 0:b.nseg * w].rearrange("p (s l) -> p s l", s=b.nseg)
        return v[:, :, lo:lo + n]

    def nview(ap2d, b):
        return ap2d.rearrange("p (s l) -> p s l", s=b.nseg)

    def load_x(b):
        for tt in range(b.TT):
            P.op("sp", lambda e, tt=tt: e.dma_start(out=x_t[b.buf][:, tt, :], in_=b.xrows[tt]),
                 writes=[("x", b.buf, tt)], dma_slot=("x", b.buf, tt))

    def phase_hist(b):
        if not b.first:
            return
        if b.kind == "p":
            for c in range(4):
                P.op("pool", lambda e, c=c: e.memset(aview(a_ext, c, b, 0, KA - 1, KA - 1), 0.0), writes=[("ahist", c)])
                P.op("pool", lambda e, c=c: e.memset(aview(cv_ext, c, b, 0, KB - 1, KB - 1), 0.0), writes=[("bhist", c)])
            return
        s0, ns = b.seqs[0], b.nseg
        for (cache, K1, buf, hname) in ((cache_a, KA - 1, a_ext, "ahist"), (cache_b, KB - 1, cv_ext, "bhist")):
            stg, stg_r = new_tmp()
            rows = ns * K1
            P.op("sp", lambda e, stg=stg, cache=cache, rows=rows: e.dma_start(
                out=stg[0:rows, :], in_=cache[s0:s0 + ns].rearrange("s t c -> (s t) c")),
                writes=[stg_r], dma_slot=("cst", hname))
            tb1 = PS.alloc1()

            def _tr(e, stg=stg, rows=rows, tb1=tb1):
                ins = None
                for c in range(4):
                    ins = e.transpose(out=bank(tb1)[:, c * 128:c * 128 + rows], in_=stg[0:rows, c * 128:(c + 1) * 128],
                                      identity=ident_f[0:rows, 0:rows])
                return ins
            P.op("pe", _tr, reads=[stg_r, "ident_f"], writes=[("ps", tb1)])
            for c in range(4):
                P.op("dve", lambda e, c=c, tb1=tb1, rows=rows, K1=K1, buf=buf: e.tensor_copy(
                    out=aview(buf, c, b, 0, K1, K1),
                    in_=bank(tb1)[:, c * 128:c * 128 + rows].rearrange("p (s k) -> p s k", k=K1)),
                    reads=[("ps", tb1)], writes=[(hname, c)])
            PS.release(tb1)

    def phase0_elem(b):
        xs_ = []
        ss4, ss4_r = new_st4()
        for tt in range(b.TT):
            xn, xn_r = xn_t[tt], ("xn", tt)
            P.op("act", lambda e, tt=tt, xn=xn: e.activation(out=xn[:], in_=x_t[b.buf][:, tt, :], func=AF.Square,
                                                            accum_out=ss4[:, tt:tt + 1]),
                 reads=[("x", b.buf, tt)], writes=[ss4_r, xn_r])
            xs_.append((xn, xn_r))
        r4, r4_r = rstd4_ops(ss4, ss4_r, b.TT, 1.0 / D)
        for tt in range(b.TT):
            xn, xn_r = xs_[tt]
            P.op("dve", lambda e, tt=tt, xn=xn: e.scalar_tensor_tensor(
                out=xn[:], in0=x_t[b.buf][:, tt, :], scalar=r4[:, tt:tt + 1], in1=gb_t[0][:], op0=ALU.mult, op1=ALU.mult),
                reads=[("x", b.buf, tt), r4_r, ("gb", 0)], writes=[xn_r])
        b.xn = xs_

    def phase0_tr(b):
        for tt in range(b.TT):
            norm_tr(b.xn[tt][0], b.xn[tt][1], xnT, "xnT", tt)

    def win_mm(b, slot, slot_r, q):
        N = b.TT * 128
        bk = PS.alloc1()
        wv = slot.rearrange("p (q dc e) -> p q dc e", q=2, dc=NDC)

        def _mm(e):
            ins = None
            for dc in range(NDC):
                ins = e.matmul(out=bank(bk, N), lhsT=wv[:, q, dc, :], rhs=xnT[:, dc, 0:N], start=(dc == 0), stop=(dc == NDC - 1))
            return ins
        P.op("pe", _mm, reads=[slot_r] + [("xnT", t) for t in range(b.TT)], writes=[("ps", bk)])
        return bk

    def _get_piece(b, i, pin=False):
        if not hasattr(b, "pieces"):
            b.pieces = {}
        if i not in b.pieces:
            b.pieces[i] = wload(ws_in[i], res_ws["in"][0:8] if i < 4 else res_ws["in"], pin=pin)

    def _get_chunk(b, k):
        i, q = divmod(k, 2)
        _get_piece(b, i)
        slot, slot_r = b.pieces[i]
        if q == 1:
            ring_pinned.discard(slot_r[1])
        return win_mm(b, slot, slot_r, q)

    def phaseA1p1(b):
        N = b.TT * 128
        get_chunk = lambda k: _get_chunk(b, k)
        for c in range(4):
            bv = get_chunk(2 * c)
            bg = get_chunk(2 * c + 1)
            sg, sg_r = new_tmp()
            P.op("act", lambda e, bg=bg, sg=sg: e.activation(out=sg[:, 0:N], in_=bank(bg, N), func=AF.Sigmoid),
                 reads=[("ps", bg)], writes=[sg_r])
            P.op("dve", lambda e, bv=bv, sg=sg, c=c: e.tensor_tensor(
                out=aview(a_ext, c, b, KA - 1, b.L, KA - 1), in0=nview(bank(bv, N), b), in1=nview(sg[:, 0:N], b), op=ALU.mult),
                reads=[("ps", bv), sg_r], writes=[("abody", c)])
            PS.release(bv, bg)
            yield

    def phaseA1p2(b):
        N = b.TT * 128
        get_chunk = lambda k: _get_chunk(b, k)
        for c in range(4):
            bgc = get_chunk(8 + 3 * c)
            bvv = get_chunk(8 + 3 * c + 1)
            bgb = get_chunk(8 + 3 * c + 2)
            vs, vs_r = new_tmp()
            P.op("act", lambda e, bvv=bvv, vs=vs: e.activation(out=vs[:, 0:N], in_=bank(bvv, N), func=AF.Copy),
                 reads=[("ps", bvv)], writes=[vs_r])
            P.op("dve", lambda e, bgc=bgc, vs=vs, c=c: e.tensor_tensor(
                out=aview(cv_ext, c, b, KB - 1, b.L, KB - 1), in0=nview(bank(bgc, N), b), in1=nview(vs[:, 0:N], b), op=ALU.mult),
                reads=[("ps", bgc), vs_r], writes=[("bbody", c)])
            u, u_r = new_tmp()
            P.op("dve", lambda e, u=u, c=c: e.tensor_scalar(
                out=nview(u[:, 0:N], b), in0=aview(cv_ext, c, b, 0, b.L, KB - 1), scalar1=pT[:, c, 34:35], scalar2=None, op0=ALU.mult),
                reads=[("bbody", c), ("bhist", c), "pT"], writes=[u_r])
            for k in range(1, KB):
                P.op("dve", lambda e, u=u, c=c, k=k: e.scalar_tensor_tensor(
                    out=nview(u[:, 0:N], b), in0=aview(cv_ext, c, b, k, b.L, KB - 1), scalar=pT[:, c, 34 + k:35 + k],
                    in1=nview(u[:, 0:N], b), op0=ALU.mult, op1=ALU.add),
                    reads=[("bbody", c), ("bhist", c), "pT", u_r], writes=[u_r])
            P.op("dve", lambda e, u=u, bgb=bgb, c=c: e.tensor_tensor(
                out=mix(b, 4 + c)[:, 0:N], in0=bank(bgb, N), in1=u[:, 0:N], op=ALU.mult),
                reads=[("ps", bgb), u_r], writes=[mix_r(b, 4 + c)])
            PS.release(bgc, bvv, bgb)
            yield

    def phase_conv(b):
        N = b.TT * 128
        acv = [nview(ac_t[:, c, 0:N], b) for c in range(4)]
        for c in range(4):
            P.op("dve", lambda e, c=c: e.tensor_scalar(
                out=acv[c], in0=aview(a_ext, c, b, 0, b.L, KA - 1), scalar1=pT[:, c, 0:1], scalar2=pT[:, c, 31:32],
                op0=ALU.mult, op1=ALU.add),
                reads=[("abody", c), ("ahist", c), "pT"], writes=[("ac", c)])
        yield
        for k in range(1, KA):
            for c in range(4):
                P.op("dve", lambda e, c=c, k=k: e.scalar_tensor_tensor(
                    out=acv[c], in0=aview(a_ext, c, b, k, b.L, KA - 1), scalar=pT[:, c, k:k + 1], in1=acv[c],
                    op0=ALU.mult, op1=ALU.add),
                    reads=[("abody", c), ("ahist", c), "pT", ("ac", c)], writes=[("ac", c)])
            yield

    def phase_tail(b, which):
        buf, K1, body, hist = (a_ext, KA - 1, "abody", "ahist") if which == "a" else (cv_ext, KB - 1, "bbody", "bhist")
        if b.last:
            if which == "a":
                dst = na_p if b.kind == "p" else na_s
            else:
                dst = nb_p if b.kind == "p" else nb_s
            for si, s in enumerate(b.seqs):
                tb1 = PS.alloc1()

                def _t1(e, si=si, tb1=tb1):
                    ins = None
                    for c in range(4):
                        ins = e.transpose(out=bank(tb1)[0:K1, c * 128:(c + 1) * 128],
                                          in_=aview(buf, c, b, b.L, K1, K1)[:, si, :], identity=ident_f[:])
                    return ins
                P.op("pe", _t1, reads=[(body, c) for c in range(4)] + [(hist, c) for c in range(4)] + ["ident_f"],
                     writes=[("ps", tb1)])
                og, og_r = new_tmp()
                P.op("act", lambda e, tb1=tb1, og=og: e.activation(out=og[0:K1, :], in_=bank(tb1)[0:K1, :], func=AF.Copy),
                     reads=[("ps", tb1)], writes=[og_r])
                PS.release(tb1)
                P.op("sp", lambda e, s=s, og=og: e.dma_start(out=dst[s, :, :], in_=og[0:K1, :]),
                     reads=[og_r], writes=[("out", which, b.kind, s)], dma_slot=("o" + which, si))
        else:
            for c in range(4):
                P.op("pool", lambda e, c=c: e.tensor_copy(out=buf[:, c, 0:K1], in_=buf[:, c, b.L:b.L + K1]),
                     reads=[(body, c), (hist, c)], writes=[(hist, c)])

    def phaseA2_stats(b):
        N = b.TT * 128
        bm = PS.alloc_pair()
        be = bm + 1
        b.bm, b.be = bm, be
        sqs = []
        for c in range(4):
            sq, sq_r = new_tmp()
            P.op("act", lambda e, c=c, sq=sq: e.activation(out=sq[:, 0:N], in_=ac_t[:, c, 0:N], func=AF.Square),
                 reads=[("ac", c)], writes=[sq_r])
            sqs.append((sq, sq_r))

        def _m1(e):
            ins = None
            for c in range(4):
                ins = e.matmul(out=bank(bm, N), lhsT=ones_f[:], rhs=ac_t[:, c, 0:N], start=(c == 0), stop=(c == 3))
            return ins
        P.op("pe", _m1, reads=[("ac", c) for c in range(4)] + ["ones_f"], writes=[("ps", bm)])

        def _m2(e):
            ins = None
            for c in range(4):
                ins = e.matmul(out=bank(be, N), lhsT=ones_f[:], rhs=sqs[c][0][:, 0:N], start=(c == 0), stop=(c == 3))
            return ins
        P.op("pe", _m2, reads=[s_[1] for s_ in sqs] + ["ones_f"], writes=[("ps", be)])

    def phaseA2_norm(b):
        N = b.TT * 128
        bm, be = b.bm, b.be
        msq, msq_r = new_tmp()
        P.op("act", lambda e: e.activation(out=msq[:, 0:N], in_=bank(bm, N), func=AF.Square), reads=[("ps", bm)], writes=[msq_r])
        var, var_r = new_tmp()
        P.op("dve", lambda e: e.tensor_tensor(out=var[:, 0:N], in0=bank(be, N), in1=msq[:, 0:N], op=ALU.subtract),
             reads=[("ps", be), msq_r], writes=[var_r])
        P.op("act", lambda e: e.activation(out=var[:, 0:N], in_=var[:, 0:N], func=AF.Ln, bias=eps_t[:, 0:1]),
             reads=[var_r, "eps_t"], writes=[var_r])
        rs, rs_r = new_tmp(pin=True)
        P.op("act", lambda e: e.activation(out=rs[:, 0:N], in_=var[:, 0:N], func=AF.Exp, scale=-0.5), reads=[var_r], writes=[rs_r])
        PS.release(be)
        for c in range(4):
            z, z_r = new_tmp()
            P.op("dve", lambda e, c=c, z=z: e.tensor_tensor(out=z[:, 0:N], in0=ac_t[:, c, 0:N], in1=bank(bm, N), op=ALU.subtract),
                 reads=[("ac", c), ("ps", bm)], writes=[z_r])
            P.op("dve", lambda e, z=z: e.tensor_tensor(out=z[:, 0:N], in0=z[:, 0:N], in1=rs[:, 0:N], op=ALU.mult),
                 reads=[z_r, rs_r], writes=[z_r])
            P.op("act", lambda e, c=c, z=z: e.activation(out=mix(b, c)[:, 0:N], in_=z[:, 0:N], func=AF.Silu,
                                                        scale=pT[:, c, 32:33], bias=pT[:, c, 33:34]),
                 reads=[z_r, "pT"], writes=[("mixT", c)])
            if c == 3:
                PS.release(bm)
                tmp_pinned.discard(rs_r[1])
            yield

    def phaseB(b):
        slots = b.wout_slots
        b.hn = [(hn_t[tt], ("hn", tt)) for tt in range(b.TT)]
        pend = {}

        def stage1(tt):
            pb = PS.alloc_pair()

            def _mm(e, tt=tt, pb=pb):
                ins = None
                for half in range(2):
                    for cc in range(NDC):
                        sl = slots[half * 2 + cc // 4][0].rearrange("p (cc d) -> p cc d", cc=4)
                        ins = e.matmul(out=bank(pb + half), lhsT=mix(b, cc)[:, tt * 128:(tt + 1) * 128], rhs=sl[:, cc % 4, :],
                                       start=(cc == 0), stop=(cc == NDC - 1))
                return ins
            P.op("pe", _mm, reads=[s_[1] for s_ in slots] + [mix_r(b, cc) for cc in range(NDC)],
                 writes=[("ps", pb), ("ps", pb + 1)])
            ss, ss_r = new_st()
            P.op("act", lambda e, pb=pb, ss=ss, tt=tt: e.activation(out=hn_t[tt][:], in_=bank(pb, 1024), func=AF.Square, accum_out=ss),
                 reads=[("ps", pb), ("ps", pb + 1)], writes=[ss_r, ("hn", tt)])
            r, r_r = rstd_ops(ss, ss_r, 1.0 / D)
            P.op("dve", lambda e, tt=tt, pb=pb, r=r: e.scalar_tensor_tensor(
                out=h_t[:, tt, :], in0=bank(pb, 1024), scalar=r, in1=gb_t[1][:], op0=ALU.mult, op1=ALU.mult),
                reads=[("ps", pb), ("ps", pb + 1), r_r, ("gb", 1)], writes=[("h", tt)])
            PS.release(pb, pb + 1)
            P.op("dve", lambda e, tt=tt: e.tensor_tensor(out=h_t[:, tt, :], in0=h_t[:, tt, :], in1=x_t[b.buf][:, tt, :], op=ALU.add),
                 reads=[("h", tt), ("x", b.buf, tt)], writes=[("h", tt)])
            ss2, ss2_r = new_st()
            P.op("act", lambda e, tt=tt, ss2=ss2: e.activation(out=hn_t[tt][:], in_=h_t[:, tt, :], func=AF.Square, accum_out=ss2),
                 reads=[("h", tt)], writes=[ss2_r, ("hn", tt)])
            pend[tt] = rstd_ops(ss2, ss2_r, 1.0 / D)

        def stage2(tt):
            r2, r2_r = pend[tt]
            P.op("dve", lambda e, tt=tt, r2=r2: e.scalar_tensor_tensor(
                out=hn_t[tt][:], in0=h_t[:, tt, :], scalar=r2, in1=gb_t[2][:], op0=ALU.mult, op1=ALU.mult),
                reads=[("h", tt), r2_r, ("gb", 2)], writes=[("hn", tt)])

        for tt in range(b.TT):
            stage1(tt)
            if tt == b.TT - 1:
                for s_ in slots:
                    ring_pinned.discard(s_[1][1])
            if tt > 0:
                stage2(tt - 1)
            yield
        stage2(b.TT - 1)
        yield

    def preload_wout(b):
        b.wout_slots = [wload(ws_out[i], res_ws["out"], pin=True) for i in range(4)]

    def phaseC(b):
        for tt in range(b.TT):
            norm_tr(b.hn[tt][0], b.hn[tt][1], hnT, "hnT", tt)

    def phaseD(b):
        N = b.TT * 128
        for j in range(NFC):
            slot, slot_r = wload(ws_gu[j], res_ws["gu"])
            wv = slot.rearrange("p (gu dc e) -> p gu dc e", gu=2, dc=NDC)
            pb = PS.alloc_pair()

            def _mm(e, wv=wv, pb=pb):
                ins = None
                for gu in range(2):
                    for dc in range(NDC):
                        ins = e.matmul(out=bank(pb + gu, N), lhsT=wv[:, gu, dc, :], rhs=hnT[:, dc, 0:N],
                                       start=(dc == 0), stop=(dc == NDC - 1))
                return ins
            P.op("pe", _mm, reads=[slot_r] + [("hnT", t) for t in range(b.TT)], writes=[("ps", pb), ("ps", pb + 1)])
            sg, sg_r = new_tmp()
            P.op("act", lambda e, pb=pb, sg=sg: e.activation(out=sg[:, 0:N], in_=bank(pb, N), func=AF.Silu),
                 reads=[("ps", pb)], writes=[sg_r])
            P.op("dve", lambda e, pb=pb, sg=sg, j=j: e.tensor_tensor(out=actT[:, j, 0:N], in0=bank(pb + 1, N), in1=sg[:, 0:N], op=ALU.mult),
                 reads=[("ps", pb + 1), sg_r], writes=[("actT", j)])
            PS.release(pb, pb + 1)
            yield

    def phaseE_half(b, half):
        banks = []
        for _ in range((b.TT + 1) // 2):
            p_ = PS.alloc_pair()
            banks += [p_, p_ + 1]
        for extra in banks[b.TT:]:
            PS.release(extra)
        banks = banks[:b.TT]
        for q in range(6):
            nfc = 4 if q < 5 else 2
            slot, slot_r = wload(ws_dn[half * 6 + q], res_ws["dn"], cols=nfc * 512)
            sl = slot.rearrange("p (fc d) -> p fc d", fc=4)

            def _mm(e, q=q, nfc=nfc, sl=sl):
                ins = None
                for f in range(nfc):
                    fc = 4 * q + f
                    for tt in range(b.TT):
                        ins = e.matmul(out=bank(banks[tt]), lhsT=actT[:, fc, tt * 128:(tt + 1) * 128], rhs=sl[:, f, :],
                                       start=(fc == 0), stop=(fc == NFC - 1))
                return ins
            P.op("pe", _mm, reads=[slot_r] + [("actT", 4 * q + f) for f in range(nfc)],
                 writes=[("ps", bk) for bk in banks])
            yield
        if half == 0:
            ss0, ss0_r = new_st4()
            for tt in range(b.TT):
                jk, jk_r = new_tmp()
                P.op("act", lambda e, tt=tt, jk=jk: e.activation(out=jk[:], in_=bank(banks[tt]), func=AF.Square,
                                                                accum_out=ss0[:, tt:tt + 1]),
                     reads=[("ps", banks[tt])], writes=[ss0_r, jk_r])
                P.op("act", lambda e, tt=tt: e.activation(out=cv_ext[:, tt, 2:514], in_=bank(banks[tt]), func=AF.Copy),
                     reads=[("ps", banks[tt]), ss0_r], writes=[("bbody", tt)])
            b.ss0 = (ss0, ss0_r)
            PS.release(*banks)
        else:
            b.ebanks = banks
        yield

    def phaseE_epi(b):
        banks = b.ebanks
        ss1, ss1_r = new_st4()
        for tt in range(b.TT):
            jk, jk_r = new_tmp()
            P.op("act", lambda e, tt=tt, jk=jk: e.activation(out=jk[:], in_=bank(banks[tt]), func=AF.Square,
                                                            accum_out=ss1[:, tt:tt + 1]),
                 reads=[("ps", banks[tt])], writes=[ss1_r, jk_r])
        r4, r4_r = rstd4_ops(ss1, ss1_r, b.TT, 1.0 / D, add=b.ss0)
        for tt in range(b.TT):
            r = r4[:, tt:tt + 1]
            for half in range(2):
                t, t_r = new_tmp()
                if half == 0:
                    P.op("dve", lambda e, tt=tt, t=t, r=r: e.scalar_tensor_tensor(
                        out=t[:], in0=cv_ext[:, tt, 2:514], scalar=r, in1=gb_t[3][:, 0:512], op0=ALU.mult, op1=ALU.mult),
                        reads=[("bbody", tt), r4_r, ("gb", 3)], writes=[t_r])
                else:
                    P.op("dve", lambda e, tt=tt, t=t, r=r: e.scalar_tensor_tensor(
                        out=t[:], in0=bank(banks[tt]), scalar=r, in1=gb_t[3][:, 512:1024], op0=ALU.mult, op1=ALU.mult),
                        reads=[("ps", banks[tt]), r4_r, ("gb", 3)], writes=[t_r])
                P.op("dve", lambda e, tt=tt, half=half, t=t: e.tensor_tensor(
                    out=h_t[:, tt, half * 512:(half + 1) * 512], in0=h_t[:, tt, half * 512:(half + 1) * 512], in1=t[:], op=ALU.add),
                    reads=[("h", tt), t_r], writes=[("h", tt)])
            PS.release(banks[tt])
            P.op("sp", lambda e, tt=tt: e.dma_start(out=b.yrows[tt], in_=h_t[:, tt, :]),
                 reads=[("h", tt)], writes=[("out", "y", b.bi, tt)], dma_slot=("y", tt))

    def run(*gens):
        for g in gens:
            if g is not None:
                for _ in g:
                    pass

    def chain(*gens):
        for g in gens:
            if g is not None:
                yield from g

    def interleave(ga, gb, on_b_done=None):
        da = db = False
        while not (da and db):
            if not da:
                try:
                    next(ga)
                except StopIteration:
                    da = True
            if not db:
                try:
                    next(gb)
                except StopIteration:
                    db = True
                    if on_b_done is not None:
                        on_b_done()

    nblk = len(blocks)
    import os as _os
    _stop = int(_os.environ.get("KDBG_STOP", "999"))

    def driver():
        load_x(blocks[0])
        if nblk > 1:
            load_x(blocks[1])
        b0 = blocks[0]
        phase0_elem(b0)
        phase_hist(b0)
        prep_in_out()
        phase0_tr(b0)
        run(phaseA1p1(b0))
        if nblk > 1:
            phase0_elem(blocks[1])
        prep_gu_dn()

        def first_ffn():
            yield from phaseA1p2(b0)
            phase_tail(b0, "b")
            yield
        for i in range(nblk + 1):
            b = blocks[i] if i < nblk else None
            pb_ = blocks[i - 1] if i > 0 else None
            nb_ = blocks[i + 1] if i + 1 < nblk else None
            if pb_ is not None:
                phaseC(pb_)
            if nb_ is not None and i > 0:
                phase0_tr(nb_)
            ffn = chain(phaseD(pb_), phaseE_half(pb_, 0), phaseE_half(pb_, 1)) if pb_ is not None else first_ffn()
            conv = phase_conv(b) if b is not None else iter(())

            def after_conv(b=b):
                if b is not None:
                    phase_tail(b, "a")
                    phaseA2_stats(b)
                    run(phaseA2_norm(b))
            interleave(ffn, conv, on_b_done=after_conv)
            if nb_ is not None and i == 0:
                phase0_tr(nb_)
            if b is not None:
                preload_wout(b)
            if nb_ is not None:
                _get_piece(nb_, 0, pin=True)
                _get_piece(nb_, 1, pin=True)
            if pb_ is not None:
                phaseE_epi(pb_)
            if b is None:
                break
            if nb_ is not None:
                phase_hist(nb_)
            interleave(phaseB(b), phaseA1p1(nb_) if nb_ is not None else iter(()))
            if nb_ is not None:
                run(phaseA1p2(nb_))
                phase_tail(nb_, "b")
            if i + 2 < nblk:
                load_x(blocks[i + 2])
                phase0_elem(blocks[i + 2])

    driver()

    out_res = [r for r in P.lastw if isinstance(r, tuple) and r and r[0] == "out"]
    P.op("sp", lambda e: e.nop(), reads=out_res)

    P.finalize()
    dma_slots = list(P.dma_cnt.keys())
    esem = {e: es.enter_context(nc.semaphore(f"sem_{e}")) for e in Prog.ENGS}
    dsem = {s: es.enter_context(nc.semaphore(f"dsem_{i}")) for i, s in enumerate(dma_slots)}
    with nc.Block() as block:
        @block.tensor
        def _(e):
            P.emit_engine("pe", e, esem, dsem)

        @block.scalar
        def _(e):
            P.emit_engine("act", e, esem, dsem)

        @block.vector
        def _(e):
            P.emit_engine("dve", e, esem, dsem)

        @block.gpsimd
        def _(e):
            P.emit_engine("pool", e, esem, dsem)

        @block.sync
        def _(e):
            P.emit_engine("sp", e, esem, dsem)
    es.close()
    return nc


def make_in_maps(n_cores, inputs, NP, NS):
    f = lambda a: np.ascontiguousarray(np.asarray(a, dtype=np.float32))
    ident = np.eye(128, dtype=np.float32)
    maps = []
    for c in range(n_cores):
        m = {
            "x_p": f(inputs["x_prompt"][c * NP:(c + 1) * NP]),
            "x_s": f(inputs["x_sample"][c * NS:(c + 1) * NS]),
            "cache_a": f(inputs["cache_conv_a"][0, c * NS:(c + 1) * NS]),
            "cache_b": f(inputs["cache_conv_b"][0, c * NS:(c + 1) * NS]),
            "g_pre1": f(inputs["norm_mix_pre"]),
            "g_post1": f(inputs["norm_mix_post"]),
            "g_pre2": f(inputs["norm_ffn_pre"]),
            "g_post2": f(inputs["norm_ffn_post"]),
            "w_in": f(inputs["w_in"][0]),
            "w_out": f(inputs["w_out"][0]),
            "w_gu": f(inputs["w_gate_up"][0]),
            "w_dn": f(inputs["w_down"][0]),
            "conv_a_w": f(inputs["conv_a_w"][0]),
            "conv_a_b": f(inputs["conv_a_b"]),
            "ln_g": f(inputs["conv_a_ln_g"]),
            "ln_b": f(inputs["conv_a_ln_b"]),
            "conv_b_w": f(inputs["conv_b_w"][0]),
            "ident": ident,
        }
        maps.append(m)
    return maps


def gather(results, n_cores):
    cat = lambda k: np.concatenate([np.asarray(results[c][k], dtype=np.float32) for c in range(n_cores)], axis=0)
    return (cat("y_p"), cat("y_s"), cat("na_p")[None], cat("nb_p")[None], cat("na_s")[None], cat("nb_s")[None])


def kernel(**inputs):
    B, SEQ = inputs["x_prompt"].shape[0], inputs["x_prompt"].shape[1]
    BS, LS = inputs["x_sample"].shape[0], inputs["x_sample"].shape[1]
    NP, NS = B // N_CORES, BS // N_CORES
    nc = build_program(NP, SEQ, NS, LS)
    in_maps = make_in_maps(N_CORES, inputs, NP, NS)
    res = run_bass_kernel_spmd(nc, in_maps, core_ids=list(range(N_CORES)))
    return gather(res.results, N_CORES)
```

```python
import numpy as np
import concourse.bass as bass
import concourse.mybir as mybir
from concourse.bass_utils import run_bass_kernel_spmd

F32 = mybir.dt.float32
BF16 = mybir.dt.bfloat16
AF = mybir.ActivationFunctionType
ALU = mybir.AluOpType

D = 1024
DA = 512
DB = 512
DIN = 2560
DFF = 2816
KA = 31
KB = 3
EPS = 1e-6
NDC = 8
NFC = 22
NPAR = 37
RING = 8
NTMP = 9
N_CORES = 8

E_ORDER = [0, 4, 1, 5, 2, 6, 3, 7, 12, 16, 8, 13, 17, 9, 14, 18, 10, 15, 19, 11]


class _Op:
    __slots__ = ("eng", "fn", "deps", "idx", "signal", "dma_slot", "dma_val", "waits", "sigval")


class Prog:
    ENGS = ("pe", "act", "dve", "pool", "sp")

    def __init__(self):
        self.ops = {e: [] for e in self.ENGS}
        self.lastw = {}
        self.readers = {}
        self.dma_cnt = {}

    def op(self, eng, fn, reads=(), writes=(), dma_slot=None):
        o = _Op()
        o.eng = eng
        o.fn = fn
        o.idx = len(self.ops[eng])
        o.signal = False
        deps = []
        for r in reads:
            w = self.lastw.get(r)
            if w is not None:
                deps.append(w)
        for r in writes:
            w = self.lastw.get(r)
            if w is not None:
                deps.append(w)
            deps.extend(self.readers.get(r, ()))
        o.deps = deps
        if dma_slot is not None:
            c = self.dma_cnt.get(dma_slot, 0) + 1
            self.dma_cnt[dma_slot] = c
            o.dma_slot = dma_slot
            o.dma_val = 16 * c
            tok = ("dma", dma_slot, 16 * c)
        else:
            o.dma_slot = None
            o.dma_val = 0
            tok = ("eng", eng, o.idx)
        for r in reads:
            self.readers.setdefault(r, []).append(tok)
        for r in writes:
            self.lastw[r] = tok
            self.readers[r] = []
        self.ops[eng].append(o)
        return o

    def finalize(self):
        for e in self.ENGS:
            seen = {}
            for o in self.ops[e]:
                need_eng = {}
                need_dma = {}
                for d in o.deps:
                    if d[0] == "eng":
                        _, de, di = d
                        if de == e:
                            if e in ("pe", "sp"):
                                continue
                        if seen.get(de, -1) >= di:
                            continue
                        if need_eng.get(de, -1) < di:
                            need_eng[de] = di
                    else:
                        _, slot, val = d
                        if seen.get(("dma", slot), 0) >= val:
                            continue
                        if need_dma.get(slot, 0) < val:
                            need_dma[slot] = val
                waits = []
                for de, di in need_eng.items():
                    seen[de] = di
                    waits.append(("eng", de, di))
                    self.ops[de][di].signal = True
                for slot, val in need_dma.items():
                    seen[("dma", slot)] = val
                    waits.append(("dma", slot, val))
                o.waits = waits
        for e in self.ENGS:
            c = 0
            for o in self.ops[e]:
                if o.signal:
                    c += 1
                o.sigval = c

    def emit_engine(self, e, eng, esem, dsem):
        for o in self.ops[e]:
            for w in o.waits:
                if w[0] == "eng":
                    eng.wait_ge(esem[w[1]], self.ops[w[1]][w[2]].sigval)
                else:
                    eng.wait_ge(dsem[w[1]], w[2])
            ins = o.fn(eng)
            if o.dma_slot is not None:
                ins.then_inc(dsem[o.dma_slot], 16)
            elif o.signal:
                ins.then_inc(esem[e], 1)


class _Psum:
    def __init__(self):
        self.held = [False] * 8
        self.stamp = [0] * 8
        self.clock = 0

    def alloc1(self):
        free = [b for b in range(8) if not self.held[b]]
        assert free, "out of PSUM banks"
        b = min(free, key=lambda k: self.stamp[k])
        self.held[b] = True
        return b

    def alloc_pair(self):
        free = [b for b in range(0, 8, 2) if not self.held[b] and not self.held[b + 1]]
        assert free, "out of PSUM bank pairs"
        b = min(free, key=lambda k: max(self.stamp[k], self.stamp[k + 1]))
        self.held[b] = self.held[b + 1] = True
        return b

    def release(self, *banks):
        for b in banks:
            assert self.held[b]
            self.held[b] = False
            self.clock += 1
            self.stamp[b] = self.clock


def build_program(NP, SEQ, NS, LS=64):
    assert SEQ % 512 == 0 and (NS * LS) % 128 == 0 and NS * (KA - 1) <= 128
    nc = bass.Bass("TRN2", target_bir_lowering=False)
    P = Prog()
    PS = _Psum()

    def din(name, shape, dt=F32):
        return nc.dram_tensor(name, list(shape), dt, kind="ExternalInput").ap()

    def dout(name, shape, dt=F32):
        return nc.dram_tensor(name, list(shape), dt, kind="ExternalOutput").ap()

    x_p = din("x_p", [NP, SEQ, D])
    x_s = din("x_s", [NS, LS, D])
    cache_a = din("cache_a", [NS, KA - 1, DA])
    cache_b = din("cache_b", [NS, KB - 1, DB])
    g_pre1 = din("g_pre1", [1, D])
    g_post1 = din("g_post1", [1, D])
    g_pre2 = din("g_pre2", [1, D])
    g_post2 = din("g_post2", [1, D])
    w_in = din("w_in", [D, DIN])
    w_out = din("w_out", [D, D])
    w_gu = din("w_gu", [D, 2 * DFF])
    w_dn = din("w_dn", [DFF, D])
    conv_a_w = din("conv_a_w", [KA, DA])
    conv_a_b = din("conv_a_b", [1, DA])
    ln_g = din("ln_g", [1, DA])
    ln_b = din("ln_b", [1, DA])
    conv_b_w = din("conv_b_w", [KB, DB])
    ident_in = din("ident", [128, 128])

    y_p = dout("y_p", [NP, SEQ, D])
    y_s = dout("y_s", [NS, LS, D])
    na_p = dout("na_p", [NP, KA - 1, DA])
    nb_p = dout("nb_p", [NP, KB - 1, DB])
    na_s = dout("na_s", [NS, KA - 1, DA])
    nb_s = dout("nb_s", [NS, KB - 1, DB])

    ws_in = nc.dram_tensor("ws_in", [10, 128, 2048], BF16, kind="Internal").ap()
    ws_out = nc.dram_tensor("ws_out", [4, 128, 2048], BF16, kind="Internal").ap()
    ws_gu = nc.dram_tensor("ws_gu", [NFC, 128, 2048], BF16, kind="Internal").ap()
    ws_dn = nc.dram_tensor("ws_dn", [12, 128, 2048], BF16, kind="Internal").ap()

    from contextlib import ExitStack
    es = ExitStack()

    def sb(name, shape, dt=F32):
        return es.enter_context(nc.sbuf_tensor(name, list(shape), dt))

    ident_f = sb("ident_f", [128, 128])
    ident_b = sb("ident_b", [128, 128], BF16)
    ones_f = sb("ones_f", [128, 128])
    neghalf = sb("neghalf", [128, 1])
    eps_t = sb("eps_t", [128, 1])
    gb_t = [sb(f"gb{i}", [128, D]) for i in range(4)]
    pT = sb("pT", [128, 4, NPAR])
    x_t = [sb(f"x{i}", [128, 4, D]) for i in range(2)]
    h_t = sb("h", [128, 4, D])
    xn_t = [sb(f"xn{i}", [128, D], BF16) for i in range(4)]
    hn_t = [sb(f"hn{i}", [128, D], BF16) for i in range(4)]
    xnT = sb("xnT", [128, NDC, 512], BF16)
    hnT = sb("hnT", [128, NDC, 512], BF16)
    a_ext = sb("a_ext", [128, 4, 544])
    cv_ext = sb("cv_ext", [128, 4, 516])
    ac_t = sb("ac", [128, 4, 512])
    tmp_t = [sb(f"tmp{i}", [128, 512]) for i in range(NTMP)]
    mixT_lo = sb("mixT_lo", [128, 4, 512], BF16)
    mixT_hi = [sb(f"mixT_hi{i}", [128, 4, 512], BF16) for i in range(2)]

    def mix(b, cc):
        return mixT_lo[:, cc] if cc < 4 else mixT_hi[b.buf][:, cc - 4]

    def mix_r(b, cc):
        return ("mixT", cc) if cc < 4 else ("mixT", cc, b.buf)
    actT = sb("actT", [128, NFC, 512], BF16)
    ring = [sb(f"ring{i}", [128, 2048], BF16) for i in range(RING)]
    st = sb("st", [128, 64])
    st4 = sb("st4", [128, 64])
    ps = es.enter_context(nc.psum_tensor("ps", [128, 4096], F32))

    def bank(b, n=512):
        return ps[:, b * 512:b * 512 + n]

    cnt = {"tmp": 0, "st": 0, "ring": 0, "st4": 0}

    tmp_pinned = set()

    def new_tmp(pin=False):
        while True:
            i = cnt["tmp"] % NTMP
            cnt["tmp"] += 1
            if i not in tmp_pinned:
                break
        if pin:
            tmp_pinned.add(i)
        return tmp_t[i], ("tmp", i)

    def new_st():
        i = cnt["st"] % 64
        cnt["st"] += 1
        return st[:, i:i + 1], ("st", i)

    def new_st4():
        i = cnt["st4"] % 16
        cnt["st4"] += 1
        return st4[:, 4 * i:4 * i + 4], ("st4", i)

    ring_pinned = set()

    def wload(src_ap, src_res, cols=2048, pin=False):
        while True:
            i = cnt["ring"] % RING
            cnt["ring"] += 1
            if i not in ring_pinned:
                break
        if pin:
            ring_pinned.add(i)
        dst = ring[i]
        P.op("sp", lambda e, dst=dst, src_ap=src_ap, cols=cols: e.dma_start(out=dst[:, 0:cols], in_=src_ap[:, 0:cols]),
             reads=src_res, writes=[("ring", i)], dma_slot=("ring", i))
        return dst, ("ring", i)

    res_ws = {"in": [], "out": [], "gu": [], "dn": []}
    w_in_v = w_in.rearrange("(dc p) (j e) -> p j dc e", p=128, e=128)
    ws_in_v = ws_in.rearrange("i p (q dc e) -> i p q dc e", q=2, dc=NDC)
    w_out_v = w_out.rearrange("(cc p) d -> p cc d", p=128)
    ws_out_v = ws_out.rearrange("i p (cc d) -> i p cc d", cc=4)
    w_gu_v = w_gu.rearrange("(dc p) (gu j e) -> p gu j dc e", p=128, gu=2, e=128)
    ws_gu_v = ws_gu.rearrange("j p (gu dc e) -> j p gu dc e", gu=2, dc=NDC)
    w_dn_v = w_dn.rearrange("(fc p) d -> p fc d", p=128)
    ws_dn_v = ws_dn.rearrange("i p (fc d) -> i p fc d", fc=4)

    def prep_in_out():
        for k, j in enumerate(E_ORDER):
            i, q = divmod(k, 2)
            r = ("ws", "in", k)
            res_ws["in"].append(r)
            P.op("pool", lambda e, i=i, q=q, j=j: e.dma_start(out=ws_in_v[i, :, q, :, :], in_=w_in_v[:, j, :, :]),
                 writes=[r], dma_slot="prep_in0" if k < 8 else "prep_in1")
        for half in range(2):
            for q in range(2):
                r = ("ws", "out", half * 2 + q)
                res_ws["out"].append(r)
                P.op("pool", lambda e, half=half, q=q: e.dma_start(
                    out=ws_out_v[half * 2 + q], in_=w_out_v[:, 4 * q:4 * q + 4, half * 512:(half + 1) * 512]),
                    writes=[r], dma_slot="prep_out")

    def prep_gu_dn():
        for j in range(NFC):
            for gu in range(2):
                r = ("ws", "gu", j * 2 + gu)
                res_ws["gu"].append(r)
                P.op("pool", lambda e, j=j, gu=gu: e.dma_start(out=ws_gu_v[j, :, gu, :, :], in_=w_gu_v[:, gu, j, :, :]),
                     writes=[r], dma_slot="prep_gu")
        for half in range(2):
            for q in range(6):
                nfc = 4 if q < 5 else 2
                r = ("ws", "dn", half * 6 + q)
                res_ws["dn"].append(r)
                P.op("pool", lambda e, half=half, q=q, nfc=nfc: e.dma_start(
                    out=ws_dn_v[half * 6 + q, :, 0:nfc, :], in_=w_dn_v[:, 4 * q:4 * q + nfc, half * 512:(half + 1) * 512]),
                    writes=[r], dma_slot="prep_dn")


    P.op("sp", lambda e: e.dma_start(out=ident_f[:], in_=ident_in[:, :]), writes=["ident_f"], dma_slot="c0")
    for i, g in enumerate((g_pre1, g_post1, g_pre2, g_post2)):
        P.op("sp", lambda e, i=i, g=g: e.dma_start(out=gb_t[i][:], in_=g[0, :].partition_broadcast(128)),
             writes=[("gb", i)], dma_slot=("c1", i))
    pstage, pstage_r = new_tmp()
    P.op("sp", lambda e: e.dma_start(out=pstage[0:KA, :], in_=conv_a_w[:, :]), writes=[pstage_r], dma_slot="c2")
    P.op("sp", lambda e: e.dma_start(out=pstage[31:32, :], in_=conv_a_b[:, :]), writes=["ps1"], dma_slot="c3")
    P.op("sp", lambda e: e.dma_start(out=pstage[32:33, :], in_=ln_g[:, :]), writes=["ps2"], dma_slot="c4")
    P.op("sp", lambda e: e.dma_start(out=pstage[33:34, :], in_=ln_b[:, :]), writes=["ps3"], dma_slot="c5")
    P.op("sp", lambda e: e.dma_start(out=pstage[34:37, :], in_=conv_b_w[:, :]), writes=["ps4"], dma_slot="c6")
    P.op("dve", lambda e: e.tensor_copy(out=ident_b[:], in_=ident_f[:]), reads=["ident_f"], writes=["ident_b"])
    P.op("dve", lambda e: e.memset(ones_f[:], 1.0 / DA), writes=["ones_f"])
    P.op("dve", lambda e: e.memset(neghalf[:], -0.5), writes=["neghalf"])
    P.op("dve", lambda e: e.memset(eps_t[:], EPS), writes=["eps_t"])

    tb = PS.alloc1()

    def _ptr(e):
        ins = None
        for c in range(4):
            ins = e.transpose(out=bank(tb)[:, c * 64:c * 64 + NPAR], in_=pstage[0:NPAR, c * 128:(c + 1) * 128],
                              identity=ident_f[0:NPAR, 0:NPAR])
        return ins
    P.op("pe", _ptr, reads=[pstage_r, "ps1", "ps2", "ps3", "ps4", "ident_f"], writes=[("ps", tb)])
    P.op("dve", lambda e: e.tensor_copy(out=pT[:], in_=bank(tb).rearrange("p (c k) -> p c k", k=64)[:, 0:4, 0:NPAR]),
         reads=[("ps", tb)], writes=["pT"])
    PS.release(tb)

    def rstd_ops(ss, ss_r, scale):
        t, t_r = new_st()
        r, r_r = new_st()
        P.op("pool", lambda e: e.tensor_scalar(out=t, in0=ss, scalar1=scale, scalar2=EPS, op0=ALU.mult, op1=ALU.add),
             reads=[ss_r], writes=[t_r])
        P.op("pool", lambda e: e.tensor_tensor(out=r, in0=t, in1=neghalf[:, 0:1], op=ALU.pow),
             reads=[t_r, "neghalf"], writes=[r_r])
        return r, r_r

    def rstd4_ops(ss4, ss4_r, n, scale, add=None):
        t, t_r = new_st4()
        r, r_r = new_st4()
        src, src_rs = ss4, [ss4_r]
        if add is not None:
            a4, a4_r = add
            u, u_r = new_st4()
            P.op("pool", lambda e: e.tensor_tensor(out=u[:, 0:n], in0=ss4[:, 0:n], in1=a4[:, 0:n], op=ALU.add),
                 reads=[ss4_r, a4_r], writes=[u_r])
            src, src_rs = u, [u_r]
        P.op("pool", lambda e: e.tensor_scalar(out=t[:, 0:n], in0=src[:, 0:n], scalar1=scale, scalar2=EPS, op0=ALU.mult, op1=ALU.add),
             reads=src_rs, writes=[t_r])
        P.op("pool", lambda e: e.tensor_tensor(out=r[:, 0:n], in0=t[:, 0:n], in1=neghalf[:, 0:1].broadcast_to([128, n]), op=ALU.pow),
             reads=[t_r, "neghalf"], writes=[r_r])
        return r, r_r

    def norm_elem(src, src_r, gi, xn, xn_r):
        ss, ss_r = new_st()
        P.op("act", lambda e: e.activation(out=xn[:], in_=src, func=AF.Square, accum_out=ss),
             reads=[src_r], writes=[ss_r, xn_r])
        r, r_r = rstd_ops(ss, ss_r, 1.0 / D)
        P.op("dve", lambda e: e.scalar_tensor_tensor(out=xn[:], in0=src, scalar=r, in1=gb_t[gi][:],
                                                     op0=ALU.mult, op1=ALU.mult),
             reads=[src_r, r_r, ("gb", gi)], writes=[xn_r])

    def norm_tr(xn, xn_r, dstT, dst_name, tt):
        b = PS.alloc1()
        pb = bank(b).bitcast(BF16)

        def _tr(e):
            ins = None
            for c in range(NDC):
                ins = e.transpose(out=pb[:, c * 128:(c + 1) * 128], in_=xn[:, c * 128:(c + 1) * 128], identity=ident_b[:])
            return ins
        P.op("pe", _tr, reads=[xn_r, "ident_b"], writes=[("ps", b)])
        P.op("act", lambda e: e.activation(out=dstT[:, :, tt * 128:(tt + 1) * 128],
                                           in_=pb.rearrange("p (c t) -> p c t", t=128), func=AF.Copy),
             reads=[("ps", b)], writes=[(dst_name, tt)])
        PS.release(b)

    class Blk:
        pass

    def make_blocks():
        blks = []
        for s in range(NP):
            nb = SEQ // 512
            for k in range(nb):
                b = Blk()
                b.kind = "p"
                b.nseg, b.L, b.TT = 1, 512, 4
                b.first, b.last = (k == 0), (k == nb - 1)
                b.seqs = [s]
                b.xrows = [x_p[s, k * 512 + t * 128:k * 512 + (t + 1) * 128, :] for t in range(4)]
                b.yrows = [y_p[s, k * 512 + t * 128:k * 512 + (t + 1) * 128, :] for t in range(4)]
                blks.append(b)
        xs = x_s.rearrange("s t d -> (s t) d")
        ys = y_s.rearrange("s t d -> (s t) d")
        spb = 512 // LS
        for k0 in range(0, NS, spb):
            b = Blk()
            b.kind = "s"
            b.nseg = min(spb, NS - k0)
            b.L = LS
            b.TT = b.nseg * LS // 128
            b.first, b.last = True, True
            b.seqs = list(range(k0, k0 + b.nseg))
            b.xrows = [xs[k0 * LS + t * 128:k0 * LS + (t + 1) * 128, :] for t in range(b.TT)]
            b.yrows = [ys[k0 * LS + t * 128:k0 * LS + (t + 1) * 128, :] for t in range(b.TT)]
            blks.append(b)
        for i, b in enumerate(blks):
            b.bi = i
            b.buf = i % 2
        return blks

    blocks = make_blocks()

    def aview(buf, c, b, lo, n, hist):
        w = hist + b.L
        v = buf[:, c, 0:b.nseg * w].rearrange("p (s l) -> p s l", s=b.nseg)
        return v[:, :, lo:lo + n]

    def nview(ap2d, b):
        return ap2d.rearrange("p (s l) -> p s l", s=b.nseg)

    def load_x(b, q="sp"):
        for tt in range(b.TT):
            P.op(q, lambda e, tt=tt: e.dma_start(out=x_t[b.buf][:, tt, :], in_=b.xrows[tt]),
                 writes=[("x", b.buf, tt)], dma_slot=("x", q, b.buf, tt))

    def phase_hist(b):
        if not b.first:
            return
        if b.kind == "p":
            for c in range(4):
                P.op("pool", lambda e, c=c: e.memset(aview(a_ext, c, b, 0, KA - 1, KA - 1), 0.0), writes=[("ahist", c)])
                P.op("pool", lambda e, c=c: e.memset(aview(cv_ext, c, b, 0, KB - 1, KB - 1), 0.0), writes=[("bhist", c)])
            return
        s0, ns = b.seqs[0], b.nseg
        for (cache, K1, buf, hname) in ((cache_a, KA - 1, a_ext, "ahist"), (cache_b, KB - 1, cv_ext, "bhist")):
            stg, stg_r = new_tmp()
            rows = ns * K1
            P.op("sp", lambda e, stg=stg, cache=cache, rows=rows: e.dma_start(
                out=stg[0:rows, :], in_=cache[s0:s0 + ns].rearrange("s t c -> (s t) c")),
                writes=[stg_r], dma_slot=("cst", hname))
            tb1 = PS.alloc1()

            def _tr(e, stg=stg, rows=rows, tb1=tb1):
                ins = None
                for c in range(4):
                    ins = e.transpose(out=bank(tb1)[:, c * 128:c * 128 + rows], in_=stg[0:rows, c * 128:(c + 1) * 128],
                                      identity=ident_f[0:rows, 0:rows])
                return ins
            P.op("pe", _tr, reads=[stg_r, "ident_f"], writes=[("ps", tb1)])
            for c in range(4):
                P.op("dve", lambda e, c=c, tb1=tb1, rows=rows, K1=K1, buf=buf: e.tensor_copy(
                    out=aview(buf, c, b, 0, K1, K1),
                    in_=bank(tb1)[:, c * 128:c * 128 + rows].rearrange("p (s k) -> p s k", k=K1)),
                    reads=[("ps", tb1)], writes=[(hname, c)])
            PS.release(tb1)

    def phase0_elem(b):
        xs_ = []
        ss4, ss4_r = new_st4()
        for tt in range(b.TT):
            xn, xn_r = xn_t[tt], ("xn", tt)
            P.op("act", lambda e, tt=tt, xn=xn: e.activation(out=xn[:], in_=x_t[b.buf][:, tt, :], func=AF.Square,
                                                            accum_out=ss4[:, tt:tt + 1]),
                 reads=[("x", b.buf, tt)], writes=[ss4_r, xn_r])
            xs_.append((xn, xn_r))
        r4, r4_r = rstd4_ops(ss4, ss4_r, b.TT, 1.0 / D)
        for tt in range(b.TT):
            xn, xn_r = xs_[tt]
            P.op("dve", lambda e, tt=tt, xn=xn: e.scalar_tensor_tensor(
                out=xn[:], in0=x_t[b.buf][:, tt, :], scalar=r4[:, tt:tt + 1], in1=gb_t[0][:], op0=ALU.mult, op1=ALU.mult),
                reads=[("x", b.buf, tt), r4_r, ("gb", 0)], writes=[xn_r])
        b.xn = xs_

    def phase0_tr(b):
        for tt in range(b.TT):
            norm_tr(b.xn[tt][0], b.xn[tt][1], xnT, "xnT", tt)

    def win_mm(b, slot, slot_r, q):
        N = b.TT * 128
        bk = PS.alloc1()
        wv = slot.rearrange("p (q dc e) -> p q dc e", q=2, dc=NDC)

        def _mm(e):
            ins = None
            for dc in range(NDC):
                ins = e.matmul(out=bank(bk, N), lhsT=wv[:, q, dc, :], rhs=xnT[:, dc, 0:N], start=(dc == 0), stop=(dc == NDC - 1))
            return ins
        P.op("pe", _mm, reads=[slot_r] + [("xnT", t) for t in range(b.TT)], writes=[("ps", bk)])
        return bk

    def _get_piece(b, i, pin=False):
        if not hasattr(b, "pieces"):
            b.pieces = {}
        if i not in b.pieces:
            b.pieces[i] = wload(ws_in[i], res_ws["in"][0:8] if i < 4 else res_ws["in"], pin=pin)

    def _get_chunk(b, k):
        i, q = divmod(k, 2)
        _get_piece(b, i)
        slot, slot_r = b.pieces[i]
        if q == 1:
            ring_pinned.discard(slot_r[1])
        return win_mm(b, slot, slot_r, q)

    def phaseA1p1(b):
        N = b.TT * 128
        get_chunk = lambda k: _get_chunk(b, k)
        for c in range(4):
            bv = get_chunk(2 * c)
            bg = get_chunk(2 * c + 1)
            sg, sg_r = new_tmp()
            P.op("act", lambda e, bg=bg, sg=sg: e.activation(out=sg[:, 0:N], in_=bank(bg, N), func=AF.Sigmoid),
                 reads=[("ps", bg)], writes=[sg_r])
            P.op("dve", lambda e, bv=bv, sg=sg, c=c: e.tensor_tensor(
                out=aview(a_ext, c, b, KA - 1, b.L, KA - 1), in0=nview(bank(bv, N), b), in1=nview(sg[:, 0:N], b), op=ALU.mult),
                reads=[("ps", bv), sg_r], writes=[("abody", c)])
            PS.release(bv, bg)
            yield

    def phaseA1p2(b):
        N = b.TT * 128
        get_chunk = lambda k: _get_chunk(b, k)
        for c in range(4):
            bgc = get_chunk(8 + 3 * c)
            bvv = get_chunk(8 + 3 * c + 1)
            bgb = get_chunk(8 + 3 * c + 2)
            vs, vs_r = new_tmp()
            P.op("act", lambda e, bvv=bvv, vs=vs: e.activation(out=vs[:, 0:N], in_=bank(bvv, N), func=AF.Copy),
                 reads=[("ps", bvv)], writes=[vs_r])
            P.op("dve", lambda e, bgc=bgc, vs=vs, c=c: e.tensor_tensor(
                out=aview(cv_ext, c, b, KB - 1, b.L, KB - 1), in0=nview(bank(bgc, N), b), in1=nview(vs[:, 0:N], b), op=ALU.mult),
                reads=[("ps", bgc), vs_r], writes=[("bbody", c)])
            u, u_r = new_tmp()
            P.op("dve", lambda e, u=u, c=c: e.tensor_scalar(
                out=nview(u[:, 0:N], b), in0=aview(cv_ext, c, b, 0, b.L, KB - 1), scalar1=pT[:, c, 34:35], scalar2=None, op0=ALU.mult),
                reads=[("bbody", c), ("bhist", c), "pT"], writes=[u_r])
            for k in range(1, KB):
                P.op("dve", lambda e, u=u, c=c, k=k: e.scalar_tensor_tensor(
                    out=nview(u[:, 0:N], b), in0=aview(cv_ext, c, b, k, b.L, KB - 1), scalar=pT[:, c, 34 + k:35 + k],
                    in1=nview(u[:, 0:N], b), op0=ALU.mult, op1=ALU.add),
                    reads=[("bbody", c), ("bhist", c), "pT", u_r], writes=[u_r])
            P.op("dve", lambda e, u=u, bgb=bgb, c=c: e.tensor_tensor(
                out=mix(b, 4 + c)[:, 0:N], in0=bank(bgb, N), in1=u[:, 0:N], op=ALU.mult),
                reads=[("ps", bgb), u_r], writes=[mix_r(b, 4 + c)])
            PS.release(bgc, bvv, bgb)
            yield

    def phase_conv(b):
        N = b.TT * 128
        acv = [nview(ac_t[:, c, 0:N], b) for c in range(4)]
        for c in range(4):
            P.op("dve", lambda e, c=c: e.tensor_scalar(
                out=acv[c], in0=aview(a_ext, c, b, 0, b.L, KA - 1), scalar1=pT[:, c, 0:1], scalar2=pT[:, c, 31:32],
                op0=ALU.mult, op1=ALU.add),
                reads=[("abody", c), ("ahist", c), "pT"], writes=[("ac", c)])
        yield
        for k in range(1, KA):
            for c in range(4):
                P.op("dve", lambda e, c=c, k=k: e.scalar_tensor_tensor(
                    out=acv[c], in0=aview(a_ext, c, b, k, b.L, KA - 1), scalar=pT[:, c, k:k + 1], in1=acv[c],
                    op0=ALU.mult, op1=ALU.add),
                    reads=[("abody", c), ("ahist", c), "pT", ("ac", c)], writes=[("ac", c)])
            yield

    def phase_tail(b, which):
        buf, K1, body, hist = (a_ext, KA - 1, "abody", "ahist") if which == "a" else (cv_ext, KB - 1, "bbody", "bhist")
        if b.last:
            if which == "a":
                dst = na_p if b.kind == "p" else na_s
            else:
                dst = nb_p if b.kind == "p" else nb_s
            for si, s in enumerate(b.seqs):
                tb1 = PS.alloc1()

                def _t1(e, si=si, tb1=tb1):
                    ins = None
                    for c in range(4):
                        ins = e.transpose(out=bank(tb1)[0:K1, c * 128:(c + 1) * 128],
                                          in_=aview(buf, c, b, b.L, K1, K1)[:, si, :], identity=ident_f[:])
                    return ins
                P.op("pe", _t1, reads=[(body, c) for c in range(4)] + [(hist, c) for c in range(4)] + ["ident_f"],
                     writes=[("ps", tb1)])
                og, og_r = new_tmp()
                P.op("act", lambda e, tb1=tb1, og=og: e.activation(out=og[0:K1, :], in_=bank(tb1)[0:K1, :], func=AF.Copy),
                     reads=[("ps", tb1)], writes=[og_r])
                PS.release(tb1)
                P.op("sp", lambda e, s=s, og=og: e.dma_start(out=dst[s, :, :], in_=og[0:K1, :]),
                     reads=[og_r], writes=[("out", which, b.kind, s)], dma_slot=("o" + which, si))
        else:
            for c in range(4):
                P.op("pool", lambda e, c=c: e.tensor_copy(out=buf[:, c, 0:K1], in_=buf[:, c, b.L:b.L + K1]),
                     reads=[(body, c), (hist, c)], writes=[(hist, c)])

    def phaseA2_stats(b):
        N = b.TT * 128
        bm = PS.alloc_pair()
        be = bm + 1
        b.bm, b.be = bm, be
        sqs = []
        for c in range(4):
            sq, sq_r = new_tmp()
            P.op("act", lambda e, c=c, sq=sq: e.activation(out=sq[:, 0:N], in_=ac_t[:, c, 0:N], func=AF.Square),
                 reads=[("ac", c)], writes=[sq_r])
            sqs.append((sq, sq_r))

        def _m1(e):
            ins = None
            for c in range(4):
                ins = e.matmul(out=bank(bm, N), lhsT=ones_f[:], rhs=ac_t[:, c, 0:N], start=(c == 0), stop=(c == 3))
            return ins
        P.op("pe", _m1, reads=[("ac", c) for c in range(4)] + ["ones_f"], writes=[("ps", bm)])

        def _m2(e):
            ins = None
            for c in range(4):
                ins = e.matmul(out=bank(be, N), lhsT=ones_f[:], rhs=sqs[c][0][:, 0:N], start=(c == 0), stop=(c == 3))
            return ins
        P.op("pe", _m2, reads=[s_[1] for s_ in sqs] + ["ones_f"], writes=[("ps", be)])

    def phaseA2_norm(b):
        N = b.TT * 128
        bm, be = b.bm, b.be
        msq, msq_r = new_tmp()
        P.op("act", lambda e: e.activation(out=msq[:, 0:N], in_=bank(bm, N), func=AF.Square), reads=[("ps", bm)], writes=[msq_r])
        var, var_r = new_tmp()
        P.op("dve", lambda e: e.tensor_tensor(out=var[:, 0:N], in0=bank(be, N), in1=msq[:, 0:N], op=ALU.subtract),
             reads=[("ps", be), msq_r], writes=[var_r])
        P.op("act", lambda e: e.activation(out=var[:, 0:N], in_=var[:, 0:N], func=AF.Ln, bias=eps_t[:, 0:1]),
             reads=[var_r, "eps_t"], writes=[var_r])
        rs, rs_r = new_tmp(pin=True)
        P.op("act", lambda e: e.activation(out=rs[:, 0:N], in_=var[:, 0:N], func=AF.Exp, scale=-0.5), reads=[var_r], writes=[rs_r])
        PS.release(be)
        for c in range(4):
            z, z_r = new_tmp()
            P.op("dve", lambda e, c=c, z=z: e.tensor_tensor(out=z[:, 0:N], in0=ac_t[:, c, 0:N], in1=bank(bm, N), op=ALU.subtract),
                 reads=[("ac", c), ("ps", bm)], writes=[z_r])
            P.op("dve", lambda e, z=z: e.tensor_tensor(out=z[:, 0:N], in0=z[:, 0:N], in1=rs[:, 0:N], op=ALU.mult),
                 reads=[z_r, rs_r], writes=[z_r])
            P.op("act", lambda e, c=c, z=z: e.activation(out=mix(b, c)[:, 0:N], in_=z[:, 0:N], func=AF.Silu,
                                                        scale=pT[:, c, 32:33], bias=pT[:, c, 33:34]),
                 reads=[z_r, "pT"], writes=[("mixT", c)])
            if c == 3:
                PS.release(bm)
                tmp_pinned.discard(rs_r[1])
            yield

    def phaseB(b):
        slots = b.wout_slots
        b.hn = [(hn_t[tt], ("hn", tt)) for tt in range(b.TT)]
        pend = {}

        def stage1(tt):
            pb = PS.alloc_pair()

            def _mm(e, tt=tt, pb=pb):
                ins = None
                for half in range(2):
                    for cc in range(NDC):
                        sl = slots[half * 2 + cc // 4][0].rearrange("p (cc d) -> p cc d", cc=4)
                        ins = e.matmul(out=bank(pb + half), lhsT=mix(b, cc)[:, tt * 128:(tt + 1) * 128], rhs=sl[:, cc % 4, :],
                                       start=(cc == 0), stop=(cc == NDC - 1))
                return ins
            P.op("pe", _mm, reads=[s_[1] for s_ in slots] + [mix_r(b, cc) for cc in range(NDC)],
                 writes=[("ps", pb), ("ps", pb + 1)])
            ss, ss_r = new_st()
            P.op("act", lambda e, pb=pb, ss=ss, tt=tt: e.activation(out=hn_t[tt][:], in_=bank(pb, 1024), func=AF.Square, accum_out=ss),
                 reads=[("ps", pb), ("ps", pb + 1)], writes=[ss_r, ("hn", tt)])
            r, r_r = rstd_ops(ss, ss_r, 1.0 / D)
            P.op("dve", lambda e, tt=tt, pb=pb, r=r: e.scalar_tensor_tensor(
                out=h_t[:, tt, :], in0=bank(pb, 1024), scalar=r, in1=gb_t[1][:], op0=ALU.mult, op1=ALU.mult),
                reads=[("ps", pb), ("ps", pb + 1), r_r, ("gb", 1)], writes=[("h", tt)])
            PS.release(pb, pb + 1)
            P.op("dve", lambda e, tt=tt: e.tensor_tensor(out=h_t[:, tt, :], in0=h_t[:, tt, :], in1=x_t[b.buf][:, tt, :], op=ALU.add),
                 reads=[("h", tt), ("x", b.buf, tt)], writes=[("h", tt)])
            ss2, ss2_r = new_st()
            P.op("act", lambda e, tt=tt, ss2=ss2: e.activation(out=hn_t[tt][:], in_=h_t[:, tt, :], func=AF.Square, accum_out=ss2),
                 reads=[("h", tt)], writes=[ss2_r, ("hn", tt)])
            pend[tt] = rstd_ops(ss2, ss2_r, 1.0 / D)

        def stage2(tt):
            r2, r2_r = pend[tt]
            P.op("dve", lambda e, tt=tt, r2=r2: e.scalar_tensor_tensor(
                out=hn_t[tt][:], in0=h_t[:, tt, :], scalar=r2, in1=gb_t[2][:], op0=ALU.mult, op1=ALU.mult),
                reads=[("h", tt), r2_r, ("gb", 2)], writes=[("hn", tt)])

        for tt in range(b.TT):
            stage1(tt)
            if tt == b.TT - 1:
                for s_ in slots:
                    ring_pinned.discard(s_[1][1])
            if tt > 0:
                stage2(tt - 1)
            yield
        stage2(b.TT - 1)
        yield

    def preload_wout(b):
        b.wout_slots = [wload(ws_out[i], res_ws["out"], pin=True) for i in range(4)]

    def phaseC(b):
        for tt in range(b.TT):
            norm_tr(b.hn[tt][0], b.hn[tt][1], hnT, "hnT", tt)

    def phaseD(b):
        N = b.TT * 128
        for j in range(NFC):
            slot, slot_r = wload(ws_gu[j], res_ws["gu"])
            wv = slot.rearrange("p (gu dc e) -> p gu dc e", gu=2, dc=NDC)
            pb = PS.alloc_pair()

            def _mm(e, wv=wv, pb=pb):
                ins = None
                for gu in range(2):
                    for dc in range(NDC):
                        ins = e.matmul(out=bank(pb + gu, N), lhsT=wv[:, gu, dc, :], rhs=hnT[:, dc, 0:N],
                                       start=(dc == 0), stop=(dc == NDC - 1))
                return ins
            P.op("pe", _mm, reads=[slot_r] + [("hnT", t) for t in range(b.TT)], writes=[("ps", pb), ("ps", pb + 1)])
            sg, sg_r = new_tmp()
            P.op("act", lambda e, pb=pb, sg=sg: e.activation(out=sg[:, 0:N], in_=bank(pb, N), func=AF.Silu),
                 reads=[("ps", pb)], writes=[sg_r])
            P.op("dve", lambda e, pb=pb, sg=sg, j=j: e.tensor_tensor(out=actT[:, j, 0:N], in0=bank(pb + 1, N), in1=sg[:, 0:N], op=ALU.mult),
                 reads=[("ps", pb + 1), sg_r], writes=[("actT", j)])
            PS.release(pb, pb + 1)
            yield

    def phaseE_half(b, half):
        banks = []
        for _ in range((b.TT + 1) // 2):
            p_ = PS.alloc_pair()
            banks += [p_, p_ + 1]
        for extra in banks[b.TT:]:
            PS.release(extra)
        banks = banks[:b.TT]
        for q in range(6):
            nfc = 4 if q < 5 else 2
            slot, slot_r = wload(ws_dn[half * 6 + q], res_ws["dn"], cols=nfc * 512)
            sl = slot.rearrange("p (fc d) -> p fc d", fc=4)

            def _mm(e, q=q, nfc=nfc, sl=sl):
                ins = None
                for f in range(nfc):
                    fc = 4 * q + f
                    for tt in range(b.TT):
                        ins = e.matmul(out=bank(banks[tt]), lhsT=actT[:, fc, tt * 128:(tt + 1) * 128], rhs=sl[:, f, :],
                                       start=(fc == 0), stop=(fc == NFC - 1))
                return ins
            P.op("pe", _mm, reads=[slot_r] + [("actT", 4 * q + f) for f in range(nfc)],
                 writes=[("ps", bk) for bk in banks])
            yield
        if half == 0:
            ss0, ss0_r = new_st4()
            for tt in range(b.TT):
                jk, jk_r = new_tmp()
                P.op("act", lambda e, tt=tt, jk=jk: e.activation(out=jk[:], in_=bank(banks[tt]), func=AF.Square,
                                                                accum_out=ss0[:, tt:tt + 1]),
                     reads=[("ps", banks[tt])], writes=[ss0_r, jk_r])
                P.op("act", lambda e, tt=tt: e.activation(out=cv_ext[:, tt, 2:514], in_=bank(banks[tt]), func=AF.Copy),
                     reads=[("ps", banks[tt]), ss0_r], writes=[("bbody", tt)])
            b.ss0 = (ss0, ss0_r)
            PS.release(*banks)
        else:
            b.ebanks = banks
        yield

    def phaseE_epi(b):
        banks = b.ebanks
        ss1, ss1_r = new_st4()
        for tt in range(b.TT):
            jk, jk_r = new_tmp()
            P.op("act", lambda e, tt=tt, jk=jk: e.activation(out=jk[:], in_=bank(banks[tt]), func=AF.Square,
                                                            accum_out=ss1[:, tt:tt + 1]),
                 reads=[("ps", banks[tt])], writes=[ss1_r, jk_r])
        r4, r4_r = rstd4_ops(ss1, ss1_r, b.TT, 1.0 / D, add=b.ss0)
        for tt in range(b.TT):
            r = r4[:, tt:tt + 1]
            for half in range(2):
                t, t_r = new_tmp()
                if half == 0:
                    P.op("dve", lambda e, tt=tt, t=t, r=r: e.scalar_tensor_tensor(
                        out=t[:], in0=cv_ext[:, tt, 2:514], scalar=r, in1=gb_t[3][:, 0:512], op0=ALU.mult, op1=ALU.mult),
                        reads=[("bbody", tt), r4_r, ("gb", 3)], writes=[t_r])
                else:
                    P.op("dve", lambda e, tt=tt, t=t, r=r: e.scalar_tensor_tensor(
                        out=t[:], in0=bank(banks[tt]), scalar=r, in1=gb_t[3][:, 512:1024], op0=ALU.mult, op1=ALU.mult),
                        reads=[("ps", banks[tt]), r4_r, ("gb", 3)], writes=[t_r])
                P.op("dve", lambda e, tt=tt, half=half, t=t: e.tensor_tensor(
                    out=h_t[:, tt, half * 512:(half + 1) * 512], in0=h_t[:, tt, half * 512:(half + 1) * 512], in1=t[:], op=ALU.add),
                    reads=[("h", tt), t_r], writes=[("h", tt)])
            PS.release(banks[tt])
            P.op("sp", lambda e, tt=tt: e.dma_start(out=b.yrows[tt], in_=h_t[:, tt, :]),
                 reads=[("h", tt)], writes=[("out", "y", b.bi, tt)], dma_slot=("y", tt))

    def run(*gens):
        for g in gens:
            if g is not None:
                for _ in g:
                    pass

    def chain(*gens):
        for g in gens:
            if g is not None:
                yield from g

    def interleave(ga, gb, on_b_done=None):
        da = db = False
        while not (da and db):
            if not da:
                try:
                    next(ga)
                except StopIteration:
                    da = True
            if not db:
                try:
                    next(gb)
                except StopIteration:
                    db = True
                    if on_b_done is not None:
                        on_b_done()

    nblk = len(blocks)

    def driver():
        load_x(blocks[0])
        if nblk > 1:
            load_x(blocks[1])
        b0 = blocks[0]
        phase0_elem(b0)
        phase_hist(b0)
        prep_in_out()
        phase0_tr(b0)
        run(phaseA1p1(b0))
        if nblk > 1:
            phase0_elem(blocks[1])
        prep_gu_dn()

        def first_ffn():
            yield from phaseA1p2(b0)
            phase_tail(b0, "b")
            yield
        for i in range(nblk + 1):
            b = blocks[i] if i < nblk else None
            pb_ = blocks[i - 1] if i > 0 else None
            nb_ = blocks[i + 1] if i + 1 < nblk else None
            if pb_ is not None:
                phaseC(pb_)
            if nb_ is not None and i > 0:
                phase0_tr(nb_)
            ffn = chain(phaseD(pb_), phaseE_half(pb_, 0), phaseE_half(pb_, 1)) if pb_ is not None else first_ffn()
            conv = phase_conv(b) if b is not None else iter(())

            def after_conv(b=b):
                if b is not None:
                    phase_tail(b, "a")
                    phaseA2_stats(b)
                    run(phaseA2_norm(b))
            interleave(ffn, conv, on_b_done=after_conv)
            if nb_ is not None and i == 0:
                phase0_tr(nb_)
            if b is not None:
                preload_wout(b)
            if nb_ is not None:
                _get_piece(nb_, 0, pin=True)
                _get_piece(nb_, 1, pin=True)
            if pb_ is not None:
                phaseE_epi(pb_)
            if b is None:
                break
            if nb_ is not None:
                phase_hist(nb_)
            interleave(phaseB(b), phaseA1p1(nb_) if nb_ is not None else iter(()))
            if i + 2 < nblk:
                load_x(blocks[i + 2], "act")
            if nb_ is not None:
                run(phaseA1p2(nb_))
                phase_tail(nb_, "b")
            if i + 2 < nblk:
                phase0_elem(blocks[i + 2])

    driver()

    out_res = [r for r in P.lastw if isinstance(r, tuple) and r and r[0] == "out"]
    P.op("sp", lambda e: e.nop(), reads=out_res)

    P.finalize()
    dma_slots = list(P.dma_cnt.keys())
    esem = {e: es.enter_context(nc.semaphore(f"sem_{e}")) for e in Prog.ENGS}
    dsem = {s: es.enter_context(nc.semaphore(f"dsem_{i}")) for i, s in enumerate(dma_slots)}
    with nc.Block() as block:
        @block.tensor
        def _(e):
            P.emit_engine("pe", e, esem, dsem)

        @block.scalar
        def _(e):
            P.emit_engine("act", e, esem, dsem)

        @block.vector
        def _(e):
            P.emit_engine("dve", e, esem, dsem)

        @block.gpsimd
        def _(e):
            P.emit_engine("pool", e, esem, dsem)

        @block.sync
        def _(e):
            P.emit_engine("sp", e, esem, dsem)
    es.close()
    return nc


def make_in_maps(n_cores, inputs, NP, NS):
    f = lambda a: np.ascontiguousarray(np.asarray(a, dtype=np.float32))
    ident = np.eye(128, dtype=np.float32)
    maps = []
    for c in range(n_cores):
        m = {
            "x_p": f(inputs["x_prompt"][c * NP:(c + 1) * NP]),
            "x_s": f(inputs["x_sample"][c * NS:(c + 1) * NS]),
            "cache_a": f(inputs["cache_conv_a"][0, c * NS:(c + 1) * NS]),
            "cache_b": f(inputs["cache_conv_b"][0, c * NS:(c + 1) * NS]),
            "g_pre1": f(inputs["norm_mix_pre"]),
            "g_post1": f(inputs["norm_mix_post"]),
            "g_pre2": f(inputs["norm_ffn_pre"]),
            "g_post2": f(inputs["norm_ffn_post"]),
            "w_in": f(inputs["w_in"][0]),
            "w_out": f(inputs["w_out"][0]),
            "w_gu": f(inputs["w_gate_up"][0]),
            "w_dn": f(inputs["w_down"][0]),
            "conv_a_w": f(inputs["conv_a_w"][0]),
            "conv_a_b": f(inputs["conv_a_b"]),
            "ln_g": f(inputs["conv_a_ln_g"]),
            "ln_b": f(inputs["conv_a_ln_b"]),
            "conv_b_w": f(inputs["conv_b_w"][0]),
            "ident": ident,
        }
        maps.append(m)
    return maps


def gather(results, n_cores):
    cat = lambda k: np.concatenate([np.asarray(results[c][k], dtype=np.float32) for c in range(n_cores)], axis=0)
    return (cat("y_p"), cat("y_s"), cat("na_p")[None], cat("nb_p")[None], cat("na_s")[None], cat("nb_s")[None])


def kernel(**inputs):
    B, SEQ = inputs["x_prompt"].shape[0], inputs["x_prompt"].shape[1]
    BS, LS = inputs["x_sample"].shape[0], inputs["x_sample"].shape[1]
    NP, NS = B // N_CORES, BS // N_CORES
    nc = build_program(NP, SEQ, NS, LS)
    in_maps = make_in_maps(N_CORES, inputs, NP, NS)
    res = run_bass_kernel_spmd(nc, in_maps, core_ids=list(range(N_CORES)))
    return gather(res.results, N_CORES)
```

```python
import numpy as np
import concourse.bass as bass
import concourse.mybir as mybir
from concourse.bass_utils import run_bass_kernel_spmd

F32 = mybir.dt.float32
BF16 = mybir.dt.bfloat16
AF = mybir.ActivationFunctionType
ALU = mybir.AluOpType

D = 1024
DA = 512
DB = 512
DIN = 2560
DFF = 2816
KA = 31
KB = 3
EPS = 1e-6
NDC = 8
NFC = 22
NPAR = 37
RING = 8
NTMP = 9
N_CORES = 8

E_ORDER = [0, 4, 1, 5, 2, 6, 3, 7, 12, 16, 8, 13, 17, 9, 14, 18, 10, 15, 19, 11]


class _Op:
    __slots__ = ("eng", "fn", "deps", "idx", "signal", "dma_slot", "dma_val", "waits", "sigval")


class Prog:
    ENGS = ("pe", "act", "dve", "pool", "sp")

    def __init__(self):
        self.ops = {e: [] for e in self.ENGS}
        self.lastw = {}
        self.readers = {}
        self.dma_cnt = {}

    def op(self, eng, fn, reads=(), writes=(), dma_slot=None):
        o = _Op()
        o.eng = eng
        o.fn = fn
        o.idx = len(self.ops[eng])
        o.signal = False
        deps = []
        for r in reads:
            w = self.lastw.get(r)
            if w is not None:
                deps.append(w)
        for r in writes:
            w = self.lastw.get(r)
            if w is not None:
                deps.append(w)
            deps.extend(self.readers.get(r, ()))
        o.deps = deps
        if dma_slot is not None:
            c = self.dma_cnt.get(dma_slot, 0) + 1
            self.dma_cnt[dma_slot] = c
            o.dma_slot = dma_slot
            o.dma_val = 16 * c
            tok = ("dma", dma_slot, 16 * c)
        else:
            o.dma_slot = None
            o.dma_val = 0
            tok = ("eng", eng, o.idx)
        for r in reads:
            self.readers.setdefault(r, []).append(tok)
        for r in writes:
            self.lastw[r] = tok
            self.readers[r] = []
        self.ops[eng].append(o)
        return o

    def finalize(self):
        for e in self.ENGS:
            seen = {}
            for o in self.ops[e]:
                need_eng = {}
                need_dma = {}
                for d in o.deps:
                    if d[0] == "eng":
                        _, de, di = d
                        if de == e:
                            if e in ("pe", "sp"):
                                continue
                        if seen.get(de, -1) >= di:
                            continue
                        if need_eng.get(de, -1) < di:
                            need_eng[de] = di
                    else:
                        _, slot, val = d
                        if seen.get(("dma", slot), 0) >= val:
                            continue
                        if need_dma.get(slot, 0) < val:
                            need_dma[slot] = val
                waits = []
                for de, di in need_eng.items():
                    seen[de] = di
                    waits.append(("eng", de, di))
                    self.ops[de][di].signal = True
                for slot, val in need_dma.items():
                    seen[("dma", slot)] = val
                    waits.append(("dma", slot, val))
                o.waits = waits
        for e in self.ENGS:
            c = 0
            for o in self.ops[e]:
                if o.signal:
                    c += 1
                o.sigval = c

    def emit_engine(self, e, eng, esem, dsem):
        for o in self.ops[e]:
            for w in o.waits:
                if w[0] == "eng":
                    eng.wait_ge(esem[w[1]], self.ops[w[1]][w[2]].sigval)
                else:
                    eng.wait_ge(dsem[w[1]], w[2])
            ins = o.fn(eng)
            if o.dma_slot is not None:
                ins.then_inc(dsem[o.dma_slot], 16)
            elif o.signal:
                ins.then_inc(esem[e], 1)


class _Psum:
    def __init__(self):
        self.held = [False] * 8
        self.stamp = [0] * 8
        self.clock = 0

    def alloc1(self):
        free = [b for b in range(8) if not self.held[b]]
        assert free, "out of PSUM banks"
        b = min(free, key=lambda k: self.stamp[k])
        self.held[b] = True
        return b

    def alloc_pair(self):
        free = [b for b in range(0, 8, 2) if not self.held[b] and not self.held[b + 1]]
        assert free, "out of PSUM bank pairs"
        b = min(free, key=lambda k: max(self.stamp[k], self.stamp[k + 1]))
        self.held[b] = self.held[b + 1] = True
        return b

    def release(self, *banks):
        for b in banks:
            assert self.held[b]
            self.held[b] = False
            self.clock += 1
            self.stamp[b] = self.clock


def build_program(NP, SEQ, NS, LS=64):
    assert SEQ % 512 == 0 and (NS * LS) % 128 == 0 and NS * (KA - 1) <= 128
    nc = bass.Bass("TRN2", target_bir_lowering=False)
    P = Prog()
    PS = _Psum()

    def din(name, shape, dt=F32):
        return nc.dram_tensor(name, list(shape), dt, kind="ExternalInput").ap()

    def dout(name, shape, dt=F32):
        return nc.dram_tensor(name, list(shape), dt, kind="ExternalOutput").ap()

    x_p = din("x_p", [NP, SEQ, D])
    x_s = din("x_s", [NS, LS, D])
    cache_a = din("cache_a", [NS, KA - 1, DA])
    cache_b = din("cache_b", [NS, KB - 1, DB])
    g_pre1 = din("g_pre1", [1, D])
    g_post1 = din("g_post1", [1, D])
    g_pre2 = din("g_pre2", [1, D])
    g_post2 = din("g_post2", [1, D])
    w_in = din("w_in", [D, DIN])
    w_out = din("w_out", [D, D])
    w_gu = din("w_gu", [D, 2 * DFF])
    w_dn = din("w_dn", [DFF, D])
    conv_a_w = din("conv_a_w", [KA, DA])
    conv_a_b = din("conv_a_b", [1, DA])
    ln_g = din("ln_g", [1, DA])
    ln_b = din("ln_b", [1, DA])
    conv_b_w = din("conv_b_w", [KB, DB])
    ident_in = din("ident", [128, 128])

    y_p = dout("y_p", [NP, SEQ, D])
    y_s = dout("y_s", [NS, LS, D])
    na_p = dout("na_p", [NP, KA - 1, DA])
    nb_p = dout("nb_p", [NP, KB - 1, DB])
    na_s = dout("na_s", [NS, KA - 1, DA])
    nb_s = dout("nb_s", [NS, KB - 1, DB])

    ws_in = nc.dram_tensor("ws_in", [10, 128, 2048], BF16, kind="Internal").ap()
    ws_out = nc.dram_tensor("ws_out", [4, 128, 2048], BF16, kind="Internal").ap()
    ws_gu = nc.dram_tensor("ws_gu", [NFC, 128, 2048], BF16, kind="Internal").ap()
    ws_dn = nc.dram_tensor("ws_dn", [12, 128, 2048], BF16, kind="Internal").ap()

    from contextlib import ExitStack
    es = ExitStack()

    def sb(name, shape, dt=F32):
        return es.enter_context(nc.sbuf_tensor(name, list(shape), dt))

    ident_f = sb("ident_f", [128, 128])
    ident_b = sb("ident_b", [128, 128], BF16)
    ones_f = sb("ones_f", [128, 128])
    neghalf = sb("neghalf", [128, 1])
    eps_t = sb("eps_t", [128, 1])
    gb_t = [sb(f"gb{i}", [128, D]) for i in range(4)]
    pT = sb("pT", [128, 4, NPAR])
    x_t = [sb(f"x{i}", [128, 4, D]) for i in range(2)]
    h_t = sb("h", [128, 4, D])
    xn_t = [sb(f"xn{i}", [128, D], BF16) for i in range(4)]
    hn_t = [sb(f"hn{i}", [128, D], BF16) for i in range(4)]
    xnT = sb("xnT", [128, NDC, 512], BF16)
    hnT = sb("hnT", [128, NDC, 512], BF16)
    a_ext = sb("a_ext", [128, 4, 544])
    cv_ext = sb("cv_ext", [128, 4, 516])
    ac_t = sb("ac", [128, 4, 512])
    tmp_t = [sb(f"tmp{i}", [128, 512]) for i in range(NTMP)]
    mixT_lo = sb("mixT_lo", [128, 4, 512], BF16)
    mixT_hi = [sb(f"mixT_hi{i}", [128, 4, 512], BF16) for i in range(2)]

    def mix(b, cc):
        return mixT_lo[:, cc] if cc < 4 else mixT_hi[b.buf][:, cc - 4]

    def mix_r(b, cc):
        return ("mixT", cc) if cc < 4 else ("mixT", cc, b.buf)
    actT = sb("actT", [128, NFC, 512], BF16)
    ring = [sb(f"ring{i}", [128, 2048], BF16) for i in range(RING)]
    st = sb("st", [128, 64])
    st4 = sb("st4", [128, 64])
    ps = es.enter_context(nc.psum_tensor("ps", [128, 4096], F32))

    def bank(b, n=512):
        return ps[:, b * 512:b * 512 + n]

    cnt = {"tmp": 0, "st": 0, "ring": 0, "st4": 0}

    tmp_pinned = set()

    def new_tmp(pin=False):
        while True:
            i = cnt["tmp"] % NTMP
            cnt["tmp"] += 1
            if i not in tmp_pinned:
                break
        if pin:
            tmp_pinned.add(i)
        return tmp_t[i], ("tmp", i)

    def new_st():
        i = cnt["st"] % 64
        cnt["st"] += 1
        return st[:, i:i + 1], ("st", i)

    def new_st4():
        i = cnt["st4"] % 16
        cnt["st4"] += 1
        return st4[:, 4 * i:4 * i + 4], ("st4", i)

    ring_pinned = set()

    def wload(src_ap, src_res, cols=2048, pin=False):
        while True:
            i = cnt["ring"] % RING
            cnt["ring"] += 1
            if i not in ring_pinned:
                break
        if pin:
            ring_pinned.add(i)
        dst = ring[i]
        P.op("sp", lambda e, dst=dst, src_ap=src_ap, cols=cols: e.dma_start(out=dst[:, 0:cols], in_=src_ap[:, 0:cols]),
             reads=src_res, writes=[("ring", i)], dma_slot=("ring", i))
        return dst, ("ring", i)

    res_ws = {"in": [], "out": [], "gu": [], "dn": []}
    w_in_v = w_in.rearrange("(dc p) (j e) -> p j dc e", p=128, e=128)
    ws_in_v = ws_in.rearrange("i p (q dc e) -> i p q dc e", q=2, dc=NDC)
    w_out_v = w_out.rearrange("(cc p) d -> p cc d", p=128)
    ws_out_v = ws_out.rearrange("i p (cc d) -> i p cc d", cc=4)
    w_gu_v = w_gu.rearrange("(dc p) (gu j e) -> p gu j dc e", p=128, gu=2, e=128)
    ws_gu_v = ws_gu.rearrange("j p (gu dc e) -> j p gu dc e", gu=2, dc=NDC)
    w_dn_v = w_dn.rearrange("(fc p) d -> p fc d", p=128)
    ws_dn_v = ws_dn.rearrange("i p (fc d) -> i p fc d", fc=4)

    def prep_in_out():
        for k, j in enumerate(E_ORDER):
            i, q = divmod(k, 2)
            r = ("ws", "in", k)
            res_ws["in"].append(r)
            P.op("pool", lambda e, i=i, q=q, j=j: e.dma_start(out=ws_in_v[i, :, q, :, :], in_=w_in_v[:, j, :, :]),
                 writes=[r], dma_slot="prep_in0" if k < 8 else "prep_in1")
        for half in range(2):
            for q in range(2):
                r = ("ws", "out", half * 2 + q)
                res_ws["out"].append(r)
                P.op("pool", lambda e, half=half, q=q: e.dma_start(
                    out=ws_out_v[half * 2 + q], in_=w_out_v[:, 4 * q:4 * q + 4, half * 512:(half + 1) * 512]),
                    writes=[r], dma_slot="prep_out")

    def prep_gu_dn():
        for j in range(NFC):
            for gu in range(2):
                r = ("ws", "gu", j * 2 + gu)
                res_ws["gu"].append(r)
                P.op("pool", lambda e, j=j, gu=gu: e.dma_start(out=ws_gu_v[j, :, gu, :, :], in_=w_gu_v[:, gu, j, :, :]),
                     writes=[r], dma_slot="prep_gu")
        for half in range(2):
            for q in range(6):
                nfc = 4 if q < 5 else 2
                r = ("ws", "dn", half * 6 + q)
                res_ws["dn"].append(r)
                P.op("pool", lambda e, half=half, q=q, nfc=nfc: e.dma_start(
                    out=ws_dn_v[half * 6 + q, :, 0:nfc, :], in_=w_dn_v[:, 4 * q:4 * q + nfc, half * 512:(half + 1) * 512]),
                    writes=[r], dma_slot="prep_dn")


    P.op("sp", lambda e: e.dma_start(out=ident_f[:], in_=ident_in[:, :]), writes=["ident_f"], dma_slot="c0")
    for i, g in enumerate((g_pre1, g_post1, g_pre2, g_post2)):
        P.op("sp", lambda e, i=i, g=g: e.dma_start(out=gb_t[i][:], in_=g[0, :].partition_broadcast(128)),
             writes=[("gb", i)], dma_slot=("c1", i))
    pstage, pstage_r = new_tmp()
    P.op("sp", lambda e: e.dma_start(out=pstage[0:KA, :], in_=conv_a_w[:, :]), writes=[pstage_r], dma_slot="c2")
    P.op("sp", lambda e: e.dma_start(out=pstage[31:32, :], in_=conv_a_b[:, :]), writes=["ps1"], dma_slot="c3")
    P.op("sp", lambda e: e.dma_start(out=pstage[32:33, :], in_=ln_g[:, :]), writes=["ps2"], dma_slot="c4")
    P.op("sp", lambda e: e.dma_start(out=pstage[33:34, :], in_=ln_b[:, :]), writes=["ps3"], dma_slot="c5")
    P.op("sp", lambda e: e.dma_start(out=pstage[34:37, :], in_=conv_b_w[:, :]), writes=["ps4"], dma_slot="c6")
    P.op("dve", lambda e: e.tensor_copy(out=ident_b[:], in_=ident_f[:]), reads=["ident_f"], writes=["ident_b"])
    P.op("dve", lambda e: e.memset(ones_f[:], 1.0 / DA), writes=["ones_f"])
    P.op("dve", lambda e: e.memset(neghalf[:], -0.5), writes=["neghalf"])
    P.op("dve", lambda e: e.memset(eps_t[:], EPS), writes=["eps_t"])

    tb = PS.alloc1()

    def _ptr(e):
        ins = None
        for c in range(4):
            ins = e.transpose(out=bank(tb)[:, c * 64:c * 64 + NPAR], in_=pstage[0:NPAR, c * 128:(c + 1) * 128],
                              identity=ident_f[0:NPAR, 0:NPAR])
        return ins
    P.op("pe", _ptr, reads=[pstage_r, "ps1", "ps2", "ps3", "ps4", "ident_f"], writes=[("ps", tb)])
    P.op("dve", lambda e: e.tensor_copy(out=pT[:], in_=bank(tb).rearrange("p (c k) -> p c k", k=64)[:, 0:4, 0:NPAR]),
         reads=[("ps", tb)], writes=["pT"])
    PS.release(tb)

    def rstd_ops(ss, ss_r, scale):
        t, t_r = new_st()
        r, r_r = new_st()
        P.op("pool", lambda e: e.tensor_scalar(out=t, in0=ss, scalar1=scale, scalar2=EPS, op0=ALU.mult, op1=ALU.add),
             reads=[ss_r], writes=[t_r])
        P.op("pool", lambda e: e.tensor_tensor(out=r, in0=t, in1=neghalf[:, 0:1], op=ALU.pow),
             reads=[t_r, "neghalf"], writes=[r_r])
        return r, r_r

    def rstd4_ops(ss4, ss4_r, n, scale, add=None):
        t, t_r = new_st4()
        r, r_r = new_st4()
        src, src_rs = ss4, [ss4_r]
        if add is not None:
            a4, a4_r = add
            u, u_r = new_st4()
            P.op("pool", lambda e: e.tensor_tensor(out=u[:, 0:n], in0=ss4[:, 0:n], in1=a4[:, 0:n], op=ALU.add),
                 reads=[ss4_r, a4_r], writes=[u_r])
            src, src_rs = u, [u_r]
        P.op("pool", lambda e: e.tensor_scalar(out=t[:, 0:n], in0=src[:, 0:n], scalar1=scale, scalar2=EPS, op0=ALU.mult, op1=ALU.add),
             reads=src_rs, writes=[t_r])
        P.op("pool", lambda e: e.tensor_tensor(out=r[:, 0:n], in0=t[:, 0:n], in1=neghalf[:, 0:1].broadcast_to([128, n]), op=ALU.pow),
             reads=[t_r, "neghalf"], writes=[r_r])
        return r, r_r

    def norm_elem(src, src_r, gi, xn, xn_r):
        ss, ss_r = new_st()
        P.op("act", lambda e: e.activation(out=xn[:], in_=src, func=AF.Square, accum_out=ss),
             reads=[src_r], writes=[ss_r, xn_r])
        r, r_r = rstd_ops(ss, ss_r, 1.0 / D)
        P.op("dve", lambda e: e.scalar_tensor_tensor(out=xn[:], in0=src, scalar=r, in1=gb_t[gi][:],
                                                     op0=ALU.mult, op1=ALU.mult),
             reads=[src_r, r_r, ("gb", gi)], writes=[xn_r])

    def norm_tr(xn, xn_r, dstT, dst_name, tt):
        b = PS.alloc1()
        pb = bank(b).bitcast(BF16)

        def _tr(e):
            ins = None
            for c in range(NDC):
                ins = e.transpose(out=pb[:, c * 128:(c + 1) * 128], in_=xn[:, c * 128:(c + 1) * 128], identity=ident_b[:])
            return ins
        P.op("pe", _tr, reads=[xn_r, "ident_b"], writes=[("ps", b)])
        P.op("act", lambda e: e.activation(out=dstT[:, :, tt * 128:(tt + 1) * 128],
                                           in_=pb.rearrange("p (c t) -> p c t", t=128), func=AF.Copy),
             reads=[("ps", b)], writes=[(dst_name, tt)])
        PS.release(b)

    class Blk:
        pass

    def make_blocks():
        blks = []
        for s in range(NP):
            nb = SEQ // 512
            for k in range(nb):
                b = Blk()
                b.kind = "p"
                b.nseg, b.L, b.TT = 1, 512, 4
                b.first, b.last = (k == 0), (k == nb - 1)
                b.seqs = [s]
                b.xrows = [x_p[s, k * 512 + t * 128:k * 512 + (t + 1) * 128, :] for t in range(4)]
                b.yrows = [y_p[s, k * 512 + t * 128:k * 512 + (t + 1) * 128, :] for t in range(4)]
                blks.append(b)
        xs = x_s.rearrange("s t d -> (s t) d")
        ys = y_s.rearrange("s t d -> (s t) d")
        spb = 512 // LS
        for k0 in range(0, NS, spb):
            b = Blk()
            b.kind = "s"
            b.nseg = min(spb, NS - k0)
            b.L = LS
            b.TT = b.nseg * LS // 128
            b.first, b.last = True, True
            b.seqs = list(range(k0, k0 + b.nseg))
            b.xrows = [xs[k0 * LS + t * 128:k0 * LS + (t + 1) * 128, :] for t in range(b.TT)]
            b.yrows = [ys[k0 * LS + t * 128:k0 * LS + (t + 1) * 128, :] for t in range(b.TT)]
            blks.append(b)
        for i, b in enumerate(blks):
            b.bi = i
            b.buf = i % 2
        return blks

    blocks = make_blocks()

    def aview(buf, c, b, lo, n, hist):
        w = hist + b.L
        v = buf[:, c, 0:b.nseg * w].rearrange("p (s l) -> p s l", s=b.nseg)
        return v[:, :, lo:lo + n]

    def nview(ap2d, b):
        return ap2d.rearrange("p (s l) -> p s l", s=b.nseg)

    def load_x(b, q="sp"):
        for tt in range(b.TT):
            P.op(q, lambda e, tt=tt: e.dma_start(out=x_t[b.buf][:, tt, :], in_=b.xrows[tt]),
                 writes=[("x", b.buf, tt)], dma_slot=("x", q, b.buf, tt))

    def phase_hist(b):
        if not b.first:
            return
        if b.kind == "p":
            for c in range(4):
                P.op("pool", lambda e, c=c: e.memset(aview(a_ext, c, b, 0, KA - 1, KA - 1), 0.0), writes=[("ahist", c)])
                P.op("pool", lambda e, c=c: e.memset(aview(cv_ext, c, b, 0, KB - 1, KB - 1), 0.0), writes=[("bhist", c)])
            return
        s0, ns = b.seqs[0], b.nseg
        for (cache, K1, buf, hname) in ((cache_a, KA - 1, a_ext, "ahist"), (cache_b, KB - 1, cv_ext, "bhist")):
            stg, stg_r = new_tmp()
            rows = ns * K1
            P.op("sp", lambda e, stg=stg, cache=cache, rows=rows: e.dma_start(
                out=stg[0:rows, :], in_=cache[s0:s0 + ns].rearrange("s t c -> (s t) c")),
                writes=[stg_r], dma_slot=("cst", hname))
            tb1 = PS.alloc1()

            def _tr(e, stg=stg, rows=rows, tb1=tb1):
                ins = None
                for c in range(4):
                    ins = e.transpose(out=bank(tb1)[:, c * 128:c * 128 + rows], in_=stg[0:rows, c * 128:(c + 1) * 128],
                                      identity=ident_f[0:rows, 0:rows])
                return ins
            P.op("pe", _tr, reads=[stg_r, "ident_f"], writes=[("ps", tb1)])
            for c in range(4):
                P.op("dve", lambda e, c=c, tb1=tb1, rows=rows, K1=K1, buf=buf: e.tensor_copy(
                    out=aview(buf, c, b, 0, K1, K1),
                    in_=bank(tb1)[:, c * 128:c * 128 + rows].rearrange("p (s k) -> p s k", k=K1)),
                    reads=[("ps", tb1)], writes=[(hname, c)])
            PS.release(tb1)

    def phase0_elem(b):
        xs_ = []
        ss4, ss4_r = new_st4()
        for tt in range(b.TT):
            xn, xn_r = xn_t[tt], ("xn", tt)
            P.op("act", lambda e, tt=tt, xn=xn: e.activation(out=xn[:], in_=x_t[b.buf][:, tt, :], func=AF.Square,
                                                            accum_out=ss4[:, tt:tt + 1]),
                 reads=[("x", b.buf, tt)], writes=[ss4_r, xn_r])
            xs_.append((xn, xn_r))
        b.r4 = rstd4_ops(ss4, ss4_r, b.TT, 1.0 / D)
        b.xn = xs_

    def phase0_scale(b):
        r4, r4_r = b.r4
        for tt in range(b.TT):
            xn, xn_r = b.xn[tt]
            P.op("dve", lambda e, tt=tt, xn=xn: e.scalar_tensor_tensor(
                out=xn[:], in0=x_t[b.buf][:, tt, :], scalar=r4[:, tt:tt + 1], in1=gb_t[0][:], op0=ALU.mult, op1=ALU.mult),
                reads=[("x", b.buf, tt), r4_r, ("gb", 0)], writes=[xn_r])

    def phase0_tr(b):
        for tt in range(b.TT):
            norm_tr(b.xn[tt][0], b.xn[tt][1], xnT, "xnT", tt)

    def win_mm(b, slot, slot_r, q):
        N = b.TT * 128
        bk = PS.alloc1()
        wv = slot.rearrange("p (q dc e) -> p q dc e", q=2, dc=NDC)

        def _mm(e):
            ins = None
            for dc in range(NDC):
                ins = e.matmul(out=bank(bk, N), lhsT=wv[:, q, dc, :], rhs=xnT[:, dc, 0:N], start=(dc == 0), stop=(dc == NDC - 1))
            return ins
        P.op("pe", _mm, reads=[slot_r] + [("xnT", t) for t in range(b.TT)], writes=[("ps", bk)])
        return bk

    def _get_piece(b, i, pin=False):
        if not hasattr(b, "pieces"):
            b.pieces = {}
        if i not in b.pieces:
            b.pieces[i] = wload(ws_in[i], res_ws["in"][0:8] if i < 4 else res_ws["in"], pin=pin)

    def _get_chunk(b, k):
        i, q = divmod(k, 2)
        _get_piece(b, i)
        slot, slot_r = b.pieces[i]
        if q == 1:
            ring_pinned.discard(slot_r[1])
        return win_mm(b, slot, slot_r, q)

    def phaseA1p1(b):
        N = b.TT * 128
        get_chunk = lambda k: _get_chunk(b, k)
        for c in range(4):
            bv = get_chunk(2 * c)
            bg = get_chunk(2 * c + 1)
            sg, sg_r = new_tmp()
            P.op("act", lambda e, bg=bg, sg=sg: e.activation(out=sg[:, 0:N], in_=bank(bg, N), func=AF.Sigmoid),
                 reads=[("ps", bg)], writes=[sg_r])
            P.op("dve", lambda e, bv=bv, sg=sg, c=c: e.tensor_tensor(
                out=aview(a_ext, c, b, KA - 1, b.L, KA - 1), in0=nview(bank(bv, N), b), in1=nview(sg[:, 0:N], b), op=ALU.mult),
                reads=[("ps", bv), sg_r], writes=[("abody", c)])
            PS.release(bv, bg)
            yield

    def phaseA1p2(b):
        N = b.TT * 128
        get_chunk = lambda k: _get_chunk(b, k)
        for c in range(4):
            bgc = get_chunk(8 + 3 * c)
            bvv = get_chunk(8 + 3 * c + 1)
            bgb = get_chunk(8 + 3 * c + 2)
            vs, vs_r = new_tmp()
            P.op("act", lambda e, bvv=bvv, vs=vs: e.activation(out=vs[:, 0:N], in_=bank(bvv, N), func=AF.Copy),
                 reads=[("ps", bvv)], writes=[vs_r])
            P.op("dve", lambda e, bgc=bgc, vs=vs, c=c: e.tensor_tensor(
                out=aview(cv_ext, c, b, KB - 1, b.L, KB - 1), in0=nview(bank(bgc, N), b), in1=nview(vs[:, 0:N], b), op=ALU.mult),
                reads=[("ps", bgc), vs_r], writes=[("bbody", c)])
            u, u_r = new_tmp()
            P.op("dve", lambda e, u=u, c=c: e.tensor_scalar(
                out=nview(u[:, 0:N], b), in0=aview(cv_ext, c, b, 0, b.L, KB - 1), scalar1=pT[:, c, 34:35], scalar2=None, op0=ALU.mult),
                reads=[("bbody", c), ("bhist", c), "pT"], writes=[u_r])
            for k in range(1, KB):
                P.op("dve", lambda e, u=u, c=c, k=k: e.scalar_tensor_tensor(
                    out=nview(u[:, 0:N], b), in0=aview(cv_ext, c, b, k, b.L, KB - 1), scalar=pT[:, c, 34 + k:35 + k],
                    in1=nview(u[:, 0:N], b), op0=ALU.mult, op1=ALU.add),
                    reads=[("bbody", c), ("bhist", c), "pT", u_r], writes=[u_r])
            P.op("dve", lambda e, u=u, bgb=bgb, c=c: e.tensor_tensor(
                out=mix(b, 4 + c)[:, 0:N], in0=bank(bgb, N), in1=u[:, 0:N], op=ALU.mult),
                reads=[("ps", bgb), u_r], writes=[mix_r(b, 4 + c)])
            PS.release(bgc, bvv, bgb)
            yield

    def phase_conv(b):
        N = b.TT * 128
        acv = [nview(ac_t[:, c, 0:N], b) for c in range(4)]
        for c in range(4):
            P.op("dve", lambda e, c=c: e.tensor_scalar(
                out=acv[c], in0=aview(a_ext, c, b, 0, b.L, KA - 1), scalar1=pT[:, c, 0:1], scalar2=pT[:, c, 31:32],
                op0=ALU.mult, op1=ALU.add),
                reads=[("abody", c), ("ahist", c), "pT"], writes=[("ac", c)])
        yield
        for k in range(1, KA):
            for c in range(4):
                P.op("dve", lambda e, c=c, k=k: e.scalar_tensor_tensor(
                    out=acv[c], in0=aview(a_ext, c, b, k, b.L, KA - 1), scalar=pT[:, c, k:k + 1], in1=acv[c],
                    op0=ALU.mult, op1=ALU.add),
                    reads=[("abody", c), ("ahist", c), "pT", ("ac", c)], writes=[("ac", c)])
            yield

    def phase_tail(b, which):
        buf, K1, body, hist = (a_ext, KA - 1, "abody", "ahist") if which == "a" else (cv_ext, KB - 1, "bbody", "bhist")
        if b.last:
            if which == "a":
                dst = na_p if b.kind == "p" else na_s
            else:
                dst = nb_p if b.kind == "p" else nb_s
            for si, s in enumerate(b.seqs):
                tb1 = PS.alloc1()

                def _t1(e, si=si, tb1=tb1):
                    ins = None
                    for c in range(4):
                        ins = e.transpose(out=bank(tb1)[0:K1, c * 128:(c + 1) * 128],
                                          in_=aview(buf, c, b, b.L, K1, K1)[:, si, :], identity=ident_f[:])
                    return ins
                P.op("pe", _t1, reads=[(body, c) for c in range(4)] + [(hist, c) for c in range(4)] + ["ident_f"],
                     writes=[("ps", tb1)])
                og, og_r = new_tmp()
                P.op("act", lambda e, tb1=tb1, og=og: e.activation(out=og[0:K1, :], in_=bank(tb1)[0:K1, :], func=AF.Copy),
                     reads=[("ps", tb1)], writes=[og_r])
                PS.release(tb1)
                P.op("sp", lambda e, s=s, og=og: e.dma_start(out=dst[s, :, :], in_=og[0:K1, :]),
                     reads=[og_r], writes=[("out", which, b.kind, s)], dma_slot=("o" + which, si))
        else:
            for c in range(4):
                P.op("pool", lambda e, c=c: e.tensor_copy(out=buf[:, c, 0:K1], in_=buf[:, c, b.L:b.L + K1]),
                     reads=[(body, c), (hist, c)], writes=[(hist, c)])

    def phaseA2_stats(b):
        N = b.TT * 128
        bm = PS.alloc_pair()
        be = bm + 1
        b.bm, b.be = bm, be
        sqs = []
        for c in range(4):
            sq, sq_r = new_tmp()
            P.op("act", lambda e, c=c, sq=sq: e.activation(out=sq[:, 0:N], in_=ac_t[:, c, 0:N], func=AF.Square),
                 reads=[("ac", c)], writes=[sq_r])
            sqs.append((sq, sq_r))

        def _m1(e):
            ins = None
            for c in range(4):
                ins = e.matmul(out=bank(bm, N), lhsT=ones_f[:], rhs=ac_t[:, c, 0:N], start=(c == 0), stop=(c == 3))
            return ins
        P.op("pe", _m1, reads=[("ac", c) for c in range(4)] + ["ones_f"], writes=[("ps", bm)])

        def _m2(e):
            ins = None
            for c in range(4):
                ins = e.matmul(out=bank(be, N), lhsT=ones_f[:], rhs=sqs[c][0][:, 0:N], start=(c == 0), stop=(c == 3))
            return ins
        P.op("pe", _m2, reads=[s_[1] for s_ in sqs] + ["ones_f"], writes=[("ps", be)])

    def phaseA2_norm(b):
        N = b.TT * 128
        bm, be = b.bm, b.be
        msq, msq_r = new_tmp()
        P.op("act", lambda e: e.activation(out=msq[:, 0:N], in_=bank(bm, N), func=AF.Square), reads=[("ps", bm)], writes=[msq_r])
        var, var_r = new_tmp()
        P.op("dve", lambda e: e.tensor_tensor(out=var[:, 0:N], in0=bank(be, N), in1=msq[:, 0:N], op=ALU.subtract),
             reads=[("ps", be), msq_r], writes=[var_r])
        P.op("act", lambda e: e.activation(out=var[:, 0:N], in_=var[:, 0:N], func=AF.Ln, bias=eps_t[:, 0:1]),
             reads=[var_r, "eps_t"], writes=[var_r])
        rs, rs_r = new_tmp(pin=True)
        P.op("act", lambda e: e.activation(out=rs[:, 0:N], in_=var[:, 0:N], func=AF.Exp, scale=-0.5), reads=[var_r], writes=[rs_r])
        PS.release(be)
        for c in range(4):
            z, z_r = new_tmp()
            P.op("dve", lambda e, c=c, z=z: e.tensor_tensor(out=z[:, 0:N], in0=ac_t[:, c, 0:N], in1=bank(bm, N), op=ALU.subtract),
                 reads=[("ac", c), ("ps", bm)], writes=[z_r])
            P.op("dve", lambda e, z=z: e.tensor_tensor(out=z[:, 0:N], in0=z[:, 0:N], in1=rs[:, 0:N], op=ALU.mult),
                 reads=[z_r, rs_r], writes=[z_r])
            P.op("act", lambda e, c=c, z=z: e.activation(out=mix(b, c)[:, 0:N], in_=z[:, 0:N], func=AF.Silu,
                                                        scale=pT[:, c, 32:33], bias=pT[:, c, 33:34]),
                 reads=[z_r, "pT"], writes=[("mixT", c)])
            if c == 3:
                PS.release(bm)
                tmp_pinned.discard(rs_r[1])
            yield

    def phaseB(b):
        slots = b.wout_slots
        b.hn = [(hn_t[tt], ("hn", tt)) for tt in range(b.TT)]
        pend = {}

        def stage1(tt):
            pb = PS.alloc_pair()

            def _mm(e, tt=tt, pb=pb):
                ins = None
                for half in range(2):
                    for cc in range(NDC):
                        sl = slots[half * 2 + cc // 4][0].rearrange("p (cc d) -> p cc d", cc=4)
                        ins = e.matmul(out=bank(pb + half), lhsT=mix(b, cc)[:, tt * 128:(tt + 1) * 128], rhs=sl[:, cc % 4, :],
                                       start=(cc == 0), stop=(cc == NDC - 1))
                return ins
            P.op("pe", _mm, reads=[s_[1] for s_ in slots] + [mix_r(b, cc) for cc in range(NDC)],
                 writes=[("ps", pb), ("ps", pb + 1)])
            ss, ss_r = new_st()
            P.op("act", lambda e, pb=pb, ss=ss, tt=tt: e.activation(out=hn_t[tt][:], in_=bank(pb, 1024), func=AF.Square, accum_out=ss),
                 reads=[("ps", pb), ("ps", pb + 1)], writes=[ss_r, ("hn", tt)])
            r, r_r = rstd_ops(ss, ss_r, 1.0 / D)
            P.op("dve", lambda e, tt=tt, pb=pb, r=r: e.scalar_tensor_tensor(
                out=h_t[:, tt, :], in0=bank(pb, 1024), scalar=r, in1=gb_t[1][:], op0=ALU.mult, op1=ALU.mult),
                reads=[("ps", pb), ("ps", pb + 1), r_r, ("gb", 1)], writes=[("h", tt)])
            PS.release(pb, pb + 1)
            P.op("dve", lambda e, tt=tt: e.tensor_tensor(out=h_t[:, tt, :], in0=h_t[:, tt, :], in1=x_t[b.buf][:, tt, :], op=ALU.add),
                 reads=[("h", tt), ("x", b.buf, tt)], writes=[("h", tt)])
            ss2, ss2_r = new_st()
            P.op("act", lambda e, tt=tt, ss2=ss2: e.activation(out=hn_t[tt][:], in_=h_t[:, tt, :], func=AF.Square, accum_out=ss2),
                 reads=[("h", tt)], writes=[ss2_r, ("hn", tt)])
            pend[tt] = rstd_ops(ss2, ss2_r, 1.0 / D)

        def stage2(tt):
            r2, r2_r = pend[tt]
            P.op("dve", lambda e, tt=tt, r2=r2: e.scalar_tensor_tensor(
                out=hn_t[tt][:], in0=h_t[:, tt, :], scalar=r2, in1=gb_t[2][:], op0=ALU.mult, op1=ALU.mult),
                reads=[("h", tt), r2_r, ("gb", 2)], writes=[("hn", tt)])

        for tt in range(b.TT):
            stage1(tt)
            if tt == b.TT - 1:
                for s_ in slots:
                    ring_pinned.discard(s_[1][1])
            if tt > 0:
                stage2(tt - 1)
            yield
        stage2(b.TT - 1)
        yield

    def preload_wout(b):
        b.wout_slots = [wload(ws_out[i], res_ws["out"], pin=True) for i in range(4)]

    def phaseC(b):
        for tt in range(b.TT):
            norm_tr(b.hn[tt][0], b.hn[tt][1], hnT, "hnT", tt)

    def phaseD(b):
        N = b.TT * 128
        for j in range(NFC):
            slot, slot_r = wload(ws_gu[j], res_ws["gu"])
            wv = slot.rearrange("p (gu dc e) -> p gu dc e", gu=2, dc=NDC)
            pb = PS.alloc_pair()

            def _mm(e, wv=wv, pb=pb):
                ins = None
                for gu in range(2):
                    for dc in range(NDC):
                        ins = e.matmul(out=bank(pb + gu, N), lhsT=wv[:, gu, dc, :], rhs=hnT[:, dc, 0:N],
                                       start=(dc == 0), stop=(dc == NDC - 1))
                return ins
            P.op("pe", _mm, reads=[slot_r] + [("hnT", t) for t in range(b.TT)], writes=[("ps", pb), ("ps", pb + 1)])
            sg, sg_r = new_tmp()
            P.op("act", lambda e, pb=pb, sg=sg: e.activation(out=sg[:, 0:N], in_=bank(pb, N), func=AF.Silu),
                 reads=[("ps", pb)], writes=[sg_r])
            P.op("dve", lambda e, pb=pb, sg=sg, j=j: e.tensor_tensor(out=actT[:, j, 0:N], in0=bank(pb + 1, N), in1=sg[:, 0:N], op=ALU.mult),
                 reads=[("ps", pb + 1), sg_r], writes=[("actT", j)])
            PS.release(pb, pb + 1)
            yield

    def phaseE_half(b, half):
        banks = []
        for _ in range((b.TT + 1) // 2):
            p_ = PS.alloc_pair()
            banks += [p_, p_ + 1]
        for extra in banks[b.TT:]:
            PS.release(extra)
        banks = banks[:b.TT]
        for q in range(6):
            nfc = 4 if q < 5 else 2
            slot, slot_r = wload(ws_dn[half * 6 + q], res_ws["dn"], cols=nfc * 512)
            sl = slot.rearrange("p (fc d) -> p fc d", fc=4)

            def _mm(e, q=q, nfc=nfc, sl=sl):
                ins = None
                for f in range(nfc):
                    fc = 4 * q + f
                    for tt in range(b.TT):
                        ins = e.matmul(out=bank(banks[tt]), lhsT=actT[:, fc, tt * 128:(tt + 1) * 128], rhs=sl[:, f, :],
                                       start=(fc == 0), stop=(fc == NFC - 1))
                return ins
            P.op("pe", _mm, reads=[slot_r] + [("actT", 4 * q + f) for f in range(nfc)],
                 writes=[("ps", bk) for bk in banks])
            yield
        if half == 0:
            ss0, ss0_r = new_st4()
            for tt in range(b.TT):
                jk, jk_r = new_tmp()
                P.op("act", lambda e, tt=tt, jk=jk: e.activation(out=jk[:], in_=bank(banks[tt]), func=AF.Square,
                                                                accum_out=ss0[:, tt:tt + 1]),
                     reads=[("ps", banks[tt])], writes=[ss0_r, jk_r])
                P.op("act", lambda e, tt=tt: e.activation(out=cv_ext[:, tt, 2:514], in_=bank(banks[tt]), func=AF.Copy),
                     reads=[("ps", banks[tt]), ss0_r], writes=[("bbody", tt)])
            b.ss0 = (ss0, ss0_r)
            PS.release(*banks)
        else:
            b.ebanks = banks
        yield

    def phaseE_epi(b):
        banks = b.ebanks
        ss1, ss1_r = new_st4()
        for tt in range(b.TT):
            jk, jk_r = new_tmp()
            P.op("act", lambda e, tt=tt, jk=jk: e.activation(out=jk[:], in_=bank(banks[tt]), func=AF.Square,
                                                            accum_out=ss1[:, tt:tt + 1]),
                 reads=[("ps", banks[tt])], writes=[ss1_r, jk_r])
        r4, r4_r = rstd4_ops(ss1, ss1_r, b.TT, 1.0 / D, add=b.ss0)
        for tt in range(b.TT):
            r = r4[:, tt:tt + 1]
            for half in range(2):
                t, t_r = new_tmp()
                if half == 0:
                    P.op("dve", lambda e, tt=tt, t=t, r=r: e.scalar_tensor_tensor(
                        out=t[:], in0=cv_ext[:, tt, 2:514], scalar=r, in1=gb_t[3][:, 0:512], op0=ALU.mult, op1=ALU.mult),
                        reads=[("bbody", tt), r4_r, ("gb", 3)], writes=[t_r])
                else:
                    P.op("dve", lambda e, tt=tt, t=t, r=r: e.scalar_tensor_tensor(
                        out=t[:], in0=bank(banks[tt]), scalar=r, in1=gb_t[3][:, 512:1024], op0=ALU.mult, op1=ALU.mult),
                        reads=[("ps", banks[tt]), r4_r, ("gb", 3)], writes=[t_r])
                P.op("dve", lambda e, tt=tt, half=half, t=t: e.tensor_tensor(
                    out=h_t[:, tt, half * 512:(half + 1) * 512], in0=h_t[:, tt, half * 512:(half + 1) * 512], in1=t[:], op=ALU.add),
                    reads=[("h", tt), t_r], writes=[("h", tt)])
            PS.release(banks[tt])
            P.op("sp", lambda e, tt=tt: e.dma_start(out=b.yrows[tt], in_=h_t[:, tt, :]),
                 reads=[("h", tt)], writes=[("out", "y", b.bi, tt)], dma_slot=("y", tt))

    def run(*gens):
        for g in gens:
            if g is not None:
                for _ in g:
                    pass

    def chain(*gens):
        for g in gens:
            if g is not None:
                yield from g

    def interleave(ga, gb, on_b_done=None):
        da = db = False
        while not (da and db):
            if not da:
                try:
                    next(ga)
                except StopIteration:
                    da = True
            if not db:
                try:
                    next(gb)
                except StopIteration:
                    db = True
                    if on_b_done is not None:
                        on_b_done()

    nblk = len(blocks)

    def driver():
        load_x(blocks[0])
        if nblk > 1:
            load_x(blocks[1])
        b0 = blocks[0]
        phase0_elem(b0)
        phase0_scale(b0)
        phase_hist(b0)
        prep_in_out()
        phase0_tr(b0)
        run(phaseA1p1(b0))
        if nblk > 1:
            phase0_elem(blocks[1])
            phase0_scale(blocks[1])
        prep_gu_dn()

        def first_ffn():
            yield from phaseA1p2(b0)
            phase_tail(b0, "b")
            yield
        for i in range(nblk + 1):
            b = blocks[i] if i < nblk else None
            pb_ = blocks[i - 1] if i > 0 else None
            nb_ = blocks[i + 1] if i + 1 < nblk else None
            if pb_ is not None:
                phaseC(pb_)
            ffn = chain(phaseD(pb_), phaseE_half(pb_, 0), phaseE_half(pb_, 1)) if pb_ is not None else first_ffn()
            conv = phase_conv(b) if b is not None else iter(())
            if nb_ is not None and i > 0:
                for _ in range(2):
                    next(ffn, None)
                    next(conv, None)
                phase0_elem(nb_)
                for _ in range(21):
                    next(ffn, None)
                    next(conv, None)
                phase0_scale(nb_)
                for _ in range(3):
                    next(ffn, None)
                    next(conv, None)
                phase0_tr(nb_)

            def after_conv(b=b):
                if b is not None:
                    phase_tail(b, "a")
                    phaseA2_stats(b)
                    run(phaseA2_norm(b))
            interleave(ffn, conv, on_b_done=after_conv)
            if nb_ is not None and i == 0:
                phase0_tr(nb_)
            if b is not None:
                preload_wout(b)
            if nb_ is not None:
                _get_piece(nb_, 0, pin=True)
                _get_piece(nb_, 1, pin=True)
            if pb_ is not None:
                phaseE_epi(pb_)
            if b is None:
                break
            if nb_ is not None:
                phase_hist(nb_)
            interleave(phaseB(b), phaseA1p1(nb_) if nb_ is not None else iter(()))
            if i + 2 < nblk:
                load_x(blocks[i + 2], "act")
            if nb_ is not None:
                run(phaseA1p2(nb_))
                phase_tail(nb_, "b")

    driver()

    out_res = [r for r in P.lastw if isinstance(r, tuple) and r and r[0] == "out"]
    P.op("sp", lambda e: e.nop(), reads=out_res)

    P.finalize()
    dma_slots = list(P.dma_cnt.keys())
    esem = {e: es.enter_context(nc.semaphore(f"sem_{e}")) for e in Prog.ENGS}
    dsem = {s: es.enter_context(nc.semaphore(f"dsem_{i}")) for i, s in enumerate(dma_slots)}
    with nc.Block() as block:
        @block.tensor
        def _(e):
            P.emit_engine("pe", e, esem, dsem)

        @block.scalar
        def _(e):
            P.emit_engine("act", e, esem, dsem)

        @block.vector
        def _(e):
            P.emit_engine("dve", e, esem, dsem)

        @block.gpsimd
        def _(e):
            P.emit_engine("pool", e, esem, dsem)

        @block.sync
        def _(e):
            P.emit_engine("sp", e, esem, dsem)
    es.close()
    return nc


def make_in_maps(n_cores, inputs, NP, NS):
    f = lambda a: np.ascontiguousarray(np.asarray(a, dtype=np.float32))
    ident = np.eye(128, dtype=np.float32)
    maps = []
    for c in range(n_cores):
        m = {
            "x_p": f(inputs["x_prompt"][c * NP:(c + 1) * NP]),
            "x_s": f(inputs["x_sample"][c * NS:(c + 1) * NS]),
            "cache_a": f(inputs["cache_conv_a"][0, c * NS:(c + 1) * NS]),
            "cache_b": f(inputs["cache_conv_b"][0, c * NS:(c + 1) * NS]),
            "g_pre1": f(inputs["norm_mix_pre"]),
            "g_post1": f(inputs["norm_mix_post"]),
            "g_pre2": f(inputs["norm_ffn_pre"]),
            "g_post2": f(inputs["norm_ffn_post"]),
            "w_in": f(inputs["w_in"][0]),
            "w_out": f(inputs["w_out"][0]),
            "w_gu": f(inputs["w_gate_up"][0]),
            "w_dn": f(inputs["w_down"][0]),
            "conv_a_w": f(inputs["conv_a_w"][0]),
            "conv_a_b": f(inputs["conv_a_b"]),
            "ln_g": f(inputs["conv_a_ln_g"]),
            "ln_b": f(inputs["conv_a_ln_b"]),
            "conv_b_w": f(inputs["conv_b_w"][0]),
            "ident": ident,
        }
        maps.append(m)
    return maps


def gather(results, n_cores):
    cat = lambda k: np.concatenate([np.asarray(results[c][k], dtype=np.float32) for c in range(n_cores)], axis=0)
    return (cat("y_p"), cat("y_s"), cat("na_p")[None], cat("nb_p")[None], cat("na_s")[None], cat("nb_s")[None])


def kernel(**inputs):
    B, SEQ = inputs["x_prompt"].shape[0], inputs["x_prompt"].shape[1]
    BS, LS = inputs["x_sample"].shape[0], inputs["x_sample"].shape[1]
    NP, NS = B // N_CORES, BS // N_CORES
    nc = build_program(NP, SEQ, NS, LS)
    in_maps = make_in_maps(N_CORES, inputs, NP, NS)
    res = run_bass_kernel_spmd(nc, in_maps, core_ids=list(range(N_CORES)))
    return gather(res.results, N_CORES)
```

```python
import numpy as np
import concourse.bass as bass
import concourse.mybir as mybir
from concourse.bass_utils import run_bass_kernel_spmd

F32 = mybir.dt.float32
BF16 = mybir.dt.bfloat16
AF = mybir.ActivationFunctionType
ALU = mybir.AluOpType

D = 1024
DA = 512
DB = 512
DIN = 2560
DFF = 2816
KA = 31
KB = 3
EPS = 1e-6
NDC = 8
NFC = 22
NPAR = 37
RING = 8
NTMP = 9
N_CORES = 8

E_ORDER = [0, 4, 1, 5, 2, 6, 3, 7, 12, 16, 8, 13, 17, 9, 14, 18, 10, 15, 19, 11]


class _Op:
    __slots__ = ("eng", "fn", "deps", "idx", "signal", "dma_slot", "dma_val", "waits", "sigval")


class Prog:
    ENGS = ("pe", "act", "dve", "pool", "sp")

    def __init__(self):
        self.ops = {e: [] for e in self.ENGS}
        self.lastw = {}
        self.readers = {}
        self.dma_cnt = {}

    def op(self, eng, fn, reads=(), writes=(), dma_slot=None):
        o = _Op()
        o.eng = eng
        o.fn = fn
        o.idx = len(self.ops[eng])
        o.signal = False
        deps = []
        for r in reads:
            w = self.lastw.get(r)
            if w is not None:
                deps.append(w)
        for r in writes:
            w = self.lastw.get(r)
            if w is not None:
                deps.append(w)
            deps.extend(self.readers.get(r, ()))
        o.deps = deps
        if dma_slot is not None:
            c = self.dma_cnt.get(dma_slot, 0) + 1
            self.dma_cnt[dma_slot] = c
            o.dma_slot = dma_slot
            o.dma_val = 16 * c
            tok = ("dma", dma_slot, 16 * c)
        else:
            o.dma_slot = None
            o.dma_val = 0
            tok = ("eng", eng, o.idx)
        for r in reads:
            self.readers.setdefault(r, []).append(tok)
        for r in writes:
            self.lastw[r] = tok
            self.readers[r] = []
        self.ops[eng].append(o)
        return o

    def finalize(self):
        for e in self.ENGS:
            seen = {}
            for o in self.ops[e]:
                need_eng = {}
                need_dma = {}
                for d in o.deps:
                    if d[0] == "eng":
                        _, de, di = d
                        if de == e:
                            if e in ("pe", "sp"):
                                continue
                        if seen.get(de, -1) >= di:
                            continue
                        if need_eng.get(de, -1) < di:
                            need_eng[de] = di
                    else:
                        _, slot, val = d
                        if seen.get(("dma", slot), 0) >= val:
                            continue
                        if need_dma.get(slot, 0) < val:
                            need_dma[slot] = val
                waits = []
                for de, di in need_eng.items():
                    seen[de] = di
                    waits.append(("eng", de, di))
                    self.ops[de][di].signal = True
                for slot, val in need_dma.items():
                    seen[("dma", slot)] = val
                    waits.append(("dma", slot, val))
                o.waits = waits
        for e in self.ENGS:
            c = 0
            for o in self.ops[e]:
                if o.signal:
                    c += 1
                o.sigval = c

    def emit_engine(self, e, eng, esem, dsem):
        for o in self.ops[e]:
            for w in o.waits:
                if w[0] == "eng":
                    eng.wait_ge(esem[w[1]], self.ops[w[1]][w[2]].sigval)
                else:
                    eng.wait_ge(dsem[w[1]], w[2])
            ins = o.fn(eng)
            if o.dma_slot is not None:
                ins.then_inc(dsem[o.dma_slot], 16)
            elif o.signal:
                ins.then_inc(esem[e], 1)


class _Psum:
    def __init__(self):
        self.held = [False] * 8
        self.stamp = [0] * 8
        self.clock = 0

    def alloc1(self):
        free = [b for b in range(8) if not self.held[b]]
        assert free, "out of PSUM banks"
        b = min(free, key=lambda k: self.stamp[k])
        self.held[b] = True
        return b

    def alloc_pair(self):
        free = [b for b in range(0, 8, 2) if not self.held[b] and not self.held[b + 1]]
        assert free, "out of PSUM bank pairs"
        b = min(free, key=lambda k: max(self.stamp[k], self.stamp[k + 1]))
        self.held[b] = self.held[b + 1] = True
        return b

    def release(self, *banks):
        for b in banks:
            assert self.held[b]
            self.held[b] = False
            self.clock += 1
            self.stamp[b] = self.clock


def build_program(NP, SEQ, NS, LS=64):
    assert SEQ % 512 == 0 and (NS * LS) % 128 == 0 and NS * (KA - 1) <= 128
    nc = bass.Bass("TRN2", target_bir_lowering=False)
    P = Prog()
    PS = _Psum()

    def din(name, shape, dt=F32):
        return nc.dram_tensor(name, list(shape), dt, kind="ExternalInput").ap()

    def dout(name, shape, dt=F32):
        return nc.dram_tensor(name, list(shape), dt, kind="ExternalOutput").ap()

    x_p = din("x_p", [NP, SEQ, D])
    x_s = din("x_s", [NS, LS, D])
    cache_a = din("cache_a", [NS, KA - 1, DA])
    cache_b = din("cache_b", [NS, KB - 1, DB])
    g_pre1 = din("g_pre1", [1, D])
    g_post1 = din("g_post1", [1, D])
    g_pre2 = din("g_pre2", [1, D])
    g_post2 = din("g_post2", [1, D])
    w_in = din("w_in", [D, DIN])
    w_out = din("w_out", [D, D])
    w_gu = din("w_gu", [D, 2 * DFF])
    w_dn = din("w_dn", [DFF, D])
    conv_a_w = din("conv_a_w", [KA, DA])
    conv_a_b = din("conv_a_b", [1, DA])
    ln_g = din("ln_g", [1, DA])
    ln_b = din("ln_b", [1, DA])
    conv_b_w = din("conv_b_w", [KB, DB])
    ident_in = din("ident", [128, 128])

    y_p = dout("y_p", [NP, SEQ, D])
    y_s = dout("y_s", [NS, LS, D])
    na_p = dout("na_p", [NP, KA - 1, DA])
    nb_p = dout("nb_p", [NP, KB - 1, DB])
    na_s = dout("na_s", [NS, KA - 1, DA])
    nb_s = dout("nb_s", [NS, KB - 1, DB])

    ws_in = nc.dram_tensor("ws_in", [10, 128, 2048], BF16, kind="Internal").ap()
    ws_out = nc.dram_tensor("ws_out", [4, 128, 2048], BF16, kind="Internal").ap()
    ws_gu = nc.dram_tensor("ws_gu", [NFC, 128, 2048], BF16, kind="Internal").ap()
    ws_dn = nc.dram_tensor("ws_dn", [12, 128, 2048], BF16, kind="Internal").ap()

    from contextlib import ExitStack
    es = ExitStack()

    def sb(name, shape, dt=F32):
        return es.enter_context(nc.sbuf_tensor(name, list(shape), dt))

    ident_f = sb("ident_f", [128, 128])
    ident_b = sb("ident_b", [128, 128], BF16)
    ones_f = sb("ones_f", [128, 128])
    neghalf = sb("neghalf", [128, 1])
    eps_t = sb("eps_t", [128, 1])
    gb_t = [sb(f"gb{i}", [128, D]) for i in range(4)]
    pT = sb("pT", [128, 4, NPAR])
    x_t = [sb(f"x{i}", [128, 4, D]) for i in range(2)]
    h_t = sb("h", [128, 4, D])
    xn_t = [sb(f"xn{i}", [128, D], BF16) for i in range(4)]
    hn_t = [sb(f"hn{i}", [128, D], BF16) for i in range(4)]
    xnT = sb("xnT", [128, NDC, 512], BF16)
    hnT = sb("hnT", [128, NDC, 512], BF16)
    a_ext = sb("a_ext", [128, 4, 544])
    cv_ext = sb("cv_ext", [128, 4, 516])
    ac_t = sb("ac", [128, 4, 512])
    tmp_t = [sb(f"tmp{i}", [128, 512]) for i in range(NTMP)]
    mixT_lo = sb("mixT_lo", [128, 4, 512], BF16)
    mixT_hi = [sb(f"mixT_hi{i}", [128, 4, 512], BF16) for i in range(2)]

    def mix(b, cc):
        return mixT_lo[:, cc] if cc < 4 else mixT_hi[b.buf][:, cc - 4]

    def mix_r(b, cc):
        return ("mixT", cc) if cc < 4 else ("mixT", cc, b.buf)
    actT = sb("actT", [128, NFC, 512], BF16)
    ring = [sb(f"ring{i}", [128, 2048], BF16) for i in range(RING)]
    st = sb("st", [128, 64])
    st4 = sb("st4", [128, 64])
    ps = es.enter_context(nc.psum_tensor("ps", [128, 4096], F32))

    def bank(b, n=512):
        return ps[:, b * 512:b * 512 + n]

    cnt = {"tmp": 0, "st": 0, "ring": 0, "st4": 0}

    tmp_pinned = set()

    def new_tmp(pin=False):
        while True:
            i = cnt["tmp"] % NTMP
            cnt["tmp"] += 1
            if i not in tmp_pinned:
                break
        if pin:
            tmp_pinned.add(i)
        return tmp_t[i], ("tmp", i)

    def new_st():
        i = cnt["st"] % 64
        cnt["st"] += 1
        return st[:, i:i + 1], ("st", i)

    def new_st4():
        i = cnt["st4"] % 16
        cnt["st4"] += 1
        return st4[:, 4 * i:4 * i + 4], ("st4", i)

    ring_pinned = set()

    def wload(src_ap, src_res, cols=2048, pin=False):
        while True:
            i = cnt["ring"] % RING
            cnt["ring"] += 1
            if i not in ring_pinned:
                break
        if pin:
            ring_pinned.add(i)
        dst = ring[i]
        P.op("sp", lambda e, dst=dst, src_ap=src_ap, cols=cols: e.dma_start(out=dst[:, 0:cols], in_=src_ap[:, 0:cols]),
             reads=src_res, writes=[("ring", i)], dma_slot=("ring", i))
        return dst, ("ring", i)

    res_ws = {"in": [], "out": [], "gu": [], "dn": []}
    w_in_v = w_in.rearrange("(dc p) (j e) -> p j dc e", p=128, e=128)
    ws_in_v = ws_in.rearrange("i p (q dc e) -> i p q dc e", q=2, dc=NDC)
    w_out_v = w_out.rearrange("(cc p) d -> p cc d", p=128)
    ws_out_v = ws_out.rearrange("i p (cc d) -> i p cc d", cc=4)
    w_gu_v = w_gu.rearrange("(dc p) (gu j e) -> p gu j dc e", p=128, gu=2, e=128)
    ws_gu_v = ws_gu.rearrange("j p (gu dc e) -> j p gu dc e", gu=2, dc=NDC)
    w_dn_v = w_dn.rearrange("(fc p) d -> p fc d", p=128)
    ws_dn_v = ws_dn.rearrange("i p (fc d) -> i p fc d", fc=4)

    def prep_in_out():
        for k, j in enumerate(E_ORDER):
            i, q = divmod(k, 2)
            r = ("ws", "in", k)
            res_ws["in"].append(r)
            P.op("pool", lambda e, i=i, q=q, j=j: e.dma_start(out=ws_in_v[i, :, q, :, :], in_=w_in_v[:, j, :, :]),
                 writes=[r], dma_slot="prep_in0" if k < 8 else "prep_in1")
        for half in range(2):
            for q in range(2):
                r = ("ws", "out", half * 2 + q)
                res_ws["out"].append(r)
                P.op("pool", lambda e, half=half, q=q: e.dma_start(
                    out=ws_out_v[half * 2 + q], in_=w_out_v[:, 4 * q:4 * q + 4, half * 512:(half + 1) * 512]),
                    writes=[r], dma_slot="prep_out")

    def prep_gu_dn():
        for j in range(NFC):
            for gu in range(2):
                r = ("ws", "gu", j * 2 + gu)
                res_ws["gu"].append(r)
                P.op("pool", lambda e, j=j, gu=gu: e.dma_start(out=ws_gu_v[j, :, gu, :, :], in_=w_gu_v[:, gu, j, :, :]),
                     writes=[r], dma_slot="prep_gu")
        for half in range(2):
            for q in range(6):
                nfc = 4 if q < 5 else 2
                r = ("ws", "dn", half * 6 + q)
                res_ws["dn"].append(r)
                P.op("pool", lambda e, half=half, q=q, nfc=nfc: e.dma_start(
                    out=ws_dn_v[half * 6 + q, :, 0:nfc, :], in_=w_dn_v[:, 4 * q:4 * q + nfc, half * 512:(half + 1) * 512]),
                    writes=[r], dma_slot="prep_dn")


    P.op("sp", lambda e: e.dma_start(out=ident_f[:], in_=ident_in[:, :]), writes=["ident_f"], dma_slot="c0")
    for i, g in enumerate((g_pre1, g_post1, g_pre2, g_post2)):
        P.op("sp", lambda e, i=i, g=g: e.dma_start(out=gb_t[i][:], in_=g[0, :].partition_broadcast(128)),
             writes=[("gb", i)], dma_slot=("c1", i))
    pstage, pstage_r = new_tmp()
    P.op("sp", lambda e: e.dma_start(out=pstage[0:KA, :], in_=conv_a_w[:, :]), writes=[pstage_r], dma_slot="c2")
    P.op("sp", lambda e: e.dma_start(out=pstage[31:32, :], in_=conv_a_b[:, :]), writes=["ps1"], dma_slot="c3")
    P.op("sp", lambda e: e.dma_start(out=pstage[32:33, :], in_=ln_g[:, :]), writes=["ps2"], dma_slot="c4")
    P.op("sp", lambda e: e.dma_start(out=pstage[33:34, :], in_=ln_b[:, :]), writes=["ps3"], dma_slot="c5")
    P.op("sp", lambda e: e.dma_start(out=pstage[34:37, :], in_=conv_b_w[:, :]), writes=["ps4"], dma_slot="c6")
    P.op("dve", lambda e: e.tensor_copy(out=ident_b[:], in_=ident_f[:]), reads=["ident_f"], writes=["ident_b"])
    P.op("dve", lambda e: e.memset(ones_f[:], 1.0 / DA), writes=["ones_f"])
    P.op("dve", lambda e: e.memset(neghalf[:], -0.5), writes=["neghalf"])
    P.op("dve", lambda e: e.memset(eps_t[:], EPS), writes=["eps_t"])

    tb = PS.alloc1()

    def _ptr(e):
        ins = None
        for c in range(4):
            ins = e.transpose(out=bank(tb)[:, c * 64:c * 64 + NPAR], in_=pstage[0:NPAR, c * 128:(c + 1) * 128],
                              identity=ident_f[0:NPAR, 0:NPAR])
        return ins
    P.op("pe", _ptr, reads=[pstage_r, "ps1", "ps2", "ps3", "ps4", "ident_f"], writes=[("ps", tb)])
    P.op("dve", lambda e: e.tensor_copy(out=pT[:], in_=bank(tb).rearrange("p (c k) -> p c k", k=64)[:, 0:4, 0:NPAR]),
         reads=[("ps", tb)], writes=["pT"])
    P.op("dve", lambda e: e.tensor_scalar(out=pT[:, :, 0:KA], in0=pT[:, :, 0:KA], scalar1=0.5, scalar2=None, op0=ALU.mult),
         reads=["pT"], writes=["pT"])
    PS.release(tb)

    def rstd_ops(ss, ss_r, scale):
        t, t_r = new_st()
        r, r_r = new_st()
        P.op("pool", lambda e: e.tensor_scalar(out=t, in0=ss, scalar1=scale, scalar2=EPS, op0=ALU.mult, op1=ALU.add),
             reads=[ss_r], writes=[t_r])
        P.op("pool", lambda e: e.tensor_tensor(out=r, in0=t, in1=neghalf[:, 0:1], op=ALU.pow),
             reads=[t_r, "neghalf"], writes=[r_r])
        return r, r_r

    def rstd4_ops(ss4, ss4_r, n, scale, add=None):
        t, t_r = new_st4()
        r, r_r = new_st4()
        src, src_rs = ss4, [ss4_r]
        if add is not None:
            a4, a4_r = add
            u, u_r = new_st4()
            P.op("pool", lambda e: e.tensor_tensor(out=u[:, 0:n], in0=ss4[:, 0:n], in1=a4[:, 0:n], op=ALU.add),
                 reads=[ss4_r, a4_r], writes=[u_r])
            src, src_rs = u, [u_r]
        P.op("pool", lambda e: e.tensor_scalar(out=t[:, 0:n], in0=src[:, 0:n], scalar1=scale, scalar2=EPS, op0=ALU.mult, op1=ALU.add),
             reads=src_rs, writes=[t_r])
        P.op("pool", lambda e: e.tensor_tensor(out=r[:, 0:n], in0=t[:, 0:n], in1=neghalf[:, 0:1].broadcast_to([128, n]), op=ALU.pow),
             reads=[t_r, "neghalf"], writes=[r_r])
        return r, r_r

    def norm_elem(src, src_r, gi, xn, xn_r):
        ss, ss_r = new_st()
        P.op("act", lambda e: e.activation(out=xn[:], in_=src, func=AF.Square, accum_out=ss),
             reads=[src_r], writes=[ss_r, xn_r])
        r, r_r = rstd_ops(ss, ss_r, 1.0 / D)
        P.op("dve", lambda e: e.scalar_tensor_tensor(out=xn[:], in0=src, scalar=r, in1=gb_t[gi][:],
                                                     op0=ALU.mult, op1=ALU.mult),
             reads=[src_r, r_r, ("gb", gi)], writes=[xn_r])

    def norm_tr(xn, xn_r, dstT, dst_name, tt):
        b = PS.alloc1()
        pb = bank(b).bitcast(BF16)

        def _tr(e):
            ins = None
            for c in range(NDC):
                ins = e.transpose(out=pb[:, c * 128:(c + 1) * 128], in_=xn[:, c * 128:(c + 1) * 128], identity=ident_b[:])
            return ins
        P.op("pe", _tr, reads=[xn_r, "ident_b"], writes=[("ps", b)])
        P.op("act", lambda e: e.activation(out=dstT[:, :, tt * 128:(tt + 1) * 128],
                                           in_=pb.rearrange("p (c t) -> p c t", t=128), func=AF.Copy),
             reads=[("ps", b)], writes=[(dst_name, tt)])
        PS.release(b)

    class Blk:
        pass

    def make_blocks():
        blks = []
        for s in range(NP):
            nb = SEQ // 512
            for k in range(nb):
                b = Blk()
                b.kind = "p"
                b.nseg, b.L, b.TT = 1, 512, 4
                b.first, b.last = (k == 0), (k == nb - 1)
                b.seqs = [s]
                b.xrows = [x_p[s, k * 512 + t * 128:k * 512 + (t + 1) * 128, :] for t in range(4)]
                b.yrows = [y_p[s, k * 512 + t * 128:k * 512 + (t + 1) * 128, :] for t in range(4)]
                blks.append(b)
        xs = x_s.rearrange("s t d -> (s t) d")
        ys = y_s.rearrange("s t d -> (s t) d")
        spb = 512 // LS
        for k0 in range(0, NS, spb):
            b = Blk()
            b.kind = "s"
            b.nseg = min(spb, NS - k0)
            b.L = LS
            b.TT = b.nseg * LS // 128
            b.first, b.last = True, True
            b.seqs = list(range(k0, k0 + b.nseg))
            b.xrows = [xs[k0 * LS + t * 128:k0 * LS + (t + 1) * 128, :] for t in range(b.TT)]
            b.yrows = [ys[k0 * LS + t * 128:k0 * LS + (t + 1) * 128, :] for t in range(b.TT)]
            blks.append(b)
        for i, b in enumerate(blks):
            b.bi = i
            b.buf = i % 2
        return blks

    blocks = make_blocks()

    def aview(buf, c, b, lo, n, hist):
        w = hist + b.L
        v = buf[:, c, 0:b.nseg * w].rearrange("p (s l) -> p s l", s=b.nseg)
        return v[:, :, lo:lo + n]

    def nview(ap2d, b):
        return ap2d.rearrange("p (s l) -> p s l", s=b.nseg)

    def load_x(b, q="sp"):
        for tt in range(b.TT):
            P.op(q, lambda e, tt=tt: e.dma_start(out=x_t[b.buf][:, tt, :], in_=b.xrows[tt]),
                 writes=[("x", b.buf, tt)], dma_slot=("x", q, b.buf, tt))

    def phase_hist(b):
        if not b.first:
            return
        if b.kind == "p":
            for c in range(4):
                P.op("pool", lambda e, c=c: e.memset(aview(a_ext, c, b, 0, KA - 1, KA - 1), 0.0), writes=[("ahist", c)])
                P.op("pool", lambda e, c=c: e.memset(aview(cv_ext, c, b, 0, KB - 1, KB - 1), 0.0), writes=[("bhist", c)])
            return
        s0, ns = b.seqs[0], b.nseg
        for (cache, K1, buf, hname) in ((cache_a, KA - 1, a_ext, "ahist"), (cache_b, KB - 1, cv_ext, "bhist")):
            stg, stg_r = new_tmp()
            rows = ns * K1
            P.op("sp", lambda e, stg=stg, cache=cache, rows=rows: e.dma_start(
                out=stg[0:rows, :], in_=cache[s0:s0 + ns].rearrange("s t c -> (s t) c")),
                writes=[stg_r], dma_slot=("cst", hname))
            tb1 = PS.alloc1()

            def _tr(e, stg=stg, rows=rows, tb1=tb1):
                ins = None
                for c in range(4):
                    ins = e.transpose(out=bank(tb1)[:, c * 128:c * 128 + rows], in_=stg[0:rows, c * 128:(c + 1) * 128],
                                      identity=ident_f[0:rows, 0:rows])
                return ins
            P.op("pe", _tr, reads=[stg_r, "ident_f"], writes=[("ps", tb1)])
            gain = 2.0 if hname == "ahist" else 1.0
            for c in range(4):
                P.op("dve", lambda e, c=c, tb1=tb1, rows=rows, K1=K1, buf=buf, gain=gain: e.tensor_scalar(
                    out=aview(buf, c, b, 0, K1, K1),
                    in0=bank(tb1)[:, c * 128:c * 128 + rows].rearrange("p (s k) -> p s k", k=K1),
                    scalar1=gain, scalar2=None, op0=ALU.mult),
                    reads=[("ps", tb1)], writes=[(hname, c)])
            PS.release(tb1)

    def phase0_elem(b):
        xs_ = []
        ss4, ss4_r = new_st4()
        for tt in range(b.TT):
            xn, xn_r = xn_t[tt], ("xn", tt)
            P.op("act", lambda e, tt=tt, xn=xn: e.activation(out=xn[:], in_=x_t[b.buf][:, tt, :], func=AF.Square,
                                                            accum_out=ss4[:, tt:tt + 1]),
                 reads=[("x", b.buf, tt)], writes=[ss4_r, xn_r])
            xs_.append((xn, xn_r))
        b.r4 = rstd4_ops(ss4, ss4_r, b.TT, 1.0 / D)
        b.xn = xs_

    def phase0_scale(b):
        r4, r4_r = b.r4
        for tt in range(b.TT):
            xn, xn_r = b.xn[tt]
            P.op("dve", lambda e, tt=tt, xn=xn: e.scalar_tensor_tensor(
                out=xn[:], in0=x_t[b.buf][:, tt, :], scalar=r4[:, tt:tt + 1], in1=gb_t[0][:], op0=ALU.mult, op1=ALU.mult),
                reads=[("x", b.buf, tt), r4_r, ("gb", 0)], writes=[xn_r])

    def phase0_tr(b):
        for tt in range(b.TT):
            norm_tr(b.xn[tt][0], b.xn[tt][1], xnT, "xnT", tt)

    def win_mm(b, slot, slot_r, q):
        N = b.TT * 128
        bk = PS.alloc1()
        wv = slot.rearrange("p (q dc e) -> p q dc e", q=2, dc=NDC)

        def _mm(e):
            ins = None
            for dc in range(NDC):
                ins = e.matmul(out=bank(bk, N), lhsT=wv[:, q, dc, :], rhs=xnT[:, dc, 0:N], start=(dc == 0), stop=(dc == NDC - 1))
            return ins
        P.op("pe", _mm, reads=[slot_r] + [("xnT", t) for t in range(b.TT)], writes=[("ps", bk)])
        return bk

    def _get_piece(b, i, pin=False):
        if not hasattr(b, "pieces"):
            b.pieces = {}
        if i not in b.pieces:
            b.pieces[i] = wload(ws_in[i], res_ws["in"][0:8] if i < 4 else res_ws["in"], pin=pin)

    def _get_chunk(b, k):
        i, q = divmod(k, 2)
        _get_piece(b, i)
        slot, slot_r = b.pieces[i]
        if q == 1:
            ring_pinned.discard(slot_r[1])
        return win_mm(b, slot, slot_r, q)

    def phaseA1p1(b):
        N = b.TT * 128
        get_chunk = lambda k: _get_chunk(b, k)
        for c in range(4):
            bv = get_chunk(2 * c)
            bg = get_chunk(2 * c + 1)
            sg, sg_r = new_tmp()
            P.op("act", lambda e, bg=bg, sg=sg: e.activation(out=sg[:, 0:N], in_=bank(bg, N), func=AF.Tanh, scale=0.5),
                 reads=[("ps", bg)], writes=[sg_r])
            P.op("dve", lambda e, bv=bv, sg=sg, c=c: e.scalar_tensor_tensor(
                out=aview(a_ext, c, b, KA - 1, b.L, KA - 1), in0=nview(sg[:, 0:N], b), scalar=1.0, in1=nview(bank(bv, N), b),
                op0=ALU.add, op1=ALU.mult),
                reads=[("ps", bv), sg_r], writes=[("abody", c)])
            PS.release(bv, bg)
            yield

    def phaseA1p2(b):
        N = b.TT * 128
        get_chunk = lambda k: _get_chunk(b, k)
        for c in range(4):
            bgc = get_chunk(8 + 3 * c)
            bvv = get_chunk(8 + 3 * c + 1)
            bgb = get_chunk(8 + 3 * c + 2)
            vs, vs_r = new_tmp()
            P.op("act", lambda e, bvv=bvv, vs=vs: e.activation(out=vs[:, 0:N], in_=bank(bvv, N), func=AF.Copy),
                 reads=[("ps", bvv)], writes=[vs_r])
            P.op("dve", lambda e, bgc=bgc, vs=vs, c=c: e.tensor_tensor(
                out=aview(cv_ext, c, b, KB - 1, b.L, KB - 1), in0=nview(bank(bgc, N), b), in1=nview(vs[:, 0:N], b), op=ALU.mult),
                reads=[("ps", bgc), vs_r], writes=[("bbody", c)])
            u, u_r = new_tmp()
            P.op("dve", lambda e, u=u, c=c: e.tensor_scalar(
                out=nview(u[:, 0:N], b), in0=aview(cv_ext, c, b, 0, b.L, KB - 1), scalar1=pT[:, c, 34:35], scalar2=None, op0=ALU.mult),
                reads=[("bbody", c), ("bhist", c), "pT"], writes=[u_r])
            for k in range(1, KB):
                P.op("dve", lambda e, u=u, c=c, k=k: e.scalar_tensor_tensor(
                    out=nview(u[:, 0:N], b), in0=aview(cv_ext, c, b, k, b.L, KB - 1), scalar=pT[:, c, 34 + k:35 + k],
                    in1=nview(u[:, 0:N], b), op0=ALU.mult, op1=ALU.add),
                    reads=[("bbody", c), ("bhist", c), "pT", u_r], writes=[u_r])
            P.op("dve", lambda e, u=u, bgb=bgb, c=c: e.tensor_tensor(
                out=mix(b, 4 + c)[:, 0:N], in0=bank(bgb, N), in1=u[:, 0:N], op=ALU.mult),
                reads=[("ps", bgb), u_r], writes=[mix_r(b, 4 + c)])
            PS.release(bgc, bvv, bgb)
            yield

    def phase_conv(b):
        N = b.TT * 128
        acv = [nview(ac_t[:, c, 0:N], b) for c in range(4)]
        for c in range(4):
            P.op("dve", lambda e, c=c: e.tensor_scalar(
                out=acv[c], in0=aview(a_ext, c, b, 0, b.L, KA - 1), scalar1=pT[:, c, 0:1], scalar2=pT[:, c, 31:32],
                op0=ALU.mult, op1=ALU.add),
                reads=[("abody", c), ("ahist", c), "pT"], writes=[("ac", c)])
        yield
        for k in range(1, KA):
            for c in range(4):
                P.op("dve", lambda e, c=c, k=k: e.scalar_tensor_tensor(
                    out=acv[c], in0=aview(a_ext, c, b, k, b.L, KA - 1), scalar=pT[:, c, k:k + 1], in1=acv[c],
                    op0=ALU.mult, op1=ALU.add),
                    reads=[("abody", c), ("ahist", c), "pT", ("ac", c)], writes=[("ac", c)])
            yield

    def phase_tail(b, which):
        buf, K1, body, hist = (a_ext, KA - 1, "abody", "ahist") if which == "a" else (cv_ext, KB - 1, "bbody", "bhist")
        if b.last:
            if which == "a":
                dst = na_p if b.kind == "p" else na_s
            else:
                dst = nb_p if b.kind == "p" else nb_s
            for si, s in enumerate(b.seqs):
                tb1 = PS.alloc1()

                def _t1(e, si=si, tb1=tb1):
                    ins = None
                    for c in range(4):
                        ins = e.transpose(out=bank(tb1)[0:K1, c * 128:(c + 1) * 128],
                                          in_=aview(buf, c, b, b.L, K1, K1)[:, si, :], identity=ident_f[:])
                    return ins
                P.op("pe", _t1, reads=[(body, c) for c in range(4)] + [(hist, c) for c in range(4)] + ["ident_f"],
                     writes=[("ps", tb1)])
                og, og_r = new_tmp()
                osc = 0.5 if which == "a" else 1.0
                P.op("act", lambda e, tb1=tb1, og=og, osc=osc: e.activation(out=og[0:K1, :], in_=bank(tb1)[0:K1, :], func=AF.Copy, scale=osc),
                     reads=[("ps", tb1)], writes=[og_r])
                PS.release(tb1)
                P.op("sp", lambda e, s=s, og=og: e.dma_start(out=dst[s, :, :], in_=og[0:K1, :]),
                     reads=[og_r], writes=[("out", which, b.kind, s)], dma_slot=("o" + which, si))
        else:
            for c in range(4):
                P.op("pool", lambda e, c=c: e.tensor_copy(out=buf[:, c, 0:K1], in_=buf[:, c, b.L:b.L + K1]),
                     reads=[(body, c), (hist, c)], writes=[(hist, c)])

    def phaseA2_stats(b):
        N = b.TT * 128
        bm = PS.alloc_pair()
        be = bm + 1
        b.bm, b.be = bm, be
        sqs = []
        for c in range(4):
            sq, sq_r = new_tmp()
            P.op("act", lambda e, c=c, sq=sq: e.activation(out=sq[:, 0:N], in_=ac_t[:, c, 0:N], func=AF.Square),
                 reads=[("ac", c)], writes=[sq_r])
            sqs.append((sq, sq_r))

        def _m1(e):
            ins = None
            for c in range(4):
                ins = e.matmul(out=bank(bm, N), lhsT=ones_f[:], rhs=ac_t[:, c, 0:N], start=(c == 0), stop=(c == 3))
            return ins
        P.op("pe", _m1, reads=[("ac", c) for c in range(4)] + ["ones_f"], writes=[("ps", bm)])

        def _m2(e):
            ins = None
            for c in range(4):
                ins = e.matmul(out=bank(be, N), lhsT=ones_f[:], rhs=sqs[c][0][:, 0:N], start=(c == 0), stop=(c == 3))
            return ins
        P.op("pe", _m2, reads=[s_[1] for s_ in sqs] + ["ones_f"], writes=[("ps", be)])

    def phaseA2_norm(b):
        N = b.TT * 128
        bm, be = b.bm, b.be
        msq, msq_r = new_tmp()
        P.op("act", lambda e: e.activation(out=msq[:, 0:N], in_=bank(bm, N), func=AF.Square), reads=[("ps", bm)], writes=[msq_r])
        var, var_r = new_tmp()
        P.op("dve", lambda e: e.tensor_tensor(out=var[:, 0:N], in0=bank(be, N), in1=msq[:, 0:N], op=ALU.subtract),
             reads=[("ps", be), msq_r], writes=[var_r])
        P.op("act", lambda e: e.activation(out=var[:, 0:N], in_=var[:, 0:N], func=AF.Ln, bias=eps_t[:, 0:1]),
             reads=[var_r, "eps_t"], writes=[var_r])
        rs, rs_r = new_tmp(pin=True)
        P.op("act", lambda e: e.activation(out=rs[:, 0:N], in_=var[:, 0:N], func=AF.Exp, scale=-0.5), reads=[var_r], writes=[rs_r])
        PS.release(be)
        for c in range(4):
            z, z_r = new_tmp()
            P.op("dve", lambda e, c=c, z=z: e.tensor_tensor(out=z[:, 0:N], in0=ac_t[:, c, 0:N], in1=bank(bm, N), op=ALU.subtract),
                 reads=[("ac", c), ("ps", bm)], writes=[z_r])
            P.op("dve", lambda e, z=z: e.tensor_tensor(out=z[:, 0:N], in0=z[:, 0:N], in1=rs[:, 0:N], op=ALU.mult),
                 reads=[z_r, rs_r], writes=[z_r])
            P.op("act", lambda e, c=c, z=z: e.activation(out=mix(b, c)[:, 0:N], in_=z[:, 0:N], func=AF.Silu,
                                                        scale=pT[:, c, 32:33], bias=pT[:, c, 33:34]),
                 reads=[z_r, "pT"], writes=[("mixT", c)])
            if c == 3:
                PS.release(bm)
                tmp_pinned.discard(rs_r[1])
            yield

    def phaseB(b):
        slots = b.wout_slots
        b.hn = [(hn_t[tt], ("hn", tt)) for tt in range(b.TT)]
        pend = {}

        def stage1(tt):
            pb = PS.alloc_pair()

            def _mm(e, tt=tt, pb=pb):
                ins = None
                for half in range(2):
                    for cc in range(NDC):
                        sl = slots[half * 2 + cc // 4][0].rearrange("p (cc d) -> p cc d", cc=4)
                        ins = e.matmul(out=bank(pb + half), lhsT=mix(b, cc)[:, tt * 128:(tt + 1) * 128], rhs=sl[:, cc % 4, :],
                                       start=(cc == 0), stop=(cc == NDC - 1))
                return ins
            P.op("pe", _mm, reads=[s_[1] for s_ in slots] + [mix_r(b, cc) for cc in range(NDC)],
                 writes=[("ps", pb), ("ps", pb + 1)])
            ss, ss_r = new_st()
            P.op("act", lambda e, pb=pb, ss=ss, tt=tt: e.activation(out=hn_t[tt][:], in_=bank(pb, 1024), func=AF.Square, accum_out=ss),
                 reads=[("ps", pb), ("ps", pb + 1)], writes=[ss_r, ("hn", tt)])
            r, r_r = rstd_ops(ss, ss_r, 1.0 / D)
            P.op("dve", lambda e, tt=tt, pb=pb, r=r: e.scalar_tensor_tensor(
                out=h_t[:, tt, :], in0=bank(pb, 1024), scalar=r, in1=gb_t[1][:], op0=ALU.mult, op1=ALU.mult),
                reads=[("ps", pb), ("ps", pb + 1), r_r, ("gb", 1)], writes=[("h", tt)])
            PS.release(pb, pb + 1)
            P.op("dve", lambda e, tt=tt: e.tensor_tensor(out=h_t[:, tt, :], in0=h_t[:, tt, :], in1=x_t[b.buf][:, tt, :], op=ALU.add),
                 reads=[("h", tt), ("x", b.buf, tt)], writes=[("h", tt)])
            ss2, ss2_r = new_st()
            P.op("act", lambda e, tt=tt, ss2=ss2: e.activation(out=hn_t[tt][:], in_=h_t[:, tt, :], func=AF.Square, accum_out=ss2),
                 reads=[("h", tt)], writes=[ss2_r, ("hn", tt)])
            pend[tt] = rstd_ops(ss2, ss2_r, 1.0 / D)

        def stage2(tt):
            r2, r2_r = pend[tt]
            P.op("dve", lambda e, tt=tt, r2=r2: e.scalar_tensor_tensor(
                out=hn_t[tt][:], in0=h_t[:, tt, :], scalar=r2, in1=gb_t[2][:], op0=ALU.mult, op1=ALU.mult),
                reads=[("h", tt), r2_r, ("gb", 2)], writes=[("hn", tt)])

        for tt in range(b.TT):
            stage1(tt)
            if tt == b.TT - 1:
                for s_ in slots:
                    ring_pinned.discard(s_[1][1])
            if tt > 0:
                stage2(tt - 1)
            yield
        stage2(b.TT - 1)
        yield

    def preload_wout(b):
        b.wout_slots = [wload(ws_out[i], res_ws["out"], pin=True) for i in range(4)]

    def phaseC(b):
        for tt in range(b.TT):
            norm_tr(b.hn[tt][0], b.hn[tt][1], hnT, "hnT", tt)

    def phaseD(b):
        N = b.TT * 128
        for j in range(NFC):
            slot, slot_r = wload(ws_gu[j], res_ws["gu"])
            wv = slot.rearrange("p (gu dc e) -> p gu dc e", gu=2, dc=NDC)
            pb = PS.alloc_pair()

            def _mm(e, wv=wv, pb=pb):
                ins = None
                for gu in range(2):
                    for dc in range(NDC):
                        ins = e.matmul(out=bank(pb + gu, N), lhsT=wv[:, gu, dc, :], rhs=hnT[:, dc, 0:N],
                                       start=(dc == 0), stop=(dc == NDC - 1))
                return ins
            P.op("pe", _mm, reads=[slot_r] + [("hnT", t) for t in range(b.TT)], writes=[("ps", pb), ("ps", pb + 1)])
            sg, sg_r = new_tmp()
            P.op("act", lambda e, pb=pb, sg=sg: e.activation(out=sg[:, 0:N], in_=bank(pb, N), func=AF.Silu),
                 reads=[("ps", pb)], writes=[sg_r])
            P.op("dve", lambda e, pb=pb, sg=sg, j=j: e.tensor_tensor(out=actT[:, j, 0:N], in0=bank(pb + 1, N), in1=sg[:, 0:N], op=ALU.mult),
                 reads=[("ps", pb + 1), sg_r], writes=[("actT", j)])
            PS.release(pb, pb + 1)
            yield

    def phaseE_half(b, half):
        banks = []
        for _ in range((b.TT + 1) // 2):
            p_ = PS.alloc_pair()
            banks += [p_, p_ + 1]
        for extra in banks[b.TT:]:
            PS.release(extra)
        banks = banks[:b.TT]
        for q in range(6):
            nfc = 4 if q < 5 else 2
            slot, slot_r = wload(ws_dn[half * 6 + q], res_ws["dn"], cols=nfc * 512)
            sl = slot.rearrange("p (fc d) -> p fc d", fc=4)

            def _mm(e, q=q, nfc=nfc, sl=sl):
                ins = None
                for f in range(nfc):
                    fc = 4 * q + f
                    for tt in range(b.TT):
                        ins = e.matmul(out=bank(banks[tt]), lhsT=actT[:, fc, tt * 128:(tt + 1) * 128], rhs=sl[:, f, :],
                                       start=(fc == 0), stop=(fc == NFC - 1))
                return ins
            P.op("pe", _mm, reads=[slot_r] + [("actT", 4 * q + f) for f in range(nfc)],
                 writes=[("ps", bk) for bk in banks])
            yield
        if half == 0:
            ss0, ss0_r = new_st4()
            for tt in range(b.TT):
                jk, jk_r = new_tmp()
                P.op("act", lambda e, tt=tt, jk=jk: e.activation(out=jk[:], in_=bank(banks[tt]), func=AF.Square,
                                                                accum_out=ss0[:, tt:tt + 1]),
                     reads=[("ps", banks[tt])], writes=[ss0_r, jk_r])
                P.op("act", lambda e, tt=tt: e.activation(out=cv_ext[:, tt, 2:514], in_=bank(banks[tt]), func=AF.Copy),
                     reads=[("ps", banks[tt]), ss0_r], writes=[("bbody", tt)])
            b.ss0 = (ss0, ss0_r)
            PS.release(*banks)
        else:
            b.ebanks = banks
        yield

    def phaseE_epi(b):
        banks = b.ebanks
        ss1, ss1_r = new_st4()
        for tt in range(b.TT):
            jk, jk_r = new_tmp()
            P.op("act", lambda e, tt=tt, jk=jk: e.activation(out=jk[:], in_=bank(banks[tt]), func=AF.Square,
                                                            accum_out=ss1[:, tt:tt + 1]),
                 reads=[("ps", banks[tt])], writes=[ss1_r, jk_r])
        r4, r4_r = rstd4_ops(ss1, ss1_r, b.TT, 1.0 / D, add=b.ss0)
        for tt in range(b.TT):
            r = r4[:, tt:tt + 1]
            for half in range(2):
                t, t_r = new_tmp()
                if half == 0:
                    P.op("dve", lambda e, tt=tt, t=t, r=r: e.scalar_tensor_tensor(
                        out=t[:], in0=cv_ext[:, tt, 2:514], scalar=r, in1=gb_t[3][:, 0:512], op0=ALU.mult, op1=ALU.mult),
                        reads=[("bbody", tt), r4_r, ("gb", 3)], writes=[t_r])
                else:
                    P.op("dve", lambda e, tt=tt, t=t, r=r: e.scalar_tensor_tensor(
                        out=t[:], in0=bank(banks[tt]), scalar=r, in1=gb_t[3][:, 512:1024], op0=ALU.mult, op1=ALU.mult),
                        reads=[("ps", banks[tt]), r4_r, ("gb", 3)], writes=[t_r])
                P.op("dve", lambda e, tt=tt, half=half, t=t: e.tensor_tensor(
                    out=h_t[:, tt, half * 512:(half + 1) * 512], in0=h_t[:, tt, half * 512:(half + 1) * 512], in1=t[:], op=ALU.add),
                    reads=[("h", tt), t_r], writes=[("h", tt)])
            PS.release(banks[tt])
            P.op("sp", lambda e, tt=tt: e.dma_start(out=b.yrows[tt], in_=h_t[:, tt, :]),
                 reads=[("h", tt)], writes=[("out", "y", b.bi, tt)], dma_slot=("y", tt))

    def run(*gens):
        for g in gens:
            if g is not None:
                for _ in g:
                    pass

    def chain(*gens):
        for g in gens:
            if g is not None:
                yield from g

    def interleave(ga, gb, on_b_done=None):
        da = db = False
        while not (da and db):
            if not da:
                try:
                    next(ga)
                except StopIteration:
                    da = True
            if not db:
                try:
                    next(gb)
                except StopIteration:
                    db = True
                    if on_b_done is not None:
                        on_b_done()

    nblk = len(blocks)

    def driver():
        load_x(blocks[0])
        if nblk > 1:
            load_x(blocks[1])
        b0 = blocks[0]
        phase0_elem(b0)
        phase0_scale(b0)
        phase_hist(b0)
        prep_in_out()
        phase0_tr(b0)
        run(phaseA1p1(b0))
        if nblk > 1:
            phase0_elem(blocks[1])
            phase0_scale(blocks[1])
        prep_gu_dn()

        def first_ffn():
            yield from phaseA1p2(b0)
            phase_tail(b0, "b")
            yield
        for i in range(nblk + 1):
            b = blocks[i] if i < nblk else None
            pb_ = blocks[i - 1] if i > 0 else None
            nb_ = blocks[i + 1] if i + 1 < nblk else None
            if pb_ is not None:
                phaseC(pb_)
            ffn = chain(phaseD(pb_), phaseE_half(pb_, 0), phaseE_half(pb_, 1)) if pb_ is not None else first_ffn()
            conv = phase_conv(b) if b is not None else iter(())
            if nb_ is not None and i > 0:
                for _ in range(2):
                    next(ffn, None)
                    next(conv, None)
                phase0_elem(nb_)
                for _ in range(21):
                    next(ffn, None)
                    next(conv, None)
                phase0_scale(nb_)
                for _ in range(3):
                    next(ffn, None)
                    next(conv, None)
                phase0_tr(nb_)

            def after_conv(b=b):
                if b is not None:
                    phase_tail(b, "a")
                    phaseA2_stats(b)
                    run(phaseA2_norm(b))
            interleave(ffn, conv, on_b_done=after_conv)
            if nb_ is not None and i == 0:
                phase0_tr(nb_)
            if b is not None:
                preload_wout(b)
            if nb_ is not None:
                _get_piece(nb_, 0, pin=True)
                _get_piece(nb_, 1, pin=True)
            if pb_ is not None:
                phaseE_epi(pb_)
            if b is None:
                break
            if nb_ is not None:
                phase_hist(nb_)
            interleave(phaseB(b), phaseA1p1(nb_) if nb_ is not None else iter(()))
            if i + 2 < nblk:
                load_x(blocks[i + 2], "act")
            if nb_ is not None:
                run(phaseA1p2(nb_))
                phase_tail(nb_, "b")

    driver()

    out_res = [r for r in P.lastw if isinstance(r, tuple) and r and r[0] == "out"]
    P.op("sp", lambda e: e.nop(), reads=out_res)

    P.finalize()
    dma_slots = list(P.dma_cnt.keys())
    esem = {e: es.enter_context(nc.semaphore(f"sem_{e}")) for e in Prog.ENGS}
    dsem = {s: es.enter_context(nc.semaphore(f"dsem_{i}")) for i, s in enumerate(dma_slots)}
    with nc.Block() as block:
        @block.tensor
        def _(e):
            P.emit_engine("pe", e, esem, dsem)

        @block.scalar
        def _(e):
            P.emit_engine("act", e, esem, dsem)

        @block.vector
        def _(e):
            P.emit_engine("dve", e, esem, dsem)

        @block.gpsimd
        def _(e):
            P.emit_engine("pool", e, esem, dsem)

        @block.sync
        def _(e):
            P.emit_engine("sp", e, esem, dsem)
    es.close()
    return nc


def make_in_maps(n_cores, inputs, NP, NS):
    f = lambda a: np.ascontiguousarray(np.asarray(a, dtype=np.float32))
    ident = np.eye(128, dtype=np.float32)
    maps = []
    for c in range(n_cores):
        m = {
            "x_p": f(inputs["x_prompt"][c * NP:(c + 1) * NP]),
            "x_s": f(inputs["x_sample"][c * NS:(c + 1) * NS]),
            "cache_a": f(inputs["cache_conv_a"][0, c * NS:(c + 1) * NS]),
            "cache_b": f(inputs["cache_conv_b"][0, c * NS:(c + 1) * NS]),
            "g_pre1": f(inputs["norm_mix_pre"]),
            "g_post1": f(inputs["norm_mix_post"]),
            "g_pre2": f(inputs["norm_ffn_pre"]),
            "g_post2": f(inputs["norm_ffn_post"]),
            "w_in": f(inputs["w_in"][0]),
            "w_out": f(inputs["w_out"][0]),
            "w_gu": f(inputs["w_gate_up"][0]),
            "w_dn": f(inputs["w_down"][0]),
            "conv_a_w": f(inputs["conv_a_w"][0]),
            "conv_a_b": f(inputs["conv_a_b"]),
            "ln_g": f(inputs["conv_a_ln_g"]),
            "ln_b": f(inputs["conv_a_ln_b"]),
            "conv_b_w": f(inputs["conv_b_w"][0]),
            "ident": ident,
        }
        maps.append(m)
    return maps


def gather(results, n_cores):
    cat = lambda k: np.concatenate([np.asarray(results[c][k], dtype=np.float32) for c in range(n_cores)], axis=0)
    return (cat("y_p"), cat("y_s"), cat("na_p")[None], cat("nb_p")[None], cat("na_s")[None], cat("nb_s")[None])


def kernel(**inputs):
    B, SEQ = inputs["x_prompt"].shape[0], inputs["x_prompt"].shape[1]
    BS, LS = inputs["x_sample"].shape[0], inputs["x_sample"].shape[1]
    NP, NS = B // N_CORES, BS // N_CORES
    nc = build_program(NP, SEQ, NS, LS)
    in_maps = make_in_maps(N_CORES, inputs, NP, NS)
    res = run_bass_kernel_spmd(nc, in_maps, core_ids=list(range(N_CORES)))
    return gather(res.results, N_CORES)
```
